# Optimizing a Trainium2 kernel written in Bass

```python
import math
import jax
import jax.numpy as jnp
from jax import lax
import numpy as np

D_MODEL = 1024
BATCH = 2
SEQ = 16384
DEPTH = 2

GRID_W = 64
CTX_LEN = 256
EPS = 1e-6
ROPE_BASE = 10000.0

NA_HEADS = 4
NA_HEAD_DIM = 64
NA_WIN_R = 8
NA_WIN_C = 16
NA_W = NA_HEADS * NA_HEAD_DIM

DIFF_HEADS = 4
DIFF_HEAD_DIM = 32
DIFF_QK_W = DIFF_HEADS * 2 * DIFF_HEAD_DIM
DIFF_V_W = DIFF_HEADS * 2 * DIFF_HEAD_DIM
DIFF_BLOCK = 128

POOL_WINDOWS = (2, 4, 8, 16)
POOL_GROUPS = len(POOL_WINDOWS)
POOL_GROUP_W = 64
POOL_W = POOL_GROUPS * POOL_GROUP_W

FNET_W = 256

N_BRANCH = 4
BRANCH_W = 256

OFF_NA_Q = 0
OFF_DF_Q = OFF_NA_Q + NA_W
OFF_POOL = OFF_DF_Q + DIFF_QK_W
OFF_FNET = OFF_POOL + POOL_W
OFF_GATE = OFF_FNET + FNET_W
OFF_KV = OFF_GATE + N_BRANCH * D_MODEL
KV_W = 2 * NA_W + DIFF_QK_W + DIFF_V_W
IN_COLS = OFF_KV + KV_W

N_EXPERTS = 16
EC_FACTOR = 2
EXPERT_FF = 1408

kernel_name = "hybrid_gated_branch_diffusion_block"


def _rmsnorm(x, g):
    x32 = x.astype(jnp.float32)
    y = x32 * lax.rsqrt(jnp.mean(x32 * x32, axis=-1, keepdims=True) + EPS)
    return (y * g.astype(jnp.float32)).astype(x.dtype)


def _modulate(h, shift, scale):
    return h * (1 + scale) + shift


def _rope_tables(pos, dim):
    half = dim // 2
    inv = ROPE_BASE ** (-jnp.arange(half, dtype=jnp.float32) / half)
    ang = pos.astype(jnp.float32)[:, None] * inv[None, :]
    return jnp.cos(ang), jnp.sin(ang)


def _rotate_half(x, cos, sin):
    half = x.shape[-1] // 2
    x1, x2 = x[..., :half], x[..., half:]
    return jnp.concatenate([x1 * cos - x2 * sin, x1 * sin + x2 * cos], axis=-1)


def _axial_rope(x, rope):
    cos_r, sin_r, cos_c, sin_c = rope
    a = x.shape[-1] // 2
    e = lambda z: z[:, None, None, :]
    x32 = x.astype(jnp.float32)
    y = jnp.concatenate([_rotate_half(x32[..., :a], e(cos_r), e(sin_r)),
                         _rotate_half(x32[..., a:], e(cos_c), e(sin_c))], axis=-1)
    return y.astype(x.dtype)


def _split_q(p, na_q_g, df_q_g):
    B, N, _ = p.shape
    na_q = _rmsnorm(p[..., OFF_NA_Q:OFF_DF_Q].reshape(B, N, NA_HEADS, NA_HEAD_DIM), na_q_g)
    df_q = _rmsnorm(p[..., OFF_DF_Q:OFF_POOL].reshape(B, N, DIFF_HEADS, 2, DIFF_HEAD_DIM), df_q_g)
    return na_q, df_q, p[..., OFF_POOL:OFF_FNET], p[..., OFF_FNET:OFF_GATE], p[..., OFF_GATE:OFF_KV]


def _split_kv(kv, na_k_g, df_k_g):
    B, N, _ = kv.shape
    na_k = _rmsnorm(kv[..., :NA_W].reshape(B, N, NA_HEADS, NA_HEAD_DIM), na_k_g)
    na_v = kv[..., NA_W:2 * NA_W].reshape(B, N, NA_HEADS, NA_HEAD_DIM)
    o = 2 * NA_W
    df_k = _rmsnorm(kv[..., o:o + DIFF_QK_W].reshape(B, N, DIFF_HEADS, 2, DIFF_HEAD_DIM), df_k_g)
    df_v = kv[..., o + DIFF_QK_W:].reshape(B, N, DIFF_HEADS, 2 * DIFF_HEAD_DIM)
    return na_k, na_v, df_k, df_v


def _dense_attn(q, k, v):
    B, Q, H, d = q.shape
    s = jnp.einsum('bqhd,bkhd->bhqk', q, k).astype(jnp.float32) * (d ** -0.5)
    p = jax.nn.softmax(s, axis=-1).astype(v.dtype)
    return jnp.einsum('bhqk,bkhd->bqhd', p, v).reshape(B, Q, H * d)


def _na_latent(q, k, v, kc, vc, rpb):
    B, T, H, dh = q.shape
    rows = T // GRID_W
    kr = min(NA_WIN_R, rows)
    scale = dh ** -0.5
    qg = q.reshape(B, rows, GRID_W, H, dh).transpose(1, 0, 3, 2, 4)
    kg = k.reshape(B, rows, GRID_W, H, dh).transpose(0, 3, 1, 2, 4)
    vg = v.reshape(B, rows, GRID_W, H, dh).transpose(0, 3, 1, 2, 4)
    kct = kc.transpose(0, 2, 1, 3)
    vct = vc.transpose(0, 2, 1, 3)
    col = jnp.arange(GRID_W)
    cs = jnp.clip(col - NA_WIN_C // 2, 0, GRID_W - NA_WIN_C)
    col_idx = cs[:, None] + jnp.arange(NA_WIN_C)[None, :]
    ci = col_idx - col[:, None] + (NA_WIN_C - 1)
    n_loc = kr * NA_WIN_C

    def row_block(args):
        r, qr = args
        rs = jnp.clip(r - kr // 2, 0, rows - kr)
        kw = lax.dynamic_slice_in_dim(kg, rs, kr, axis=2)[:, :, :, col_idx]
        vw = lax.dynamic_slice_in_dim(vg, rs, kr, axis=2)[:, :, :, col_idx]
        ri = rs + jnp.arange(kr) - r + (NA_WIN_R - 1)
        bias = rpb[:, ri[None, :, None], ci[:, None, :]]
        s_loc = (jnp.einsum('bhqd,bhiqjd->bhqij', qr, kw).astype(jnp.float32) * scale
                 + bias.astype(jnp.float32)[None]).reshape(B, H, GRID_W, n_loc)
        s_ctx = jnp.einsum('bhqd,bhkd->bhqk', qr, kct).astype(jnp.float32) * scale
        p = jax.nn.softmax(jnp.concatenate([s_loc, s_ctx], axis=-1), axis=-1).astype(qr.dtype)
        p_loc = p[..., :n_loc].reshape(B, H, GRID_W, kr, NA_WIN_C)
        return (jnp.einsum('bhqij,bhiqjd->bhqd', p_loc, vw)
                + jnp.einsum('bhqk,bhkd->bhqd', p[..., n_loc:], vct))

    o = lax.map(row_block, (jnp.arange(rows), qg))
    return o.transpose(1, 0, 3, 2, 4).reshape(B, T, H * dh)


def _diff_core(qh, kh, vh, lam):
    scale = qh.shape[-1] ** -0.5
    s = jnp.einsum('bhmqd,bhmkd->bhmqk', qh, kh).astype(jnp.float32) * scale
    p = jax.nn.softmax(s, axis=-1)
    a = (p[:, :, 0] - lam * p[:, :, 1]).astype(vh.dtype)
    return jnp.einsum('bhqk,bhkd->bhqd', a, vh)


def _diff_latent(q, k, v, kc, vc, lam):
    B, T, H, _, dq = q.shape
    nblk = T // DIFF_BLOCK
    kall = jnp.concatenate([k, kc], axis=1).transpose(0, 2, 3, 1, 4)
    vall = jnp.concatenate([v, vc], axis=1).transpose(0, 2, 1, 3)
    qb = q.reshape(B, nblk, DIFF_BLOCK, H, 2, dq).transpose(1, 0, 3, 4, 2, 5)
    o = lax.map(lambda qi: _diff_core(qi, kall, vall, lam), qb)
    return o.transpose(1, 2, 0, 3, 4).reshape(B, H, T, 2 * dq)


def _diff_post(o, sub_g, lam_init):
    B, H, N, dv = o.shape
    o = _rmsnorm(o, sub_g) * (1.0 - lam_init)
    return o.transpose(0, 2, 1, 3).reshape(B, N, H * dv)


def _pool_mixer(u, pool_w, pool_scale):
    N = u.shape[1]
    u32 = u.astype(jnp.float32)
    csum = jnp.concatenate([jnp.zeros_like(u32[:, :1]), jnp.cumsum(u32, axis=1)], axis=1)
    t = jnp.arange(N)
    outs = []
    for gi, w in enumerate(POOL_WINDOWS):
        lo = jnp.clip(t - w // 2, 0, N)
        hi = jnp.clip(t + w // 2, 0, N)
        cnt = (hi - lo).astype(jnp.float32)[None, :, None]
        sl = slice(gi * POOL_GROUP_W, (gi + 1) * POOL_GROUP_W)
        cg = csum[..., sl]
        mean = (cg[:, hi] - cg[:, lo]) / cnt
        outs.append((mean - u32[..., sl]) @ pool_w[gi].astype(jnp.float32))
    y = jnp.concatenate(outs, axis=-1) * pool_scale.astype(jnp.float32)
    return y.astype(u.dtype)


def _fourier_mixer(u, w):
    f = jnp.fft.fft2(u.astype(jnp.float32), axes=(1, 2), norm='ortho').real
    return f.astype(u.dtype) @ w


def _merge(ys, gate_logits, w_br, w_o):
    acc = None
    for i, y in enumerate(ys):
        g = jax.nn.sigmoid(gate_logits[..., i * D_MODEL:(i + 1) * D_MODEL].astype(jnp.float32))
        term = g.astype(y.dtype) * (y @ w_br[i])
        acc = term if acc is None else acc + term
    return acc @ w_o


def _ec_ffn(h, w_router, w_gate, w_up, w_down):
    B, N, D = h.shape
    cap = max(1, EC_FACTOR * N // N_EXPERTS)
    aff = jax.nn.softmax((h @ w_router).astype(jnp.float32), axis=-1)
    g, idx = lax.top_k(aff.transpose(0, 2, 1), cap)
    bidx = jnp.arange(B)[:, None, None]
    xe = h[bidx, idx]
    hid = (jax.nn.silu(jnp.einsum('becd,edf->becf', xe, w_gate))
           * jnp.einsum('becd,edf->becf', xe, w_up))
    ye = jnp.einsum('becf,efd->becd', hid, w_down) * g[..., None].astype(h.dtype)
    return jnp.zeros_like(h).at[bidx, idx].add(ye)


def setup_inputs(seed: int = 0) -> dict:
    key = jax.random.key(seed)
    ks = jax.random.split(key, 32)
    f32 = jnp.float32
    L, D, E, F = DEPTH, D_MODEL, N_EXPERTS, EXPERT_FF
    nrm = lambda k, shape, s: jax.random.normal(k, shape, f32) * s
    return {
        "x": nrm(ks[0], (BATCH, SEQ, D), 1.0),
        "c": nrm(ks[1], (BATCH, D), 1.0),
        "ctx": nrm(ks[2], (BATCH, CTX_LEN, D), 1.0),
        "c_ctx": nrm(ks[3], (D,), 1.0),
        "w_ada": nrm(ks[4], (L, D, 6 * D), 0.5 * D ** -0.5),
        "b_ada": nrm(ks[5], (L, 6 * D), 0.02),
        "g_mix": 1.0 + nrm(ks[6], (L, D), 0.05),
        "g_ffn": 1.0 + nrm(ks[7], (L, D), 0.05),
        "w_in": nrm(ks[8], (L, D, IN_COLS), D ** -0.5),
        "na_q_g": 1.0 + nrm(ks[9], (L, NA_HEAD_DIM), 0.05),
        "na_k_g": 1.0 + nrm(ks[10], (L, NA_HEAD_DIM), 0.05),
        "na_rpb": nrm(ks[11], (L, NA_HEADS, 2 * NA_WIN_R - 1, 2 * NA_WIN_C - 1), 0.1),
        "df_q_g": 1.0 + nrm(ks[12], (L, DIFF_HEAD_DIM), 0.05),
        "df_k_g": 1.0 + nrm(ks[13], (L, DIFF_HEAD_DIM), 0.05),
        "df_lambda": nrm(ks[14], (L, 4, DIFF_HEAD_DIM), 0.1),
        "df_subln_g": 1.0 + nrm(ks[15], (L, 2 * DIFF_HEAD_DIM), 0.05),
        "pool_w": nrm(ks[16], (L, POOL_GROUPS, POOL_GROUP_W, POOL_GROUP_W), POOL_GROUP_W ** -0.5),
        "pool_scale": 1.0 + nrm(ks[17], (L, POOL_W), 0.05),
        "fnet_w": nrm(ks[18], (L, FNET_W, FNET_W), FNET_W ** -0.5),
        "w_branch": nrm(ks[19], (L, N_BRANCH, BRANCH_W, D), BRANCH_W ** -0.5),
        "w_out": nrm(ks[20], (L, D, D), D ** -0.5),
        "w_router": nrm(ks[21], (L, D, E), D ** -0.5),
        "w_gate_e": nrm(ks[22], (L, E, D, F), D ** -0.5),
        "w_up_e": nrm(ks[23], (L, E, D, F), D ** -0.5),
        "w_down_e": nrm(ks[24], (L, E, F, D), F ** -0.5),
    }


def reference(x, c, ctx, c_ctx, w_ada, b_ada, g_mix, g_ffn, w_in, na_q_g, na_k_g, na_rpb,
              df_q_g, df_k_g, df_lambda, df_subln_g, pool_w, pool_scale, fnet_w, w_branch,
              w_out, w_router, w_gate_e, w_up_e, w_down_e):
    T = x.shape[1]
    t = jnp.arange(T)
    rope = (*_rope_tables(t // GRID_W, DIFF_HEAD_DIM // 2),
            *_rope_tables(t % GRID_W, DIFF_HEAD_DIM // 2))
    s_c = jax.nn.silu(c)
    s_cc = jax.nn.silu(c_ctx)
    for l in range(DEPTH):
        last = l == DEPTH - 1
        mod = (s_c @ w_ada[l] + b_ada[l])[:, None, :]
        sh1, sc1, gt1, sh2, sc2, gt2 = jnp.split(mod, 6, axis=-1)
        cmod = s_cc @ w_ada[l] + b_ada[l]
        csh1, csc1, cgt1, csh2, csc2, cgt2 = jnp.split(cmod, 6, axis=-1)
        lam_init = 0.8 - 0.6 * math.exp(-0.3 * l)
        lv = df_lambda[l].astype(jnp.float32)
        lam = jnp.exp(jnp.sum(lv[0] * lv[1])) - jnp.exp(jnp.sum(lv[2] * lv[3])) + lam_init

        h = _modulate(_rmsnorm(x, g_mix[l]), sh1, sc1)
        hc = _modulate(_rmsnorm(ctx, g_mix[l]), csh1, csc1)
        pl = h @ w_in[l]
        na_q, df_q, pool_in, fnet_in, gate_l = _split_q(pl, na_q_g[l], df_q_g[l])
        na_k, na_v, df_k, df_v = _split_kv(pl[..., OFF_KV:], na_k_g[l], df_k_g[l])
        df_q = _axial_rope(df_q, rope)
        df_k = _axial_rope(df_k, rope)
        pc = hc @ (w_in[l][:, OFF_KV:] if last else w_in[l])
        na_kc, na_vc, df_kc, df_vc = _split_kv(pc[..., -KV_W:], na_k_g[l], df_k_g[l])

        y_na = _na_latent(na_q, na_k, na_v, na_kc, na_vc, na_rpb[l])
        y_df = _diff_post(_diff_latent(df_q, df_k, df_v, df_kc, df_vc, lam), df_subln_g[l], lam_init)
        y_pool = _pool_mixer(pool_in, pool_w[l], pool_scale[l])
        y_fnet = _fourier_mixer(fnet_in, fnet_w[l])
        x_mix = _merge((y_na, y_df, y_pool, y_fnet), gate_l, w_branch[l], w_out[l])

        if not last:
            na_qc, df_qc, pool_c, fnet_c, gate_c = _split_q(pc, na_q_g[l], df_q_g[l])
            yc_na = _dense_attn(na_qc, na_kc, na_vc)
            yc_df = _diff_post(_diff_core(df_qc.transpose(0, 2, 3, 1, 4),
                                          df_kc.transpose(0, 2, 3, 1, 4),
                                          df_vc.transpose(0, 2, 1, 3), lam),
                               df_subln_g[l], lam_init)
            yc_pool = _pool_mixer(pool_c, pool_w[l], pool_scale[l])
            yc_fnet = _fourier_mixer(fnet_c, fnet_w[l])
            ctx_mix = _merge((yc_na, yc_df, yc_pool, yc_fnet), gate_c, w_branch[l], w_out[l])

        x = x + gt1 * x_mix

        h2 = _modulate(_rmsnorm(x, g_ffn[l]), sh2, sc2)
        x = x + gt2 * _ec_ffn(h2, w_router[l], w_gate_e[l], w_up_e[l], w_down_e[l])

        if not last:
            ctx = ctx + cgt1 * ctx_mix
            hc2 = _modulate(_rmsnorm(ctx, g_ffn[l]), csh2, csc2)
            ctx = ctx + cgt2 * _ec_ffn(hc2, w_router[l], w_gate_e[l], w_up_e[l], w_down_e[l])
    return x
```

```python
import numpy as np
from contextlib import ExitStack
import concourse.bass as bass
import concourse.mybir as mybir
from concourse.bass_utils import run_bass_kernel_spmd

F32 = mybir.dt.float32
BF16 = mybir.dt.bfloat16
I32 = mybir.dt.int32
AF = mybir.ActivationFunctionType
ALU = mybir.AluOpType
AX = mybir.AxisListType

ENGS = ["tensor", "vector", "scalar", "gpsimd", "sync"]
DMA_RING = 6


class Buf:
    _n = 0

    def __init__(self, name, t=None):
        Buf._n += 1
        self.id = Buf._n
        self.name = name
        self.t = t

    def __getitem__(self, idx):
        return self.t[idx]

    def sub(self, s):
        return (self, s)


class Op:
    __slots__ = ("eng", "fn", "reads", "writes", "dma", "idx", "sig", "waits", "need_sig", "prewait")

    def __init__(self, eng, fn, reads, writes, dma):
        self.eng, self.fn, self.reads, self.writes, self.dma = eng, fn, reads, writes, dma
        self.sig = None
        self.waits = []
        self.need_sig = False
        self.prewait = None


def _norm(keys):
    out = []
    for k in keys:
        if isinstance(k, Buf):
            out.append((k.id, None))
        else:
            out.append((k[0].id, k[1]))
    return out


class Prog:
    def __init__(self, nc):
        self.nc = nc
        self.ops = []
        self.stack = ExitStack()

    def sb(self, name, shape, dtype):
        t = self.stack.enter_context(self.nc.sbuf_tensor("sb_" + name, list(shape), dtype))
        return Buf(name, t)

    def ps(self, name, shape, dtype=F32):
        t = self.stack.enter_context(self.nc.psum_tensor("ps_" + name, list(shape), dtype))
        return Buf(name, t)

    def op(self, eng, fn, reads=(), writes=(), dma=False):
        o = Op(eng, fn, _norm(reads), _norm(writes), dma)
        o.idx = len(self.ops)
        self.ops.append(o)
        return o

    def dma(self, eng, out, in_, reads=(), writes=(), **kw):
        return self.op(eng, lambda e: e.dma_start(out=out, in_=in_, **kw), reads, writes, dma=True)

    def finalize(self):
        nc = self.nc
        ops = self.ops
        last_w = {}
        readers = {}
        by_buf = {}

        def related(key):
            b, s = key
            subs = by_buf.setdefault(b, set())
            subs.add(s)
            if s is None:
                return [(b, x) for x in subs]
            return [(b, s), (b, None)] if None in subs else [(b, s)]

        deps = [None] * len(ops)
        for o in ops:
            d = {}
            for k in o.reads:
                for r in related(k):
                    w = last_w.get(r)
                    if w is not None:
                        d[w] = True
            for k in o.writes:
                for r in related(k):
                    w = last_w.get(r)
                    if w is not None:
                        d.setdefault(w, False)
                    for rd in readers.get(r, ()):
                        d.setdefault(rd, False)
            d.pop(o.idx, None)
            deps[o.idx] = d
            for k in o.reads:
                readers.setdefault(k, []).append(o.idx)
            for k in o.writes:
                last_w[k] = o.idx
                readers[k] = []
                if k[1] is None:
                    for s in by_buf.get(k[0], ()):
                        if s is not None:
                            last_w[(k[0], s)] = o.idx
                            readers[(k[0], s)] = []

        for o in ops:
            for j, raw in deps[o.idx].items():
                p = ops[j]
                if p.dma:
                    p.need_sig = True
                elif p.eng != o.eng:
                    p.need_sig = True
                elif p.eng != "tensor" or o.dma:
                    p.need_sig = True
        LIM = 30000
        DLIM = 1800
        sems = {e: [] for e in ENGS}
        rings = {e: [[] for _ in range(DMA_RING)] for e in ("sync", "scalar", "gpsimd")}
        cnt = {e: 0 for e in ENGS}
        dcnt = {e: 0 for e in rings}
        self.nsem = 0
        def newsem(tag):
            self.nsem += 1
            return self.stack.enter_context(nc.semaphore("%s_%d" % (tag, self.nsem)))
        def dsig(e, n):
            r, u = n % DMA_RING, n // DMA_RING
            lst = rings[e][r]
            while len(lst) <= u // DLIM:
                lst.append(newsem("d" + e))
            return (lst[u // DLIM], 16 * (u % DLIM + 1), 16)
        last_dma = {e: {} for e in rings}
        for o in ops:
            if o.dma:
                n = dcnt[o.eng]
                dcnt[o.eng] += 1
                o.prewait = dsig(o.eng, n - DMA_RING)[:2] if n >= DMA_RING else None
                o.sig = dsig(o.eng, n)
                last_dma[o.eng][n % DMA_RING] = o.sig
            elif o.need_sig:
                n = cnt[o.eng]
                cnt[o.eng] += 1
                lst = sems[o.eng]
                while len(lst) <= n // LIM:
                    lst.append(newsem("s" + o.eng))
                o.sig = (lst[n // LIM], n % LIM + 1, 1)
        known = {e: {} for e in ENGS}
        for o in ops:
            kn = known[o.eng]
            need = {}
            if o.prewait is not None:
                need[o.prewait[0]] = o.prewait[1]
            for j, raw in deps[o.idx].items():
                p = ops[j]
                if not p.dma and not o.dma and p.eng == o.eng and p.eng == "tensor":
                    continue
                sem, val, _ = p.sig
                if need.get(sem, 0) < val:
                    need[sem] = val
            for sem, val in need.items():
                if kn.get(sem, 0) < val:
                    kn[sem] = val
                    o.waits.append((sem, val))
        finals = {e: [(sg[0], sg[1]) for sg in last_dma[e].values()] for e in rings}
        self.nsig = dict(cnt)
        self.ndma = dict(dcnt)

        with nc.Block() as block:
            def mk(ename):
                def body(eng):
                    for o in ops:
                        if o.eng != ename:
                            continue
                        for sem, val in o.waits:
                            eng.wait_ge(sem, val)
                        ins = o.fn(eng)
                        if o.sig is not None:
                            ins.then_inc(o.sig[0], o.sig[2])
                    if ename in finals:
                        for sem, val in finals[ename]:
                            eng.wait_ge(sem, val)
                return body
            block.tensor(mk("tensor"))
            block.vector(mk("vector"))
            block.scalar(mk("scalar"))
            block.gpsimd(mk("gpsimd"))
            block.sync(mk("sync"))
        self.stack.close()

import numpy as np, ml_dtypes
BF = ml_dtypes.bfloat16
def fm8(v):
    return np.ascontiguousarray(v.reshape(8, 128).T)
def rope_tables(pos, dim=16, base=10000.0):
    half = dim // 2
    inv = base ** (-np.arange(half, dtype=np.float32) / half)
    ang = pos.astype(np.float32)[:, None] * inv[None, :]
    return np.cos(ang), np.sin(ang)
def cossin_table(tpos, rope_on=True):
    n = len(tpos)
    out = np.zeros((128, 2, n), np.float32)
    if not rope_on:
        out[:, 0, :] = 1.0
        return out
    cr, sr = rope_tables(tpos // 64); cc, sc = rope_tables(tpos % 64)
    for p in range(128):
        i = p % 32
        if i < 16:
            out[p, 0] = cr[:, i % 8]; out[p, 1] = sr[:, i % 8]
        else:
            out[p, 0] = cc[:, i % 8]; out[p, 1] = sc[:, i % 8]
    return out
def cmats():
    m = np.zeros((128, 4, 128), np.float32)
    m[:, 0, :] = 1.0
    for p in range(128):
        m[p, 1, (p // 64) * 64:(p // 64 + 1) * 64] = 1.0
        m[p, 2, (p // 32) * 32:(p // 32 + 1) * 32] = 1.0
    Pm = np.zeros((128, 128), np.float32)
    for blk in range(8):
        o = blk * 16
        for i in range(8):
            Pm[o + i, o + i + 8] = -1.0
            Pm[o + i + 8, o + i] = 1.0
    m[:, 3, :] = Pm.T
    return m
def dft_ch():
    c = np.arange(256)[:, None].astype(np.float64); mm = np.arange(256)[None, :].astype(np.float64)
    ang = 2 * np.pi * c * mm / 256
    return (np.concatenate([np.cos(ang), -np.sin(ang)], axis=1) / 16.0).astype(np.float32)

def na_bias_table(rpb, J):
    kb = int(np.clip(2 * J - 4, 0, 246))
    kr = np.arange(10)[:, None, None, None] + kb
    kc = np.arange(64)[None, :, None, None]
    qr = np.arange(2)[None, None, :, None] + 2 * J
    qc = np.arange(64)[None, None, None, :]
    rs = np.clip(qr - 4, 0, 248); cs = np.clip(qc - 8, 0, 48)
    valid = (kr >= rs) & (kr < rs + 8) & (kc >= cs) & (kc < cs + 16)
    ri = np.clip(kr - qr + 7, 0, 14); cidx = np.clip(kc - qc + 15, 0, 30)
    ri, cidx, valid = np.broadcast_arrays(ri, cidx, valid)
    tab = rpb[:, ri, cidx]
    tab = np.where(valid[None], tab, np.float32(-30000.0)).astype(np.float32)
    return kb, tab.reshape(4, 640, 128)

def na_core_inputs(rpb, R0, kT_full, v_full):
    J0 = R0 // 2
    kT = np.zeros((256, 74 * 64), kT_full.dtype); v = np.zeros((74 * 64, 256), v_full.dtype)
    g0 = R0 - 4
    lo, hi = max(g0, 0), min(g0 + 74, 256)
    kT[:, (lo - g0) * 64:(hi - g0) * 64] = kT_full[:, lo * 64:hi * 64]
    v[(lo - g0) * 64:(hi - g0) * 64] = v_full[lo * 64:hi * 64]
    ks = np.zeros((4, 256, 640), kT_full.dtype); vs = np.zeros((4, 640, 256), v_full.dtype)
    bias = np.zeros((128, 5, 4, 5, 128), np.float32)
    for var, j in enumerate((0, 1, 2, 30, 31)):
        kb, tab = na_bias_table(rpb, J0 + j)
        bias[:, var] = tab.reshape(4, 5, 128, 128).transpose(2, 0, 1, 3)
        if j != 2:
            sp = {0: 0, 1: 1, 30: 2, 31: 3}[j]
            ks[sp] = kT_full[:, kb * 64:(kb + 10) * 64]; vs[sp] = v_full[kb * 64:(kb + 10) * 64]
    v = np.ascontiguousarray(v.reshape(37, 128, 256).transpose(1, 0, 2))
    vs = np.ascontiguousarray(vs.reshape(4, 5, 128, 256).transpose(2, 0, 1, 3))
    return dict(kT=kT, v=v, ks=ks, vs=vs, bias=bias)

def fft_consts():
    i = np.arange(128, dtype=np.float64)
    a1 = 2 * np.pi * np.outer(i, i) / 128.0
    C1, S1 = np.cos(a1), np.sin(a1)
    cm = np.stack([C1.T, S1.T, -S1.T, C1.T / 128.0, S1.T / 128.0], axis=1)
    at = 2 * np.pi * np.outer(i, i) / 16384.0
    tw = np.stack([np.tile(np.cos(at)[:, None, :], (1, 4, 1)), np.tile(np.sin(at)[:, None, :], (1, 4, 1))], axis=1)
    return cm.astype(BF), tw.astype(np.float32)
def fft_z_layout(zr, zi):
    CH = zr.shape[1]
    f = lambda a: a.reshape(128, 128, CH).transpose(0, 2, 1)
    return np.ascontiguousarray(np.stack([f(zr), f(zi)], axis=1))

POOL_WINDOWS = (2, 4, 8, 16)
def pool_band(gt, N):
    out = np.zeros((128, 4, 3, 128), np.float32)
    for gi, w in enumerate(POOL_WINDOWS):
        for to in range(128):
            t = gt * 128 + to
            lo = min(max(t - w // 2, 0), N); hi = min(max(t + w // 2, 0), N)
            cnt = hi - lo
            for ti in range(lo, hi):
                j = ti // 128 - gt + 1
                out[ti % 128, gi, j, to] += 1.0 / cnt
            out[to, gi, 1, to] -= 1.0
    return out
def pool_bands_for_core(gt_first, nt_local, N):
    b = np.zeros((128, 3, 4, 3, 128), np.float32)
    b[:, 0] = pool_band(gt_first, N)
    if nt_local > 1:
        b[:, 1] = pool_band(gt_first + 1, N) if nt_local > 2 else 0
        b[:, 2] = pool_band(gt_first + nt_local - 1, N)
    return b.astype(BF)
def pool_u_layout(u_full, t0, ntok):
    N = u_full.shape[0]
    nt = ntok // 128
    buf = np.zeros(((nt + 2) * 128, 256), u_full.dtype)
    lo, hi = max(t0 - 128, 0), min(t0 + ntok + 128, N)
    buf[lo - (t0 - 128):hi - (t0 - 128)] = u_full[lo:hi]
    return np.ascontiguousarray(buf.reshape(nt + 2, 128, 256).transpose(1, 0, 2))


EPS = 1e-6
C_NAQ, C_DFQ, C_POOL, C_FNET, C_GATE, C_NAK, C_NAV, C_DFK, C_DFV = 0, 256, 512, 768, 1024, 5120, 5376, 5632, 5888


def build_proj(ntok, G):
    NG = ntok // G
    TT = max(1, G // 128)
    TM = min(G, 128)
    nc = bass.Bass("TRN2", target_bir_lowering=False)
    D = lambda n, s, dt, k: nc.dram_tensor(n, list(s), dt, kind=k).ap()
    xT = D("xT", [1024, ntok], F32, "ExternalInput")
    w_in = D("w_in", [1024, 2048], F32, "ExternalInput")
    vecs = D("vecs", [128, 24], F32, "ExternalInput")
    gq = D("gq", [128, 4], F32, "ExternalInput")
    cmats = D("cmats", [128, 4, 128], F32, "ExternalInput")
    cossin = D("cossin", [128, 2, ntok], F32, "ExternalInput")
    dftm = D("dftm", [256, 512], F32, "ExternalInput")
    qkT = D("qkT", [1024, ntok], BF16, "ExternalOutput")
    tm = D("tm", [ntok, 1280], BF16, "ExternalOutput")

    P = Prog(nc)
    hT = P.sb("hT", [128, 8, ntok], BF16)
    xs = [P.sb("xs%d" % i, [128, 8, G], F32) for i in range(2)]
    sq = P.sb("sq", [128, 8, G], BF16)
    tmpf = [P.sb("tmpf%d" % i, [128, G], F32) for i in range(2)]
    rstd = P.sb("rstd", [128, G], F32)
    sd = P.sb("sd", [128, G], F32)
    vec_sb = P.sb("vec_sb", [128, 24], F32)
    A_sb = P.sb("A_sb", [128, 8], F32)
    gq_sb = P.sb("gq_sb", [128, 4], F32)
    cm_f = P.sb("cm_f", [128, 4, 128], F32)
    cm = P.sb("cm", [128, 4, 128], BF16)
    dft_f = P.sb("dft_f", [128, 2, 512], F32)
    dft = P.sb("dft", [128, 2, 512], BF16)
    css = [P.sb("cs%d" % i, [128, 2, G], F32) for i in range(2)]
    wst = xs if G == 512 else [P.sb("wst%d" % i, [128, 8, 512], F32) for i in range(2)]
    wb = [P.sb("wb%d" % i, [128, 8, 512], BF16) for i in range(2)]
    outs = [P.sb("o%d" % i, [128, 512], BF16) for i in range(4)]
    sqh = P.sb("sqh", [128, G], BF16)
    xn = P.sb("xn", [128, G], BF16)
    t1 = P.sb("t1", [128, G], F32)
    t2 = P.sb("t2", [128, G], F32)
    uT = P.sb("uT", [128, 2, G], BF16)
    pm = [P.ps("pm%d" % i, [128, 512]) for i in range(3)]
    pst = P.ps("pst", [128, 512])
    prot = P.ps("prot", [128, 512])
    epsb = P.sb("epsb", [128, 1], F32)

    P.dma("sync", vec_sb[:], vecs, writes=[vec_sb])
    P.dma("sync", gq_sb[:], gq, writes=[gq_sb])
    P.dma("sync", cm_f[:], cmats, writes=[cm_f])
    P.dma("sync", dft_f[:], dftm.rearrange("(k p) n -> p k n", p=128), writes=[dft_f])
    P.op("vector", lambda e: e.tensor_copy(cm[:], cm_f[:]), [cm_f], [cm])
    P.op("vector", lambda e: e.tensor_copy(dft[:], dft_f[:]), [dft_f], [dft])
    P.op("vector", lambda e: e.memset(epsb[:], EPS), [], [epsb])
    P.op("vector", lambda e: e.tensor_scalar(A_sb[:], vec_sb[:, 8:16], 1.0, None, ALU.add), [vec_sb], [A_sb])
    P.op("vector", lambda e: e.tensor_tensor(A_sb[:], A_sb[:], vec_sb[:, 16:24], ALU.mult), [A_sb, vec_sb], [A_sb])

    def load_x(g):
        b = xs[g % 2]
        P.dma("sync", b[:], xT[:, g * G:(g + 1) * G].rearrange("(k p) t -> p k t", p=128), writes=[b])
    load_x(0)
    for g in range(NG):
        if g + 1 < NG:
            load_x(g + 1)
        b = xs[g % 2]
        P.op("scalar", lambda e, b=b: e.activation(sq[:], b[:], AF.Square), [b], [sq])
        for k in range(8):
            P.op("tensor", lambda e, k=k: e.matmul(pst[:, :G], cm[:, 0, :], sq[:, k, :], start=(k == 0), stop=(k == 7)),
                 [cm, sq], [pst])
        P.op("scalar", lambda e: e.activation(sd[:], pst[:, :G], AF.Sqrt, bias=epsb[:], scale=1.0 / 1024), [pst, epsb], [sd])
        P.op("vector", lambda e: e.reciprocal(rstd[:], sd[:]), [sd], [rstd])
        for k in range(8):
            tf = tmpf[k % 2]
            P.op("vector", lambda e, k=k, tf=tf, b=b: e.scalar_tensor_tensor(tf[:], b[:, k, :], A_sb[:, k:k + 1], rstd[:], ALU.mult, ALU.mult),
                 [b, A_sb, rstd], [tf])
            P.op("scalar", lambda e, k=k, tf=tf, g=g: e.activation(hT[:, k, g * G:(g + 1) * G], tf[:], AF.Identity, bias=vec_sb[:, k:k + 1], scale=1.0),
                 [tf, vec_sb], [hT.sub(g)])

    NB = 12
    def load_w(cb):
        st = wst[cb % 2]
        P.dma("sync", st[:], w_in[:, cb * 512:(cb + 1) * 512].rearrange("(k p) n -> p k n", p=128), writes=[st])
    def cast_w(cb):
        st, w = wst[cb % 2], wb[cb % 2]
        for k in range(8):
            eng = ("vector", "gpsimd")[k % 2]
            P.op(eng, lambda e, k=k, st=st, w=w: e.tensor_copy(w[:, k, :], st[:, k, :]), [st], [w.sub(k)])
    CBL = [0, 1, 2, 3]
    load_w(CBL[0])
    cast_w(CBL[0])
    cnt = {"pm": 0, "o": 0, "st": 0, "cs": 0}
    def store(dst, src, rd):
        eng = ("gpsimd", "sync")[cnt["st"] % 2]
        cnt["st"] += 1
        P.dma(eng, dst, src, reads=rd)

    def fm_chunk(w, jj, g, kind, row0):
        ps = pm[cnt["pm"] % 3]; cnt["pm"] += 1
        ts = slice(g * G, (g + 1) * G)
        for k in range(8):
            P.op("tensor", lambda e, k=k, ps=ps: e.matmul(ps[:, :G], w[:, k, jj * 128:(jj + 1) * 128], hT[:, k, ts], start=(k == 0), stop=(k == 7)),
                 [w.sub(k), hT.sub(g)], [ps])
        if kind == "gate":
            o = outs[cnt["o"] % 4]; cnt["o"] += 1
            P.op("scalar", lambda e: e.activation(o[:, :G], ps[:, :G], AF.Sigmoid), [ps], [o])
            store(gT[row0:row0 + 128, ts], o[:, :G], [o])
            return
        if kind == "fnet":
            c = jj % 2
            P.op("scalar", lambda e: e.copy(uT[:, c, :], ps[:, :G]), [ps], [uT.sub(c)])
            return
        hd, ci, gcol = {"naq": (64, 1, 0), "nak": (64, 1, 2), "dfq": (32, 2, 1), "dfk": (32, 2, 3)}[kind]
        P.op("scalar", lambda e: e.activation(sqh[:], ps[:, :G], AF.Square), [ps], [sqh])
        P.op("tensor", lambda e: e.matmul(pst[:, :G], cm[:, ci, :], sqh[:], start=True, stop=True), [cm, sqh], [pst])
        P.op("scalar", lambda e: e.activation(sd[:], pst[:, :G], AF.Sqrt, bias=epsb[:], scale=1.0 / hd), [pst, epsb], [sd])
        P.op("vector", lambda e: e.reciprocal(rstd[:], sd[:]), [sd], [rstd])
        o = outs[cnt["o"] % 4]; cnt["o"] += 1
        if kind in ("naq", "nak"):
            P.op("vector", lambda e: e.scalar_tensor_tensor(o[:, :G], ps[:, :G], gq_sb[:, gcol:gcol + 1], rstd[:], ALU.mult, ALU.mult),
                 [ps, gq_sb, rstd], [o])
        else:
            cs = css[cnt["cs"] % 2]; cnt["cs"] += 1
            P.dma("sync", cs[:], cossin[:, :, ts], writes=[cs])
            P.op("vector", lambda e: e.scalar_tensor_tensor(xn[:], ps[:, :G], gq_sb[:, gcol:gcol + 1], rstd[:], ALU.mult, ALU.mult),
                 [ps, gq_sb, rstd], [xn])
            P.op("tensor", lambda e: e.matmul(prot[:, :G], cm[:, 3, :], xn[:], start=True, stop=True), [cm, xn], [prot])
            P.op("vector", lambda e: e.tensor_tensor(t1[:], xn[:], cs[:, 0, :], ALU.mult), [xn, cs], [t1])
            P.op("vector", lambda e: e.tensor_tensor(t2[:], prot[:, :G], cs[:, 1, :], ALU.mult), [prot, cs], [t2])
            P.op("gpsimd", lambda e: e.tensor_tensor(o[:, :G], t1[:], t2[:], ALU.add), [t1, t2], [o])
        store(qkT[row0:row0 + 128, ts], o[:, :G], [o])

    def tm_block(w, c0, ncols, g, dcol):
        for tt in range(TT):
            ps = pm[cnt["pm"] % 3]; cnt["pm"] += 1
            t0 = g * G + tt * TM
            for k in range(8):
                P.op("tensor", lambda e, k=k, ps=ps, t0=t0: e.matmul(ps[:TM, :ncols], hT[:, k, t0:t0 + TM], w[:, k, c0:c0 + ncols], start=(k == 0), stop=(k == 7)),
                     [w.sub(k), hT.sub(g)], [ps])
            o = outs[cnt["o"] % 4]; cnt["o"] += 1
            P.op("vector", lambda e, ps=ps, o=o: e.tensor_copy(o[:TM, :ncols], ps[:TM, :ncols]), [ps], [o])
            store(tm[t0:t0 + TM, dcol:dcol + ncols], o[:TM, :ncols], [o])

    def fft_ab(g):
        for tt in range(TT):
            ps = pm[cnt["pm"] % 3]; cnt["pm"] += 1
            t0 = tt * TM
            for c in range(2):
                P.op("tensor", lambda e, c=c, ps=ps, t0=t0: e.matmul(ps[:TM, :], uT[:, c, t0:t0 + TM], dft[:, c, :], start=(c == 0), stop=(c == 1)),
                     [uT, dft], [ps])
            o = outs[cnt["o"] % 4]; cnt["o"] += 1
            P.op("vector", lambda e, ps=ps, o=o: e.tensor_copy(o[:TM, :], ps[:TM, :]), [ps], [o])
            store(tm[g * G + t0:g * G + t0 + TM, 256:768], o[:TM, :], [o])

    for ci_, cb in enumerate(CBL):
        if ci_ + 1 < len(CBL):
            load_w(CBL[ci_ + 1])
        w = wb[cb % 2]
        c0 = cb * 512
        for g in range(NG):
            if c0 == 0:
                for jj in range(4):
                    fm_chunk(w, jj, g, "naq" if jj < 2 else "dfq", jj * 128)
            elif c0 == 512:
                tm_block(w, 0, 256, g, 0)
                for jj in (2, 3):
                    fm_chunk(w, jj, g, "fnet", 0)
                fft_ab(g)
            elif c0 == 1024:
                for jj in range(2):
                    fm_chunk(w, jj, g, "nak", 512 + jj * 128)
                tm_block(w, 256, 256, g, 768)
            else:
                for jj in range(2):
                    fm_chunk(w, jj, g, "dfk", 768 + jj * 128)
                tm_block(w, 256, 256, g, 1024)
        if ci_ + 1 < len(CBL):
            cast_w(CBL[ci_ + 1])
    P.finalize()
    return nc


NKR = 74
NKT = NKR * 64 // 128


def var_of_j(j):
    return {0: 0, 1: 1, 30: 3, 31: 4}.get(j, 2)


def emit_na(P, nc, pre, qT_d, kT_d, v_d, ks_d, vs_d, kcT_d, vc_d, bias_d, ident_d, y_d):
    qT = P.sb(pre + "qT", [128, 2, 4096], BF16)
    kT = P.sb(pre + "kT", [128, 2, NKT * 128], BF16)
    va = P.sb(pre + "va", [128, NKT, 4, 128], BF16)
    kTs = P.sb(pre + "kTs", [128, 2, 4, 640], BF16)
    vas = P.sb(pre + "vas", [128, 4, 5, 4, 128], BF16)
    kcT = P.sb(pre + "kcT", [128, 2, 256], BF16)
    vca = P.sb(pre + "vca", [128, 2, 4, 128], BF16)
    bias = P.sb(pre + "bias", [128, 5, 4, 5, 128], F32)
    ident = P.sb(pre + "ident", [128, 128], F32)
    ts_ = [P.sb(pre + "t%d" % i, [128, 640], F32) for i in range(2)]
    pts = [P.sb(pre + "pt%d" % i, [128, 896], BF16) for i in range(2)]
    accs = [P.sb(pre + "accs%d" % i, [128, 512], F32) for i in range(2)]
    rl = P.sb(pre + "rl", [128, 4], F32)
    yo = [P.sb(pre + "yo%d" % i, [128, 256], BF16) for i in range(2)]
    pss = [P.ps(pre + "pss%d" % i, [128, 1024]) for i in range(2)]
    pacc = [P.ps(pre + "pacc%d" % i, [128, 512]) for i in range(2)]
    ptr = P.ps(pre + "ptr", [128, 512])

    P.dma("sync", qT[:], qT_d.rearrange("(c p) t -> p c t", p=128), writes=[qT])
    P.dma("sync", kT[:], kT_d.rearrange("(c p) t -> p c t", p=128), writes=[kT])
    P.dma("sync", kcT[:], kcT_d.rearrange("(c p) t -> p c t", p=128), writes=[kcT])
    vst = P.sb(pre + "vst", [128, NKT, 256], BF16)
    vcst = P.sb(pre + "vcst", [128, 2, 256], BF16)
    vsst = P.sb(pre + "vsst", [128, 4, 5, 256], BF16)
    P.dma("gpsimd", vst[:], v_d, writes=[vst])
    P.dma("gpsimd", vcst[:], vc_d, writes=[vcst])
    P.dma("gpsimd", vsst[:], vs_d, writes=[vsst])
    for h in range(4):
        eng = ("vector", "gpsimd")[h % 2]
        P.op(eng, lambda e, h=h: e.tensor_copy(va[:, :, h, 0:64], vst[:, :, h * 64:(h + 1) * 64]), [vst], [va.sub("v%d" % h)])
        P.op(eng, lambda e, h=h: e.tensor_copy(vca[:, :, h, 0:64], vcst[:, :, h * 64:(h + 1) * 64]), [vcst], [vca.sub("v%d" % h)])
        P.op(eng, lambda e, h=h: e.tensor_copy(vas[:, :, :, h, 0:64].rearrange("p a b d -> p (a b) d"), vsst[:, :, :, h * 64:(h + 1) * 64].rearrange("p a b d -> p (a b) d")), [vsst], [vas.sub("v%d" % h)])
    P.op("gpsimd", lambda e: e.memset(va[:, :, :, 64:128], 1.0), [], [va.sub("o")])
    P.op("gpsimd", lambda e: e.memset(vca[:, :, :, 64:128], 1.0), [], [vca.sub("o")])
    for sp in range(4):
        P.dma("sync", kTs[:, :, sp, :], ks_d[sp].rearrange("(c p) t -> p c t", p=128), writes=[kTs.sub(sp)])
    P.op("gpsimd", lambda e: e.memset(vas[:, :, :, :, 64:128].rearrange("p a b c d -> p (a b c) d"), 1.0), [], [vas.sub("o")])
    P.dma("sync", bias[:], bias_d, writes=[bias])
    P.dma("sync", ident[:], ident_d, writes=[ident])

    ci = 0
    for j in range(32):
        var = var_of_j(j)
        sp = {0: 0, 1: 1, 30: 2, 31: 3}.get(j)
        kt0 = j
        pa = pacc[j % 2]
        ac = accs[j % 2]
        for h in range(4):
            ps = pss[ci % 2]; t = ts_[ci % 2]; pt = pts[ci % 2]
            ci += 1
            c, p0 = h // 2, (h % 2) * 64
            for i in range(5):
                if sp is None:
                    kl = lambda i=i, c=c, p0=p0, kt0=kt0: kT[p0:p0 + 64, c, (kt0 + i) * 128:(kt0 + i + 1) * 128]
                else:
                    kl = lambda i=i, c=c, p0=p0, sp=sp: kTs[p0:p0 + 64, c, sp, i * 128:(i + 1) * 128]
                P.op("tensor", lambda e, ps=ps, i=i, c=c, p0=p0, j=j, kl=kl: e.matmul(
                    ps[:, i * 128:(i + 1) * 128], kl(),
                    qT[p0:p0 + 64, c, j * 128:(j + 1) * 128], start=True, stop=True), [kT, kTs, qT], [ps])
            for i in range(2):
                P.op("tensor", lambda e, ps=ps, i=i, c=c, p0=p0, j=j: e.matmul(
                    ps[:, 640 + i * 128:640 + (i + 1) * 128], kcT[p0:p0 + 64, c, i * 128:(i + 1) * 128],
                    qT[p0:p0 + 64, c, j * 128:(j + 1) * 128], start=True, stop=True), [kcT, qT], [ps])
            P.op("vector", lambda e, ps=ps, t=t, var=var, h=h: e.scalar_tensor_tensor(
                t[:], ps[:, 0:640], 0.125, bias[:, var, h, :, :].rearrange("p a b -> p (a b)"), ALU.mult, ALU.add), [ps, bias], [t])
            P.op("scalar", lambda e, t=t, pt=pt: e.activation(pt[:, 0:640], t[:], AF.Exp), [t], [pt.sub(0)])
            P.op("scalar", lambda e, ps=ps, pt=pt: e.activation(pt[:, 640:896], ps[:, 640:896], AF.Exp, scale=0.125), [ps], [pt.sub(1)])
            for i in range(7):
                if i < 5 and sp is not None:
                    lhs = lambda i=i, h=h, sp=sp: vas[:, sp, i, h, :]
                elif i < 5:
                    lhs = lambda i=i, h=h, kt0=kt0: va[:, kt0 + i, h, :]
                else:
                    lhs = lambda i=i, h=h: vca[:, i - 5, h, :]
                P.op("tensor", lambda e, lhs=lhs, pt=pt, i=i, pa=pa, h=h: e.matmul(
                    pa[:, h * 128:(h + 1) * 128], lhs(), pt[:, i * 128:(i + 1) * 128], start=(i == 0), stop=(i == 6)),
                    [va, vas, vca, pt], [pa.sub(h)])
        P.op("scalar", lambda e, ac=ac, pa=pa: e.copy(ac[:], pa[:]), [pa], [ac])
        for h in range(4):
            P.op("tensor", lambda e, h=h, ac=ac: e.transpose(ptr[:, h * 128:(h + 1) * 128], ac[:, h * 128:(h + 1) * 128], ident[:]),
                 [ac, ident], [ptr])
        y = yo[j % 2]
        for h in range(4):
            P.op("vector", lambda e, h=h: e.reciprocal(rl[:, h:h + 1], ptr[:, h * 128 + 64:h * 128 + 65]), [ptr], [rl.sub(h)])
            P.op("vector", lambda e, h=h, y=y: e.tensor_scalar(y[:, h * 64:(h + 1) * 64], ptr[:, h * 128:h * 128 + 64], rl[:, h:h + 1], None, ALU.mult),
                 [ptr.sub(h), rl.sub(h)], [y.sub(h)])
        P.dma("sync", y_d[j * 128:(j + 1) * 128, :], y[:], reads=[y])


def build_na():
    nc = bass.Bass("TRN2", target_bir_lowering=False)
    D = lambda n, s, dt, k: nc.dram_tensor(n, list(s), dt, kind=k).ap()
    qT = D("qT", [256, 4096], BF16, "ExternalInput")
    kT = D("kT", [256, NKT * 128], BF16, "ExternalInput")
    v = D("v", [128, NKT, 256], BF16, "ExternalInput")
    ks = D("ks", [4, 256, 640], BF16, "ExternalInput")
    vs = D("vs", [128, 4, 5, 256], BF16, "ExternalInput")
    kcT = D("kcT", [256, 256], BF16, "ExternalInput")
    vc = D("vc", [128, 2, 256], BF16, "ExternalInput")
    bias = D("bias", [128, 5, 4, 5, 128], F32, "ExternalInput")
    ident = D("ident", [128, 128], F32, "ExternalInput")
    y = D("y", [4096, 256], BF16, "ExternalOutput")
    P = Prog(nc)
    emit_na(P, nc, "n_", qT, kT, v, ks, vs, kcT, vc, bias, ident, y)
    P.finalize()
    return nc


EPS = 1e-6


def emit_fattn(P, nc, pre, Tq, Tk, nmaps, dk, diff, qT_d, kT_d, v_d, lamv_d, gsub_d, cst_d, ident_d, y_d):
    KT = Tk // 128
    QG = min(512, Tq)
    NQ = Tq // QG
    TT = QG // 128
    R = nmaps * dk
    scale = float(dk) ** -0.5
    qT = P.sb(pre + "qT", [R, Tq], BF16)
    kT = P.sb(pre + "kT", [R, Tk], BF16)
    va = P.sb(pre + "va", [128, KT, 128], BF16)
    ident = P.sb(pre + "ident", [128, 128], F32)
    lamv = P.sb(pre + "lamv", [128, 128], F32)
    gsub = P.sb(pre + "gsub", [128, 64], F32)
    cst = P.sb(pre + "cst", [128, 2], F32)
    gsc = P.sb(pre + "gsc", [128, 64], F32)
    sm = P.sb(pre + "sm", [128, 8], F32)
    prod = P.sb(pre + "prod", [128, 64], F32)
    epsb = P.sb(pre + "epsb", [128, 1], F32)
    pts = [P.sb(pre + "pt%d" % i, [128, QG], BF16) for i in range(3)]
    accs = [P.sb(pre + "accs%d" % m, [128, QG], F32) for m in range(nmaps)]
    om = [P.sb(pre + "om%d" % m, [128, 64], F32) for m in range(2)]
    rl = P.sb(pre + "rl", [128, 2], F32)
    ss = P.sb(pre + "ss", [128, 2], F32)
    junk = P.sb(pre + "junk", [128, 64], F32)
    yo = [P.sb(pre + "yo%d" % i, [128, 64], BF16) for i in range(2)]
    pss = [P.ps(pre + "pss%d" % i, [128, 512]) for i in range(3)]
    pacc = [P.ps(pre + "pacc%d" % m, [128, 512]) for m in range(nmaps)]
    ptr = [P.ps(pre + "ptr%d" % m, [128, 128]) for m in range(nmaps)]

    P.dma("sync", qT[:], qT_d, writes=[qT])
    P.dma("sync", kT[:], kT_d, writes=[kT])
    vst = P.sb(pre + "vst", [128, KT, 64], BF16)
    P.dma("sync", vst[:], v_d, writes=[vst])
    P.op("vector", lambda e: e.tensor_copy(va[:, :, 0:64], vst[:]), [vst], [va.sub("v")])
    P.op("gpsimd", lambda e: e.memset(va[:, :, 64:128], 1.0), [], [va.sub("o")])
    P.dma("sync", ident[:], ident_d, writes=[ident])
    P.op("vector", lambda e: e.memset(epsb[:], EPS), [], [epsb])
    if diff:
        P.dma("sync", lamv[:], lamv_d, writes=[lamv])
        P.dma("sync", gsub[:], gsub_d, writes=[gsub])
        P.dma("sync", cst[:], cst_d, writes=[cst])
        P.op("vector", lambda e: e.tensor_tensor(prod[:, 0:32], lamv[:, 0:32], lamv[:, 32:64], ALU.mult), [lamv], [prod])
        P.op("vector", lambda e: e.tensor_tensor(prod[:, 32:64], lamv[:, 64:96], lamv[:, 96:128], ALU.mult), [lamv], [prod])
        P.op("vector", lambda e: e.reduce_sum(sm[:, 0:1], prod[:, 0:32], AX.X), [prod], [sm])
        P.op("vector", lambda e: e.reduce_sum(sm[:, 1:2], prod[:, 32:64], AX.X), [prod, sm], [sm])
        P.op("scalar", lambda e: e.activation(sm[:, 2:4], sm[:, 0:2], AF.Exp), [sm], [sm])
        P.op("vector", lambda e: e.tensor_tensor(sm[:, 4:5], sm[:, 3:4], sm[:, 2:3], ALU.subtract), [sm], [sm])
        P.op("vector", lambda e: e.tensor_tensor(sm[:, 4:5], sm[:, 4:5], cst[:, 0:1], ALU.subtract), [sm, cst], [sm])
        P.op("vector", lambda e: e.tensor_scalar(gsc[:], gsub[:], cst[:, 1:2], None, ALU.mult), [gsub, cst], [gsc])

    ci = 0
    yi = 0
    for qg in range(NQ):
        qs = slice(qg * QG, (qg + 1) * QG)
        for kt in range(KT):
            for m in range(nmaps):
                ps = pss[ci % 3]
                pt = pts[ci % 3]
                ci += 1
                rs = slice(m * dk, (m + 1) * dk)
                P.op("tensor", lambda e, ps=ps, rs=rs, kt=kt, qs=qs: e.matmul(ps[:, :QG], kT[rs, kt * 128:(kt + 1) * 128], qT[rs, qs], start=True, stop=True),
                     [kT, qT], [ps])
                P.op("scalar", lambda e, ps=ps, pt=pt: e.activation(pt[:], ps[:, :QG], AF.Exp, scale=scale), [ps], [pt])
                P.op("tensor", lambda e, pt=pt, kt=kt, m=m: e.matmul(pacc[m][:, :QG], va[:, kt, :], pt[:], start=(kt == 0), stop=(kt == KT - 1)),
                     [va, pt], [pacc[m]])
        for m in range(nmaps):
            eng = ("vector", "scalar")[m % 2]
            if eng == "vector":
                P.op("vector", lambda e, m=m: e.tensor_copy(accs[m][:], pacc[m][:, :QG]), [pacc[m]], [accs[m]])
            else:
                P.op("scalar", lambda e, m=m: e.copy(accs[m][:], pacc[m][:, :QG]), [pacc[m]], [accs[m]])
        for tt in range(TT):
            for m in range(nmaps):
                P.op("tensor", lambda e, m=m, tt=tt: e.transpose(ptr[m][:], accs[m][:, tt * 128:(tt + 1) * 128], ident[:]),
                     [accs[m], ident], [ptr[m]])
                P.op("vector", lambda e, m=m: e.reciprocal(rl[:, m:m + 1], ptr[m][:, 64:65]), [ptr[m]], [rl.sub(m)])
                P.op("vector", lambda e, m=m: e.tensor_scalar(om[m][:], ptr[m][:, 0:64], rl[:, m:m + 1], None, ALU.mult),
                     [ptr[m], rl.sub(m)], [om[m]])
            y = yo[yi % 2]
            yi += 1
            if diff:
                P.op("vector", lambda e: e.scalar_tensor_tensor(om[0][:], om[1][:], sm[:, 4:5], om[0][:], ALU.mult, ALU.add),
                     [om[0], om[1], sm], [om[0]])
                P.op("scalar", lambda e: e.activation(junk[:], om[0][:], AF.Square, accum_out=ss[:, 0:1]), [om[0]], [junk, ss])
                P.op("scalar", lambda e: e.activation(ss[:, 1:2], ss[:, 0:1], AF.Sqrt, bias=epsb[:], scale=1.0 / 64), [ss, epsb], [ss])
                P.op("vector", lambda e: e.reciprocal(rl[:, 0:1], ss[:, 1:2]), [ss], [rl.sub(0)])
                P.op("vector", lambda e, y=y: e.scalar_tensor_tensor(y[:], om[0][:], rl[:, 0:1], gsc[:], ALU.mult, ALU.mult),
                     [om[0], rl.sub(0), gsc], [y])
            else:
                P.op("vector", lambda e, y=y: e.tensor_copy(y[:], om[0][:]), [om[0]], [y])
            t0 = qg * QG + tt * 128
            P.dma("gpsimd", y_d[t0:t0 + 128, :], y[:], reads=[y])


def build_fattn(Tq, Tk, nmaps, dk, diff):
    nc = bass.Bass("TRN2", target_bir_lowering=False)
    D = lambda n, s, dt, k: nc.dram_tensor(n, list(s), dt, kind=k).ap()
    R = nmaps * dk
    qT = D("qT", [R, Tq], BF16, "ExternalInput")
    kT = D("kT", [R, Tk], BF16, "ExternalInput")
    v = D("v", [128, Tk // 128, 64], BF16, "ExternalInput")
    lamv = D("lamv", [128, 128], F32, "ExternalInput")
    gsub = D("gsub", [128, 64], F32, "ExternalInput")
    cst = D("cst", [128, 2], F32, "ExternalInput")
    ident = D("ident", [128, 128], F32, "ExternalInput")
    y = D("y", [Tq, 64], BF16, "ExternalOutput")
    P = Prog(nc)
    emit_fattn(P, nc, "a_", Tq, Tk, nmaps, dk, diff, qT, kT, v, lamv, gsub, cst, ident, y)
    P.finalize()
    return nc


def emit_fft(P, nc, pre, z_d, cm_d, tw_d, fT_d, CH=64):
    z = P.sb(pre + "z", [128, 2, CH, 128], BF16)
    cm = P.sb(pre + "cm", [128, 5, 128], BF16)
    tw = P.sb(pre + "tw", [128, 2, 4, 128], F32)
    tt = [P.sb(pre + "tt%d" % i, [128, 512], F32) for i in range(4)]
    ypr = [P.sb(pre + "ypr%d" % i, [128, 512], BF16) for i in range(2)]
    ypi = [P.sb(pre + "ypi%d" % i, [128, 512], BF16) for i in range(2)]
    xo = [P.sb(pre + "xo%d" % i, [128, 512], BF16) for i in range(2)]
    pyr = [P.ps(pre + "pyr%d" % i, [128, 512]) for i in range(2)]
    pyi = [P.ps(pre + "pyi%d" % i, [128, 512]) for i in range(2)]
    px = [P.ps(pre + "px%d" % i, [128, 512]) for i in range(2)]
    P.dma("sync", z[:, 0], z_d[:, 0], writes=[z.sub(0)])
    P.dma("gpsimd", z[:, 1], z_d[:, 1], writes=[z.sub(1)])
    P.dma("sync", cm[:], cm_d, writes=[cm])
    P.dma("sync", tw[:], tw_d, writes=[tw])
    ctf = tw[:, 0].rearrange("p a b -> p (a b)")
    stf = tw[:, 1].rearrange("p a b -> p (a b)")
    for g in range(CH // 4):
        yr, yi = pyr[g % 2], pyi[g % 2]
        for cc in range(4):
            c = g * 4 + cc
            o = slice(cc * 128, (cc + 1) * 128)
            P.op("tensor", lambda e, c=c, o=o, yr=yr: e.matmul(yr[:, o], z[:, 0, c, :], cm[:, 0, :], start=True, stop=False), [z, cm], [yr])
            P.op("tensor", lambda e, c=c, o=o, yr=yr: e.matmul(yr[:, o], z[:, 1, c, :], cm[:, 1, :], start=False, stop=True), [z, cm], [yr])
            P.op("tensor", lambda e, c=c, o=o, yi=yi: e.matmul(yi[:, o], z[:, 1, c, :], cm[:, 0, :], start=True, stop=False), [z, cm], [yi])
            P.op("tensor", lambda e, c=c, o=o, yi=yi: e.matmul(yi[:, o], z[:, 0, c, :], cm[:, 2, :], start=False, stop=True), [z, cm], [yi])
        a, b = ypr[g % 2], ypi[g % 2]
        P.op("vector", lambda e, yr=yr: e.tensor_tensor(tt[0][:], yr[:], ctf, ALU.mult), [yr, tw], [tt[0]])
        P.op("vector", lambda e, yi=yi: e.tensor_tensor(tt[1][:], yi[:], stf, ALU.mult), [yi, tw], [tt[1]])
        P.op("gpsimd", lambda e, a=a: e.tensor_tensor(a[:], tt[0][:], tt[1][:], ALU.add), [tt[0], tt[1]], [a])
        P.op("vector", lambda e, yi=yi: e.tensor_tensor(tt[2][:], yi[:], ctf, ALU.mult), [yi, tw], [tt[2]])
        P.op("vector", lambda e, yr=yr: e.tensor_tensor(tt[3][:], yr[:], stf, ALU.mult), [yr, tw], [tt[3]])
        P.op("gpsimd", lambda e, b=b: e.tensor_tensor(b[:], tt[2][:], tt[3][:], ALU.subtract), [tt[2], tt[3]], [b])
        x = px[g % 2]
        P.op("tensor", lambda e, x=x, a=a: e.matmul(x[:], cm[:, 3, :], a[:], start=True, stop=False), [cm, a], [x])
        P.op("tensor", lambda e, x=x, b=b: e.matmul(x[:], cm[:, 4, :], b[:], start=False, stop=True), [cm, b], [x])
        o_ = xo[g % 2]
        P.op("scalar", lambda e, x=x, o_=o_: e.copy(o_[:], x[:]), [x], [o_])
        P.dma(("sync", "gpsimd")[g % 2], fT_d[g * 4:(g + 1) * 4, :].rearrange("c (k2 k1) -> k2 c k1", k1=128),
              o_[:].rearrange("p (c k) -> p c k", c=4), reads=[o_])


def build_fft(CH=64):
    nc = bass.Bass("TRN2", target_bir_lowering=False)
    D = lambda n, s, dt, k: nc.dram_tensor(n, list(s), dt, kind=k).ap()
    z = D("z", [128, 2, CH, 128], BF16, "ExternalInput")
    cm = D("cm", [128, 5, 128], BF16, "ExternalInput")
    tw = D("tw", [128, 2, 4, 128], F32, "ExternalInput")
    fT = D("fT", [CH, 16384], BF16, "ExternalOutput")
    P = Prog(nc)
    emit_fft(P, nc, "f_", z, cm, tw, fT, CH)
    P.finalize()
    return nc


def build_fft256(CH=64):
    nc = bass.Bass("TRN2", target_bir_lowering=False)
    D = lambda n, s, dt, k: nc.dram_tensor(n, list(s), dt, kind=k).ap()
    z_d = D("z", [128, 2, 2, CH], BF16, "ExternalInput")
    cn_d = D("cn", [128, 2, 2, 256], BF16, "ExternalInput")
    fT = D("fT", [CH, 256], BF16, "ExternalOutput")
    P = Prog(nc)
    z = P.sb("z", [128, 2, 2, CH], BF16)
    cn = P.sb("cn", [128, 2, 2, 256], BF16)
    o = P.sb("o", [CH, 256], BF16)
    ps = P.ps("ps", [128, 512])
    P.dma("sync", z[:], z_d, writes=[z])
    P.dma("sync", cn[:], cn_d, writes=[cn])
    n = 0
    for ri in range(2):
        for t in range(2):
            P.op("tensor", lambda e, ri=ri, t=t, n=n: e.matmul(ps[:CH, :256], z[:, ri, t, :], cn[:, ri, t, :], start=(n == 0), stop=(n == 3)), [z, cn], [ps])
            n += 1
    P.op("vector", lambda e: e.tensor_copy(o[:], ps[:CH, :256]), [ps], [o])
    P.dma("sync", fT, o[:], reads=[o])
    P.finalize()
    return nc


EPS = 1e-6


def build_merge(ntok, G):
    NG = ntok // G
    TT = G // 128
    NT = ntok // 128
    nc = bass.Bass("TRN2", target_bir_lowering=False)
    D = lambda n, s, dt, k: nc.dram_tensor(n, list(s), dt, kind=k).ap()
    xT = D("xT", [1024, ntok], F32, "ExternalInput")
    vecs = D("vecs", [128, 56], F32, "ExternalInput")
    wg_d = D("wg", [1024, 4096], F32, "ExternalInput")
    wbr_d = D("wbr", [4, 256, 1024], F32, "ExternalInput")
    wo_d = D("wo", [1024, 1024], F32, "ExternalInput")
    wr_d = D("wr", [1024, 16], F32, "ExternalInput")
    fw_d = D("fw", [256, 256], F32, "ExternalInput")
    pw_d = D("pw", [4, 64, 64], F32, "ExternalInput")
    psc_d = D("psc", [64, 4], F32, "ExternalInput")
    band_d = D("band", [128, 3, 4, 3, 128], BF16, "ExternalInput")
    ones_d = D("ones", [128, 128], BF16, "ExternalInput")
    yT_d = D("yT", [3, 256, ntok], BF16, "ExternalInput")
    u_d = D("u", [128, NT + 2, 256], BF16, "ExternalInput")
    xmT = D("xmT", [1024, ntok], F32, "ExternalOutput")
    h2T = D("h2T", [1024, ntok], BF16, "ExternalOutput")
    aff = D("aff", [ntok, 16], F32, "ExternalOutput")

    P = Prog(nc)
    wg = P.sb("wg", [128, 8, 4096], BF16)
    wo = P.sb("wo", [128, 8, 1024], BF16)
    wb = P.sb("wb", [128, 3, 2, 1024], BF16)
    wb2 = P.sb("wb2", [64, 4, 1024], BF16)
    fw = P.sb("fw", [128, 2, 256], BF16)
    pw = P.sb("pw", [64, 4, 64], BF16)
    wr = P.sb("wr", [128, 8, 16], F32)
    psc = P.sb("psc", [64, 4], F32)
    band = P.sb("band", [128, 3, 4, 3, 128], BF16)
    ones = P.sb("ones", [128, 128], BF16)
    vec = P.sb("vec", [128, 56], F32)
    A1 = P.sb("A1", [128, 8], F32)
    A2 = P.sb("A2", [128, 8], F32)
    epsb = P.sb("epsb", [128, 1], F32)
    xs = P.sb("xs", [128, 8, G], F32)
    stg = P.sb("stg", [128, 8, 512], F32)
    hT = P.sb("hT", [128, 8, G], BF16)
    sqb = P.sb("sqb", [128, 8, G], BF16)
    acc = P.sb("acc", [128, 8, G], F32)
    yt = P.sb("yt", [128, 3, 2, G], BF16)
    yfn = P.sb("yfn", [128, 2, G], BF16)
    ypl = P.sb("ypl", [64, 4, G], BF16)
    pld = P.sb("pld", [64, 4, G], BF16)
    ub = P.sb("ub", [128, TT + 2, 256], BF16)
    gsb = [P.sb("gsb%d" % i, [128, G], BF16) for i in range(2)]
    tmp = [P.sb("tmp%d" % i, [128, G], F32) for i in range(2)]
    rstd = P.sb("rstd", [128, G], F32)
    sd = P.sb("sd", [128, G], F32)
    lg = P.sb("lg", [128, 16], F32)
    ex = P.sb("ex", [128, 16], F32)
    sm = P.sb("sm", [128, 4], F32)
    ao = [P.sb("ao%d" % i, [128, 16], F32) for i in range(2)]
    pg = [P.ps("pg%d" % i, [128, 512]) for i in range(2)]
    pz = [P.ps("pz%d" % i, [128, 512]) for i in range(2)]
    pst = P.ps("pst", [128, 512])
    pp = P.ps("pp", [64, 4, 128])
    pl = P.ps("pl", [128, 16])
    h2f = stg

    P.dma("sync", vec[:], vecs, writes=[vec])
    P.dma("sync", band[:], band_d, writes=[band])
    P.dma("sync", ones[:], ones_d, writes=[ones])
    P.dma("sync", psc[:], psc_d, writes=[psc])
    P.dma("sync", wr[:], wr_d.rearrange("(k p) n -> p k n", p=128), writes=[wr])
    P.op("vector", lambda e: e.memset(epsb[:], EPS), [], [epsb])
    P.op("vector", lambda e: e.tensor_scalar(A1[:], vec[:, 8:16], 1.0, None, ALU.add), [vec], [A1])
    P.op("vector", lambda e: e.tensor_tensor(A1[:], A1[:], vec[:, 16:24], ALU.mult), [A1, vec], [A1])
    P.op("vector", lambda e: e.tensor_scalar(A2[:], vec[:, 40:48], 1.0, None, ALU.add), [vec], [A2])
    P.op("vector", lambda e: e.tensor_tensor(A2[:], A2[:], vec[:, 48:56], ALU.mult), [A2, vec], [A2])
    ce = [0]
    def cast(dst, src, rd, wr_):
        eng = ("vector", "gpsimd", "scalar")[ce[0] % 3]
        ce[0] += 1
        if eng == "scalar":
            P.op("scalar", lambda e: e.copy(dst, src), rd, wr_)
        else:
            P.op(eng, lambda e: e.tensor_copy(dst, src), rd, wr_)
    for cb in range(8):
        P.dma("sync", stg[:], wg_d[:, cb * 512:(cb + 1) * 512].rearrange("(k p) n -> p k n", p=128), writes=[stg])
        for k in range(8):
            cast(wg[:, k, cb * 512:(cb + 1) * 512], stg[:, k, :], [stg], [wg.sub((cb, k))])
    for cb in range(2):
        P.dma("sync", stg[:], wo_d[:, cb * 512:(cb + 1) * 512].rearrange("(k p) n -> p k n", p=128), writes=[stg])
        for k in range(8):
            cast(wo[:, k, cb * 512:(cb + 1) * 512], stg[:, k, :], [stg], [wo.sub((cb, k))])
    for cb in range(2):
        P.dma("sync", stg[:], wbr_d[:, :, cb * 512:(cb + 1) * 512].rearrange("i (c p) n -> p (i c) n", p=128), writes=[stg])
        for bi, i in enumerate((0, 1, 3)):
            for c in range(2):
                cast(wb[:, bi, c, cb * 512:(cb + 1) * 512], stg[:, i * 2 + c, :], [stg], [wb.sub((bi, c, cb))])
    for cb in range(2):
        P.dma("sync", stg[0:64, 0:4, :], wbr_d[2, :, cb * 512:(cb + 1) * 512].rearrange("(g p) n -> p g n", p=64), writes=[stg])
        cast(wb2[:, :, cb * 512:(cb + 1) * 512], stg[0:64, 0:4, :], [stg], [wb2.sub(cb)])
    P.dma("sync", stg[:, 0:2, 0:256], fw_d.rearrange("(c p) n -> p c n", p=128), writes=[stg])
    cast(fw[:], stg[:, 0:2, 0:256], [stg], [fw])
    P.dma("sync", stg[0:64, 0:4, 0:64], pw_d.rearrange("g p n -> p g n"), writes=[stg])
    cast(pw[:], stg[0:64, 0:4, 0:64], [stg], [pw])

    def norm(src, A, shcol, dst_bf, dst_f32, rd):
        P.op("scalar", lambda e: e.activation(sqb[:], src[:], AF.Square), [src], [sqb])
        for k in range(8):
            P.op("tensor", lambda e, k=k: e.matmul(pst[:, :G], ones[:], sqb[:, k, :], start=(k == 0), stop=(k == 7)), [ones, sqb], [pst])
        P.op("scalar", lambda e: e.activation(sd[:], pst[:, :G], AF.Sqrt, bias=epsb[:], scale=1.0 / 1024), [pst, epsb], [sd])
        P.op("vector", lambda e: e.reciprocal(rstd[:], sd[:]), [sd], [rstd])
        for k in range(8):
            tf = tmp[k % 2]
            P.op("vector", lambda e, k=k, tf=tf: e.scalar_tensor_tensor(tf[:], src[:, k, :], A[:, k:k + 1], rstd[:], ALU.mult, ALU.mult),
                 [src, A, rstd], [tf])
            if dst_f32 is None:
                P.op("scalar", lambda e, k=k, tf=tf: e.activation(dst_bf[:, k, :], tf[:], AF.Identity, bias=vec[:, shcol + k:shcol + k + 1], scale=1.0),
                     [tf, vec], [dst_bf.sub(k)])
            else:
                P.op("scalar", lambda e, k=k, tf=tf: e.activation(dst_f32[:, k, :G], tf[:], AF.Identity, bias=vec[:, shcol + k:shcol + k + 1], scale=1.0),
                     [tf, vec], [dst_f32.sub(k)])
                P.op("gpsimd", lambda e, k=k: e.tensor_copy(dst_bf[:, k, :], dst_f32[:, k, :G]), [dst_f32.sub(k)], [dst_bf.sub(k)])

    ci = [0]
    for g in range(NG):
        ts = slice(g * G, (g + 1) * G)
        P.dma("sync", xs[:], xT[:, ts].rearrange("(k p) t -> p k t", p=128), writes=[xs])
        P.dma("gpsimd", yt[:].rearrange("p i c t -> p (i c) t"), yT_d[:, :, ts].rearrange("i (c p) t -> p (i c) t", p=128), writes=[yt])
        P.dma("gpsimd", ub[:], u_d[:, g * TT:g * TT + TT + 2, :], writes=[ub])
        norm(xs, A1, 0, hT, None, None)
        for mo in range(2):
            ps = pz[ci[0] % 2]; ci[0] += 1
            for c in range(2):
                P.op("tensor", lambda e, ps=ps, c=c, mo=mo: e.matmul(ps[:, :G], fw[:, c, mo * 128:(mo + 1) * 128], yt[:, 2, c, :], start=(c == 0), stop=(c == 1)),
                     [fw, yt], [ps])
            P.op("scalar", lambda e, ps=ps, mo=mo: e.copy(yfn[:, mo, :], ps[:, :G]), [ps], [yfn.sub(mo)])
        for tt in range(TT):
            li = g * TT + tt
            var = 0 if li == 0 else (2 if li == NT - 1 else 1)
            for gr in range(4):
                for j in range(3):
                    P.op("tensor", lambda e, gr=gr, j=j, tt=tt, var=var: e.matmul(pp[:, gr, :], ub[:, tt + j, gr * 64:(gr + 1) * 64], band[:, var, gr, j, :],
                                                                              start=(j == 0), stop=(j == 2)), [ub, band], [pp])
            P.op("vector", lambda e, tt=tt: e.tensor_copy(pld[:, :, tt * 128:(tt + 1) * 128], pp[:]), [pp], [pld.sub(tt)])
        for gr in range(4):
            ps = pz[ci[0] % 2]; ci[0] += 1
            P.op("tensor", lambda e, ps=ps, gr=gr: e.matmul(ps[0:64, :G], pw[:, gr, :], pld[:, gr, :], start=True, stop=True), [pw, pld], [ps])
            P.op("scalar", lambda e, ps=ps, gr=gr: e.activation(ypl[:, gr, :], ps[0:64, :G], AF.Identity, scale=psc[:, gr:gr + 1]), [ps, psc], [ypl.sub(gr)])
        for dc in range(8):
            ds_ = slice(dc * 128, (dc + 1) * 128)
            for i in range(4):
                pgt = pg[ci[0] % 2]; pzt = pz[ci[0] % 2]; gs = gsb[ci[0] % 2]; ci[0] += 1
                for k in range(8):
                    P.op("tensor", lambda e, pgt=pgt, k=k, i=i, dc=dc: e.matmul(pgt[:, :G], wg[:, k, i * 1024 + dc * 128:i * 1024 + (dc + 1) * 128], hT[:, k, :],
                                                                              start=(k == 0), stop=(k == 7)), [wg, hT], [pgt])
                if i == 2:
                    for gr in range(4):
                        P.op("tensor", lambda e, pzt=pzt, gr=gr, ds_=ds_: e.matmul(pzt[:, :G], wb2[:, gr, ds_], ypl[:, gr, :], start=(gr == 0), stop=(gr == 3)),
                             [wb2, ypl], [pzt])
                else:
                    bi = {0: 0, 1: 1, 3: 2}[i]
                    for c in range(2):
                        rhs = (lambda c=c, i=i: yt[:, i, c, :]) if i < 2 else (lambda c=c: yfn[:, c, :])
                        P.op("tensor", lambda e, pzt=pzt, c=c, bi=bi, ds_=ds_, rhs=rhs: e.matmul(pzt[:, :G], wb[:, bi, c, ds_], rhs(), start=(c == 0), stop=(c == 1)),
                             [wb, yt, yfn], [pzt])
                P.op("scalar", lambda e, pgt=pgt, gs=gs: e.activation(gs[:], pgt[:, :G], AF.Sigmoid), [pgt], [gs])
                if i == 0:
                    P.op("vector", lambda e, pzt=pzt, gs=gs, dc=dc: e.tensor_tensor(acc[:, dc, :], pzt[:, :G], gs[:], ALU.mult), [pzt, gs], [acc.sub(dc)])
                else:
                    tf = tmp[i % 2]
                    P.op("vector", lambda e, pzt=pzt, gs=gs, tf=tf: e.tensor_tensor(tf[:], pzt[:, :G], gs[:], ALU.mult), [pzt, gs], [tf])
                    if i < 3:
                        P.op("gpsimd", lambda e, tf=tf, dc=dc: e.tensor_tensor(acc[:, dc, :], acc[:, dc, :], tf[:], ALU.add), [acc.sub(dc), tf], [acc.sub(dc)])
                    else:
                        P.op("gpsimd", lambda e, tf=tf, dc=dc: e.tensor_tensor(sqb[:, dc, :], acc[:, dc, :], tf[:], ALU.add), [acc.sub(dc), tf], [sqb.sub(dc)])
        for d2 in range(8):
            ps = pz[ci[0] % 2]; ci[0] += 1
            for dc in range(8):
                P.op("tensor", lambda e, ps=ps, dc=dc, d2=d2: e.matmul(ps[:, :G], wo[:, dc, d2 * 128:(d2 + 1) * 128], sqb[:, dc, :], start=(dc == 0), stop=(dc == 7)),
                     [wo, sqb], [ps])
            P.op("vector", lambda e, ps=ps, d2=d2: e.scalar_tensor_tensor(xs[:, d2, :], ps[:, :G], vec[:, 24 + d2:25 + d2], xs[:, d2, :], ALU.mult, ALU.add),
                 [ps, vec, xs.sub(d2)], [xs.sub(d2)])
        P.dma("sync", xmT[:, ts].rearrange("(k p) t -> p k t", p=128), xs[:], reads=[xs])
        norm(xs, A2, 32, hT, h2f, None)
        P.dma("gpsimd", h2T[:, ts].rearrange("(k p) t -> p k t", p=128), hT[:], reads=[hT])
        for tt in range(TT):
            for k in range(8):
                P.op("tensor", lambda e, k=k, tt=tt: e.matmul(pl[:], h2f[:, k, tt * 128:(tt + 1) * 128], wr[:, k, :], start=(k == 0), stop=(k == 7)),
                     [h2f, wr], [pl])
            a = ao[tt % 2]
            P.op("vector", lambda e: e.tensor_copy(lg[:], pl[:]), [pl], [lg])
            P.op("vector", lambda e: e.reduce_max(sm[:, 0:1], lg[:], AX.X), [lg], [sm.sub(0)])
            P.op("vector", lambda e: e.tensor_scalar(sm[:, 1:2], sm[:, 0:1], -1.0, None, ALU.mult), [sm.sub(0)], [sm.sub(1)])
            P.op("scalar", lambda e: e.activation(ex[:], lg[:], AF.Exp, bias=sm[:, 1:2], scale=1.0, accum_out=sm[:, 2:3]), [lg, sm.sub(1)], [ex, sm.sub(2)])
            P.op("vector", lambda e: e.reciprocal(sm[:, 3:4], sm[:, 2:3]), [sm.sub(2)], [sm.sub(3)])
            P.op("vector", lambda e, a=a: e.tensor_scalar(a[:], ex[:], sm[:, 3:4], None, ALU.mult), [ex, sm.sub(3)], [a])
            t0 = g * G + tt * 128
            P.dma("sync", aff[t0:t0 + 128, :], a[:], reads=[a])
    P.finalize()
    return nc


FF = 1408
NITER = 26


def build_experts(NI, CAP, NE=4):
    T = NI * 128
    SR = min(128, CAP)
    NSL = CAP // SR
    HS = min(1024, CAP)
    NH = CAP // HS
    SG = min(512, HS)
    NSG = HS // SG
    STH = HS // SR
    CH = min(512, NI * 16)
    NCH = NI * 16 // CH
    IPC = CH // 16
    nc = bass.Bass("TRN2", target_bir_lowering=False)
    D = lambda n, s, dt, k: nc.dram_tensor(n, list(s), dt, kind=k).ap()
    aff_d = D("aff", [128, NI, 16], F32, "ExternalInput")
    h2_d = D("h2", [T, 1024], BF16, "ExternalInput")
    wgate_d = D("wgate", [NE, 1024, FF], F32, "ExternalInput")
    wup_d = D("wup", [NE, 1024, FF], F32, "ExternalInput")
    wdown_d = D("wdown", [NE, FF, 1024], F32, "ExternalInput")
    cst_d = D("cst", [128, 3, 128], BF16, "ExternalInput")
    pos_d = D("pos", [128, NI, 16], I32, "ExternalOutput")
    ye_d = D("ye", [NE, CAP, 1024], BF16, "ExternalOutput")
    xe_d = nc.dram_tensor("xe", [NE * CAP, 1024], BF16, kind="Internal").ap()

    P = Prog(nc)
    aff = P.sb("aff", [128, NI, 16], F32)
    cmpb = P.sb("cmpb", [128, NI, 16], BF16)
    M = P.sb("M", [128, NI, 16], BF16)
    S = P.sb("S", [128, NI, 16], F32)
    W = P.sb("W", [128, NI, 16], F32)
    incl = P.sb("incl", [128, NI, 16], F32)
    zer = P.sb("zer", [128, NI], F32)
    posi = P.sb("posi", [128, NI, 16], I32)
    tau = P.sb("tau", [128, 16], F32)
    mid = P.sb("mid", [128, 16], F32)
    inc = P.sb("inc", [128, 16], F32)
    cntp = P.sb("cntp", [128, 16], BF16)
    cst = P.sb("cst", [128, 3, 128], BF16)
    pc = P.ps("pc", [128, 16])
    pw_ = [P.ps("pw%d" % i, [128, 512]) for i in range(2)]
    ones, ltri, ident = (lambda: cst[:, 0, :]), (lambda: cst[:, 1, :]), (lambda: cst[:, 2, :])

    P.dma("sync", aff[:], aff_d, writes=[aff])
    P.dma("sync", cst[:], cst_d, writes=[cst])
    P.op("vector", lambda e: e.memset(tau[:], 0.0), [], [tau])
    P.op("gpsimd", lambda e: e.memset(zer[:], 0.0), [], [zer])
    for it in range(NITER):
        step = 2.0 ** -(it + 1)
        P.op("vector", lambda e, step=step: e.tensor_scalar(mid[:], tau[:], step, None, ALU.add), [tau], [mid])
        P.op("vector", lambda e: e.tensor_tensor(cmpb[:], aff[:], mid[:].unsqueeze(1).to_broadcast([128, NI, 16]), ALU.is_ge), [aff, mid], [cmpb])
        def red(e):
            with nc.allow_low_precision(reason="exact small integer counts"):
                return e.tensor_reduce(cntp[:], cmpb[:].rearrange("p i e -> p e i"), AX.X, ALU.add)
        P.op("vector", red, [cmpb], [cntp])
        P.op("tensor", lambda e: e.matmul(pc[:], ones(), cntp[:], start=True, stop=True), [cst, cntp], [pc])
        P.op("vector", lambda e, step=step: e.tensor_scalar(inc[:], pc[:], CAP - 0.5, step, ALU.is_ge, ALU.mult), [pc], [inc])
        P.op("vector", lambda e: e.tensor_tensor(tau[:], tau[:], inc[:], ALU.add), [tau, inc], [tau])
    P.op("vector", lambda e: e.tensor_tensor(M[:], aff[:], tau[:].unsqueeze(1).to_broadcast([128, NI, 16]), ALU.is_ge), [aff, tau], [M])
    Mf = lambda c: M[:].rearrange("p i e -> p (i e)")[:, c * CH:(c + 1) * CH]
    for c in range(NCH):
        a, b_ = pw_[0], pw_[1]
        P.op("tensor", lambda e, c=c: e.matmul(a[:, :CH], ltri(), Mf(c), start=True, stop=True), [cst, M], [a])
        P.op("tensor", lambda e, c=c: e.matmul(b_[:, :CH], ones(), Mf(c), start=True, stop=True), [cst, M], [b_])
        P.op("vector", lambda e, c=c: e.tensor_copy(W[:].rearrange("p i e -> p (i e)")[:, c * CH:(c + 1) * CH], a[:, :CH]), [a], [W.sub(c)])
        P.op("scalar", lambda e, c=c: e.copy(S[:].rearrange("p i e -> p (i e)")[:, c * CH:(c + 1) * CH], b_[:, :CH]), [b_], [S.sub(c)])
    for ex in range(16):
        P.op("vector", lambda e, ex=ex: e.tensor_tensor_scan(incl[:, :, ex], S[:, :, ex], zer[:], 0.0, ALU.add, ALU.add), [S, zer], [incl.sub(ex)])
    P.op("vector", lambda e: e.tensor_tensor(W[:], W[:], incl[:], ALU.add), [W, incl], [W])
    P.op("vector", lambda e: e.tensor_tensor(W[:], W[:], S[:], ALU.subtract), [W, S], [W])
    BIG = float(2 ** 20)
    P.op("vector", lambda e: e.tensor_scalar(W[:], W[:], -BIG, None, ALU.add), [W], [W])
    P.op("vector", lambda e: e.tensor_tensor(W[:], W[:], M[:], ALU.mult), [W, M], [W])
    P.op("vector", lambda e: e.tensor_scalar(W[:], W[:], BIG, None, ALU.add), [W], [W])
    P.op("vector", lambda e: e.tensor_copy(posi[:], W[:]), [W], [posi])
    P.dma("sync", pos_d, posi[:], reads=[posi])
    padj = P.sb("padj", [128, NI, NE], I32)
    for e_ in range(NE):
        P.op("vector", lambda e, e_=e_: e.tensor_scalar(padj[:, :, e_], W[:, :, e_], float(e_ * CAP), None, ALU.add), [W], [padj.sub(e_)])

    return nc, P, dict(pw_=pw_, posi=padj, h2_d=h2_d, xe_d=xe_d, ye_d=ye_d, wgate_d=wgate_d, wup_d=wup_d, wdown_d=wdown_d, cst=cst, ident=ident,
                       NI=NI, CAP=CAP, NE=NE, SR=SR, NSL=NSL, HS=HS, NH=NH, SG=SG, NSG=NSG, STH=STH)


def emit_expert_ffn(nc, P, d, e_off):
    NI, CAP, NE, SR, NSL, HS, NH, SG, NSG, STH = (d[k] for k in ("NI", "CAP", "NE", "SR", "NSL", "HS", "NH", "SG", "NSG", "STH"))
    posi, h2_d, xe_d, ye_d, cst, ident = d["posi"], d["h2_d"], d["xe_d"], d["ye_d"], d["cst"], d["ident"]
    h2t = [P.sb("h2t%d" % i, [128, 1024], BF16) for i in range(3)]
    xe_tm = P.sb("xe_tm", [128, STH, 1024], BF16)
    xeT = P.sb("xeT", [128, 8, HS], BF16)
    hidT = P.sb("hidT", [128, 11, HS], BF16)
    wgu = P.sb("wgu", [128, 8, 2 * FF], BF16)
    wd = P.sb("wd", [128, 11, 1024], BF16)
    stg = [P.sb("stg%d" % i, [128, FF], F32) for i in range(3)]
    sg_ = [P.sb("sg%d" % i, [128, SG], BF16) for i in range(2)]
    yo = [P.sb("yo%d" % i, [128, 1024], BF16) for i in range(2)]
    ptr = [P.ps("ptr%d" % i, [128, 4, 128], BF16) for i in range(2)]
    pga = d["pw_"]
    pup = [P.ps("pup%d" % i, [128, 512]) for i in range(2)]
    xeB = Buf("xe_dram")
    breg = {}
    def mkreg(e):
        breg["r"] = e.alloc_register("bchk")
        return e.reg_mov(breg["r"], NE * CAP - 1)
    P.op("gpsimd", mkreg, [], [])
    for i in range(NI):
        ht = h2t[i % 3]
        P.dma("sync", ht[:], h2_d[i * 128:(i + 1) * 128, :], writes=[ht])
        for e_ in range(NE):
            P.op("gpsimd", lambda e, ht=ht, i=i, e_=e_: e.indirect_dma_start(
                out=xe_d, out_offset=bass.IndirectOffsetOnAxis(ap=posi[:, i, e_off + e_:e_off + e_ + 1], axis=0),
                in_=ht[:], in_offset=None, bounds_check=breg["r"], oob_is_err=False), [ht, posi], [xeB.sub((e_, i))], dma=True)
    ce = [0]
    def cast(dst, src, rd, wr_, psum=False):
        eng = ("vector", "scalar")[ce[0] % 2] if psum else ("vector", "gpsimd", "scalar")[ce[0] % 3]
        ce[0] += 1
        if eng == "scalar":
            P.op("scalar", lambda e: e.copy(dst, src), rd, wr_)
        else:
            P.op(eng, lambda e: e.tensor_copy(dst, src), rd, wr_)
    si = [0]
    ci = [0]
    for e_ in range(NE):
        for k in range(8):
            for gu, wsrc in enumerate((d["wgate_d"], d["wup_d"])):
                s = stg[si[0] % 3]; si[0] += 1
                P.dma("sync", s[:, 0:FF], wsrc[e_, k * 128:(k + 1) * 128, :], writes=[s])
                cast(wgu[:, k, gu * FF:(gu + 1) * FF], s[:, :], [s], [wgu.sub((k, gu))])
        for f in range(11):
            s = stg[si[0] % 3]; si[0] += 1
            P.dma("sync", s[:, 0:1024], d["wdown_d"][e_, f * 128:(f + 1) * 128, :], writes=[s])
            cast(wd[:, f, :], s[:, 0:1024], [s], [wd.sub(f)])
        for hh in range(NH):
            P.dma("gpsimd", xe_tm[:SR], xe_d[e_ * CAP + hh * HS:e_ * CAP + (hh + 1) * HS, :].rearrange("(s p) d -> p s d", p=SR), reads=[xeB.sub((e_, i)) for i in range(NI)], writes=[xe_tm])
            for st in range(STH):
                for kq in range(2):
                    pt = ptr[ci[0] % 2]; ci[0] += 1
                    for kk in range(4):
                        k = kq * 4 + kk
                        P.op("tensor", lambda e, pt=pt, kk=kk, k=k, st=st: e.transpose(pt[:, kk, :SR], xe_tm[:SR, st, k * 128:(k + 1) * 128], cst[:SR, 2, :SR]),
                             [xe_tm, cst], [pt])
                    cast(xeT[:, kq * 4:(kq + 1) * 4, st * SR:(st + 1) * SR], pt[:, :, :SR], [pt], [xeT.sub((st, kq))], psum=True)
            for f in range(11):
                for sgi in range(NSG):
                    ss = slice(sgi * SG, (sgi + 1) * SG)
                    a, b_ = pga[ci[0] % 2], pup[ci[0] % 2]; sgt = sg_[ci[0] % 2]; ci[0] += 1
                    for k in range(8):
                        P.op("tensor", lambda e, a=a, k=k, f=f, ss=ss: e.matmul(a[:, :SG], wgu[:, k, f * 128:(f + 1) * 128], xeT[:, k, ss], start=(k == 0), stop=(k == 7)),
                             [wgu, xeT], [a])
                    for k in range(8):
                        P.op("tensor", lambda e, b_=b_, k=k, f=f, ss=ss: e.matmul(b_[:, :SG], wgu[:, k, FF + f * 128:FF + (f + 1) * 128], xeT[:, k, ss], start=(k == 0), stop=(k == 7)),
                             [wgu, xeT], [b_])
                    P.op("scalar", lambda e, a=a, sgt=sgt: e.activation(sgt[:], a[:, :SG], AF.Silu), [a], [sgt])
                    P.op("vector", lambda e, b_=b_, sgt=sgt, f=f, ss=ss: e.tensor_tensor(hidT[:, f, ss], b_[:, :SG], sgt[:], ALU.mult), [b_, sgt], [hidT.sub((f, sgi))])
            for st in range(STH):
                y = yo[st % 2]
                for dh in range(2):
                    a = pga[ci[0] % 2]; ci[0] += 1
                    for f in range(11):
                        P.op("tensor", lambda e, a=a, f=f, st=st, dh=dh: e.matmul(a[:SR, :], hidT[:, f, st * SR:(st + 1) * SR], wd[:, f, dh * 512:(dh + 1) * 512], start=(f == 0), stop=(f == 10)),
                             [hidT, wd], [a])
                    if dh == 0:
                        P.op("scalar", lambda e, a=a, y=y: e.copy(y[:SR, 0:512], a[:SR, :]), [a], [y.sub(0)])
                    else:
                        P.op("vector", lambda e, a=a, y=y: e.tensor_copy(y[:SR, 512:1024], a[:SR, :]), [a], [y.sub(1)])
                r0 = hh * HS + st * SR
                P.dma("sync", ye_d[e_, r0:r0 + SR, :], y[:SR, :], reads=[y])


def build_experts_full(NI, CAP, NE=4):
    nc, P, d = build_experts(NI, CAP, NE)
    emit_expert_ffn(nc, P, d, 0)
    P.finalize()
    return nc


def build_combine(ntok, NEXP_CAP):
    NT = ntok // 128
    nc = bass.Bass("TRN2", target_bir_lowering=False)
    D = lambda n, s, dt, k: nc.dram_tensor(n, list(s), dt, kind=k).ap()
    xm_d = D("xm", [ntok, 1024], F32, "ExternalInput")
    pos_d = D("pos", [128, NT, 16], I32, "ExternalInput")
    aff_d = D("aff", [128, NT, 16], F32, "ExternalInput")
    eoff_d = D("eoff", [128, 16], F32, "ExternalInput")
    gt2_d = D("gt2", [128, 1024], F32, "ExternalInput")
    ye_d = D("ye", [NEXP_CAP, 1024], BF16, "ExternalInput")
    xo_d = D("xo", [ntok, 1024], F32, "ExternalOutput")
    P = Prog(nc)
    posi = P.sb("posi", [128, NT, 16], I32)
    posf = P.sb("posf", [128, NT, 16], F32)
    idx = P.sb("idx", [128, NT, 16], I32)
    aff = P.sb("aff", [128, NT, 16], F32)
    eoff = P.sb("eoff", [128, 16], F32)
    gt2 = P.sb("gt2", [128, 1024], F32)
    xm = [P.sb("xm%d" % i, [128, 1024], F32) for i in range(2)]
    acc = [P.sb("acc%d" % i, [128, 1024], F32) for i in range(2)]
    R = [P.sb("R%d" % i, [128, 1024], BF16) for i in range(6)]
    P.dma("sync", posi[:], pos_d, writes=[posi])
    P.dma("sync", aff[:], aff_d, writes=[aff])
    P.dma("sync", eoff[:], eoff_d, writes=[eoff])
    P.dma("sync", gt2[:], gt2_d, writes=[gt2])
    P.op("vector", lambda e: e.tensor_copy(posf[:], posi[:]), [posi], [posf])
    P.op("vector", lambda e: e.tensor_tensor(posf[:], posf[:], eoff[:].unsqueeze(1).to_broadcast([128, NT, 16]), ALU.add), [posf, eoff], [posf])
    P.op("vector", lambda e: e.tensor_copy(idx[:], posf[:]), [posf], [idx])
    ri = 0
    breg = {}
    def mkreg(e):
        breg["r"] = e.alloc_register("bchk")
        return e.reg_mov(breg["r"], NEXP_CAP - 1)
    P.op("gpsimd", mkreg, [], [])
    for i in range(NT):
        x = xm[i % 2]; a = acc[i % 2]
        P.dma("sync", x[:], xm_d[i * 128:(i + 1) * 128, :], writes=[x])
        for ex in range(16):
            r = R[ri % 6]; ri += 1
            P.op("gpsimd", lambda e, r=r: e.memset(r[:], 0.0), [], [r])
            P.op("gpsimd", lambda e, r=r, i=i, ex=ex: e.indirect_dma_start(
                out=r[:], out_offset=None, in_=ye_d, in_offset=bass.IndirectOffsetOnAxis(ap=idx[:, i, ex:ex + 1], axis=0),
                bounds_check=breg["r"], oob_is_err=False), [idx], [r], dma=True)
            if ex == 0:
                P.op("vector", lambda e, r=r, a=a, i=i, ex=ex: e.tensor_scalar(a[:], r[:], aff[:, i, ex:ex + 1], None, ALU.mult), [r, aff], [a])
            else:
                P.op("vector", lambda e, r=r, a=a, i=i, ex=ex: e.scalar_tensor_tensor(a[:], r[:], aff[:, i, ex:ex + 1], a[:], ALU.mult, ALU.add), [r, aff, a], [a])
        P.op("vector", lambda e, a=a: e.tensor_tensor(a[:], a[:], gt2[:], ALU.mult), [a, gt2], [a])
        P.op("gpsimd", lambda e, a=a, x=x: e.tensor_tensor(a[:], a[:], x[:], ALU.add), [a, x], [a])
        P.dma("sync", xo_d[i * 128:(i + 1) * 128, :], a[:], reads=[a])
    P.finalize()
    return nc


def build_ada(ncols):
    NCH = ncols // 128
    nc = bass.Bass("TRN2", target_bir_lowering=False)
    D = lambda n, s, dt, k: nc.dram_tensor(n, list(s), dt, kind=k).ap()
    wa_d = D("wa", [1024, ncols], F32, "ExternalInput")
    ba_d = D("ba", [128, NCH], F32, "ExternalInput")
    cv_d = D("cv", [128, 8, 3], F32, "ExternalInput")
    out_d = D("modT", [128, NCH, 3], F32, "ExternalOutput")
    P = Prog(nc)
    wa = P.sb("wa", [128, 8, ncols], F32)
    ba = P.sb("ba", [128, NCH], F32)
    cv = P.sb("cv", [128, 8, 3], F32)
    sv = P.sb("sv", [128, 8, 3], F32)
    o = P.sb("o", [128, NCH, 3], F32)
    pm = [P.ps("pm%d" % i, [128, 4]) for i in range(2)]
    for k in range(8):
        P.dma(("sync", "gpsimd")[k % 2], wa[:, k, :], wa_d[k * 128:(k + 1) * 128, :], writes=[wa.sub(k)])
    P.dma("sync", ba[:], ba_d, writes=[ba])
    P.dma("sync", cv[:], cv_d, writes=[cv])
    P.op("scalar", lambda e: e.activation(sv[:], cv[:], AF.Silu), [cv], [sv])
    for j in range(NCH):
        p = pm[j % 2]
        for k in range(8):
            P.op("tensor", lambda e, p=p, k=k, j=j: e.matmul(p[:, 0:3], wa[:, k, j * 128:(j + 1) * 128], sv[:, k, :], start=(k == 0), stop=(k == 7)), [wa, sv], [p])
        P.op("vector", lambda e, p=p, j=j: e.tensor_scalar(o[:, j, :], p[:, 0:3], ba[:, j:j + 1], None, ALU.add), [p, ba], [o.sub(j)])
    P.dma("sync", out_d, o[:], reads=[o])
    P.finalize()
    return nc

import math
import numpy as np

_PROGS = {}


def _prog(key, fn):
    if key not in _PROGS:
        _PROGS[key] = fn()
    return _PROGS[key]


def _run(nc, maps):
    return run_bass_kernel_spmd(nc, maps, core_ids=list(range(8))).results


def _c(a):
    return np.ascontiguousarray(a)


def _lay_pie(a, ni):
    return _c(a.reshape(ni, 128, 16).transpose(1, 0, 2))


def _expert_csts():
    c = np.zeros((128, 3, 128), np.float32)
    c[:, 0] = 1.0
    c[:, 1] = np.triu(np.ones((128, 128), np.float32), 1)
    c[:, 2] = np.eye(128, dtype=np.float32)
    return c.astype(BF)


def _fft256_consts():
    n = np.arange(256, dtype=np.float64)
    ang = 2 * np.pi * np.outer(n, n) / 256.0
    cn = np.stack([np.cos(ang) / 16.0, np.sin(ang) / 16.0], axis=0)
    return _c(cn.reshape(2, 2, 128, 256).transpose(2, 0, 1, 3)).astype(BF)


def kernel(x, c, ctx, c_ctx, w_ada, b_ada, g_mix, g_ffn, w_in, na_q_g, na_k_g, na_rpb, df_q_g, df_k_g, df_lambda,
           df_subln_g, pool_w, pool_scale, fnet_w, w_branch, w_out, w_router, w_gate_e, w_up_e, w_down_e, _dbg=None):
    f32 = np.float32
    x = np.asarray(x, f32); ctx = np.asarray(ctx, f32)
    B, T, Dm = x.shape
    TC = ctx.shape[1]
    L = w_in.shape[0]
    dbg = _dbg or (lambda *a, **k: None)
    ident_f = np.eye(128, dtype=f32)
    ones_bf = np.ones((128, 128), BF)
    cm_ = cmats(); dftm_ = dft_ch(); fcm, ftw = fft_consts(); ecst = _expert_csts(); cn256 = _fft256_consts()

    c3 = np.stack([c[0], c[1], c_ctx], axis=1).astype(f32)
    cv = _c(c3.reshape(8, 128, 3).transpose(1, 0, 2))
    wall = np.concatenate([w_ada[l] for l in range(L)], axis=1)
    ball = np.concatenate([b_ada[l] for l in range(L)])
    ncol = wall.shape[1] // 8
    nc = _prog(("ada", ncol), lambda: build_ada(ncol))
    res = _run(nc, [dict(wa=_c(wall[:, k * ncol:(k + 1) * ncol]), ba=_c(ball[k * ncol:(k + 1) * ncol].reshape(ncol // 128, 128).T), cv=cv)
                    for k in range(8)])
    modT = np.concatenate([r["modT"].transpose(1, 0, 2).reshape(ncol, 3) for r in res], axis=0)
    mods = [modT[l * 6144:(l + 1) * 6144].T.copy() for l in range(L)]
    dbg("mod", mods)

    def seg(m, j):
        return m[j * 1024:(j + 1) * 1024]

    for l in range(L):
        last = l == L - 1
        lam_init = 0.8 - 0.6 * math.exp(-0.3 * l)
        gq = _c(np.stack([np.tile(na_q_g[l], 2), np.tile(df_q_g[l], 4), np.tile(na_k_g[l], 2), np.tile(df_k_g[l], 4)], axis=1).astype(f32))
        wc = _c(np.concatenate([w_in[l][:, 0:1024], w_in[l][:, 5120:6144]], axis=1))
        lamv = np.tile(df_lambda[l].reshape(1, 128), (128, 1)).astype(f32)
        gsub = np.tile(df_subln_g[l][None], (128, 1)).astype(f32)
        cst2 = np.tile(np.array([[lam_init, 1 - lam_init]], f32), (128, 1))

        def proj(tok_arrays, modrows, tposs, rope_on, ntok, G):
            nc = _prog(("proj", ntok, G), lambda: build_proj(ntok, G))
            maps = []
            for k in range(8):
                m = modrows[k]
                vec = np.concatenate([fm8(seg(m, 0)), fm8(seg(m, 1)), fm8(g_mix[l])], axis=1)
                maps.append(dict(xT=_c(tok_arrays[k].T), w_in=wc, vecs=vec, gq=gq, cmats=cm_, cossin=cossin_table(tposs[k], rope_on), dftm=dftm_))
            return _run(nc, maps)
        NQ = T // 4
        rl_ = proj([x[k // 4, (k % 4) * NQ:(k % 4 + 1) * NQ] for k in range(8)], [mods[l][k // 4] for k in range(8)],
                   [np.arange((k % 4) * NQ, (k % 4 + 1) * NQ) for k in range(8)], True, NQ, 512)
        CQ = TC // 4
        rc_ = proj([ctx[k // 4, (k % 4) * CQ:(k % 4 + 1) * CQ] for k in range(8)], [mods[l][2]] * 8,
                   [np.arange(CQ)] * 8, False, CQ, CQ)
        qkT = [np.concatenate([rl_[b * 4 + q]["qkT"] for q in range(4)], axis=1) for b in range(B)]
        tm = [np.concatenate([rl_[b * 4 + q]["tm"] for q in range(4)], axis=0) for b in range(B)]
        cqkT = [np.concatenate([rc_[b * 4 + q]["qkT"] for q in range(4)], axis=1) for b in range(B)]
        ctm = [np.concatenate([rc_[b * 4 + q]["tm"] for q in range(4)], axis=0) for b in range(B)]
        dbg("proj", l, qkT, tm, cqkT, ctm)

        nc = _prog("na", build_na)
        maps = []
        for k in range(8):
            b, q = k // 4, k % 4
            d = na_core_inputs(na_rpb[l], q * 64, qkT[b][512:768], tm[b][:, 768:1024])
            d.update(qT=_c(qkT[b][0:256, q * NQ:(q + 1) * NQ]), kcT=_c(cqkT[b][512:768]),
                     vc=_c(ctm[b][:, 768:1024].reshape(2, 128, 256).transpose(1, 0, 2)), ident=ident_f)
            maps.append(d)
        r = _run(nc, maps)
        y_na = [np.concatenate([r[b * 4 + q]["y"] for q in range(4)], axis=0) for b in range(B)]

        TK = T + TC
        nc = _prog(("fattn", T, TK, 2, 32, True), lambda: build_fattn(T, TK, 2, 32, True))
        maps = []
        for k in range(8):
            b, h = k // 4, k % 4
            hs = slice(h * 64, (h + 1) * 64)
            kT = np.concatenate([qkT[b][768:1024][hs], cqkT[b][768:1024][hs]], axis=1)
            v = np.concatenate([tm[b][:, 1024:1280][:, hs], ctm[b][:, 1024:1280][:, hs]], axis=0)
            maps.append(dict(qT=_c(qkT[b][256:512][hs]), kT=_c(kT), v=_c(v.reshape(TK // 128, 128, 64).transpose(1, 0, 2)),
                             lamv=lamv, gsub=gsub, cst=cst2, ident=ident_f))
        r = _run(nc, maps)
        y_df = [np.concatenate([r[b * 4 + h]["y"] for h in range(4)], axis=1) for b in range(B)]

        nc = _prog("fft", build_fft)
        maps = []
        for k in range(8):
            b, cg = k // 4, k % 4
            Z = tm[b][:, 256:768]
            maps.append(dict(z=fft_z_layout(Z[:, cg * 64:(cg + 1) * 64], Z[:, 256 + cg * 64:256 + (cg + 1) * 64]), cm=fcm, tw=ftw))
        r = _run(nc, maps)
        fT = [np.concatenate([r[b * 4 + cg]["fT"] for cg in range(4)], axis=0) for b in range(B)]
        dbg("mix", l, y_na, y_df, fT)

        if not last:
            nc = _prog(("fattn", TC, TC, 1, 64, False), lambda: build_fattn(TC, TC, 1, 64, False))
            maps = []
            for k in range(8):
                b, h = k // 4, k % 4
                hs = slice(h * 64, (h + 1) * 64)
                maps.append(dict(qT=_c(cqkT[b][0:256][hs]), kT=_c(cqkT[b][512:768][hs]),
                                 v=_c(ctm[b][:, 768:1024][:, hs].reshape(TC // 128, 128, 64).transpose(1, 0, 2)),
                                 lamv=lamv, gsub=gsub, cst=cst2, ident=ident_f))
            r = _run(nc, maps)
            yc_na = [np.concatenate([r[b * 4 + h]["y"] for h in range(4)], axis=1) for b in range(B)]
            nc = _prog(("fattn", TC, TC, 2, 32, True), lambda: build_fattn(TC, TC, 2, 32, True))
            maps = []
            for k in range(8):
                b, h = k // 4, k % 4
                hs = slice(h * 64, (h + 1) * 64)
                maps.append(dict(qT=_c(cqkT[b][256:512][hs]), kT=_c(cqkT[b][768:1024][hs]),
                                 v=_c(ctm[b][:, 1024:1280][:, hs].reshape(TC // 128, 128, 64).transpose(1, 0, 2)),
                                 lamv=lamv, gsub=gsub, cst=cst2, ident=ident_f))
            r = _run(nc, maps)
            yc_df = [np.concatenate([r[b * 4 + h]["y"] for h in range(4)], axis=1) for b in range(B)]
            nc = _prog("fft256", build_fft256)
            maps = []
            for k in range(8):
                b, cg = k // 4, k % 4
                Z = ctm[b][:, 256:768]
                lay = lambda a: a.reshape(2, 128, 64).transpose(1, 0, 2)
                z = _c(np.stack([lay(Z[:, cg * 64:(cg + 1) * 64]), lay(Z[:, 256 + cg * 64:256 + (cg + 1) * 64])], axis=1))
                maps.append(dict(z=z, cn=cn256))
            r = _run(nc, maps)
            fcT = [np.concatenate([r[b * 4 + cg]["fT"] for cg in range(4)], axis=0) for b in range(B)]
            dbg("cmix", l, yc_na, yc_df, fcT)

        def merge(xs_tok, modrows, yTs, us, bands, ntok, G):
            nc = _prog(("merge", ntok, G), lambda: build_merge(ntok, G))
            maps = []
            for k in range(8):
                m = modrows[k]
                vec = np.concatenate([fm8(seg(m, 0)), fm8(seg(m, 1)), fm8(g_mix[l]), fm8(seg(m, 2)),
                                      fm8(seg(m, 3)), fm8(seg(m, 4)), fm8(g_ffn[l])], axis=1)
                maps.append(dict(xT=_c(xs_tok[k].T), vecs=vec, wg=_c(w_in[l][:, 1024:5120]), wbr=w_branch[l], wo=w_out[l], wr=w_router[l],
                                 fw=fnet_w[l], pw=pool_w[l], psc=_c(pool_scale[l].reshape(4, 64).T), band=bands[k], ones=ones_bf,
                                 yT=yTs[k], u=us[k]))
            return _run(nc, maps)
        sl = lambda q: slice(q * NQ, (q + 1) * NQ)
        r = merge([x[k // 4, sl(k % 4)] for k in range(8)], [mods[l][k // 4] for k in range(8)],
                  [_c(np.stack([y_na[k // 4][sl(k % 4)].T, y_df[k // 4][sl(k % 4)].T, fT[k // 4][:, sl(k % 4)]])) for k in range(8)],
                  [pool_u_layout(tm[k // 4][:, 0:256], (k % 4) * NQ, NQ) for k in range(8)],
                  [pool_bands_for_core((k % 4) * (NQ // 128), NQ // 128, T) for k in range(8)], NQ, 512)
        x_mid = [np.concatenate([r[b * 4 + q]["xmT"].T for q in range(4)], axis=0) for b in range(B)]
        h2 = [np.concatenate([r[b * 4 + q]["h2T"].T for q in range(4)], axis=0) for b in range(B)]
        aff = [np.concatenate([r[b * 4 + q]["aff"] for q in range(4)], axis=0) for b in range(B)]
        dbg("merge", l, x_mid, aff)
        if not last:
            cidx = [(k % 4) // 2 for k in range(8)], [(k % 4) % 2 for k in range(8)]
            csl = lambda t: slice(t * 128, (t + 1) * 128)
            r = merge([ctx[cidx[0][k], csl(cidx[1][k])] for k in range(8)], [mods[l][2]] * 8,
                      [_c(np.stack([yc_na[cidx[0][k]][csl(cidx[1][k])].T, yc_df[cidx[0][k]][csl(cidx[1][k])].T, fcT[cidx[0][k]][:, csl(cidx[1][k])]])) for k in range(8)],
                      [pool_u_layout(ctm[cidx[0][k]][:, 0:256], cidx[1][k] * 128, 128) for k in range(8)],
                      [pool_bands_for_core(cidx[1][k], 1, TC) for k in range(8)], 128, 128)
            c_mid = [np.concatenate([r[b * 2 + t]["xmT"].T for t in range(2)], axis=0) for b in range(B)]
            c_h2 = [np.concatenate([r[b * 2 + t]["h2T"].T for t in range(2)], axis=0) for b in range(B)]
            c_aff = [np.concatenate([r[b * 2 + t]["aff"] for t in range(2)], axis=0) for b in range(B)]
            dbg("cmerge", l, c_mid, c_aff)

        def experts(affs, h2s, NI, CAP):
            nc = _prog(("exp", NI, CAP), lambda: build_experts_full(NI, CAP))
            maps = []
            for k in range(8):
                b, eg = k // 4, k % 4
                perm = np.roll(np.arange(16), -4 * eg)
                es = slice(4 * eg, 4 * eg + 4)
                maps.append(dict(aff=_lay_pie(affs[b][:, perm], NI), h2=_c(h2s[b]), wgate=_c(w_gate_e[l][es]), wup=_c(w_up_e[l][es]),
                                 wdown=_c(w_down_e[l][es]), cst=ecst))
            r = _run(nc, maps)
            pos = [r[b * 4]["pos"] for b in range(B)]
            ye = [np.concatenate([r[b * 4 + eg]["ye"].reshape(4 * CAP, 1024) for eg in range(4)], axis=0) for b in range(B)]
            return pos, ye
        CAP = max(1, 2 * T // 16)
        pos, ye = experts(aff, h2, T // 128, CAP)
        dbg("experts", l, pos, ye)

        def combine(xms, poss, affs, yes, gt2s, ntok, CAPx):
            nc = _prog(("comb", ntok, CAPx), lambda: build_combine(ntok, 16 * CAPx))
            eoff = np.tile((np.arange(16) * CAPx).astype(f32)[None], (128, 1))
            maps = [dict(xm=_c(xms[k]), pos=_c(poss[k]), aff=affs[k], eoff=eoff, gt2=np.tile(gt2s[k][None].astype(f32), (128, 1)), ye=yes[k])
                    for k in range(8)]
            return _run(nc, maps)
        nt = NQ // 128
        r = combine([x_mid[k // 4][sl(k % 4)] for k in range(8)], [pos[k // 4][:, (k % 4) * nt:(k % 4 + 1) * nt, :] for k in range(8)],
                    [_lay_pie(aff[k // 4][sl(k % 4)], nt) for k in range(8)], [ye[k // 4] for k in range(8)],
                    [seg(mods[l][k // 4], 5) for k in range(8)], NQ, CAP)
        x = np.stack([np.concatenate([r[b * 4 + q]["xo"] for q in range(4)], axis=0) for b in range(B)])
        dbg("xout", l, x)
        if not last:
            CAPc = max(1, 2 * TC // 16)
            cpos, cye = experts(c_aff, c_h2, TC // 128, CAPc)
            r = combine([c_mid[cidx[0][k]][csl(cidx[1][k])] for k in range(8)], [cpos[cidx[0][k]][:, cidx[1][k]:cidx[1][k] + 1, :] for k in range(8)],
                        [_lay_pie(c_aff[cidx[0][k]][csl(cidx[1][k])], 1) for k in range(8)], [cye[cidx[0][k]] for k in range(8)],
                        [seg(mods[l][2], 5)] * 8, 128, CAPc)
            ctx = np.stack([np.concatenate([r[b * 2 + t]["xo"] for t in range(2)], axis=0) for b in range(B)])
            dbg("ctxout", l, ctx)
    return x.astype(np.float32)
```

```python
import numpy as np
from contextlib import ExitStack
import concourse.bass as bass
import concourse.mybir as mybir
from concourse.bass_utils import run_bass_kernel_spmd

F32 = mybir.dt.float32
BF16 = mybir.dt.bfloat16
I32 = mybir.dt.int32
AF = mybir.ActivationFunctionType
ALU = mybir.AluOpType
AX = mybir.AxisListType

ENGS = ["tensor", "vector", "scalar", "gpsimd", "sync"]
DMA_RING = 6


class Buf:
    _n = 0

    def __init__(self, name, t=None):
        Buf._n += 1
        self.id = Buf._n
        self.name = name
        self.t = t

    def __getitem__(self, idx):
        return self.t[idx]

    def sub(self, s):
        return (self, s)


class Op:
    __slots__ = ("eng", "fn", "reads", "writes", "dma", "idx", "sig", "waits", "need_sig", "prewait")

    def __init__(self, eng, fn, reads, writes, dma):
        self.eng, self.fn, self.reads, self.writes, self.dma = eng, fn, reads, writes, dma
        self.sig = None
        self.waits = []
        self.need_sig = False
        self.prewait = None


def _norm(keys):
    out = []
    for k in keys:
        if isinstance(k, Buf):
            out.append((k.id, None))
        else:
            out.append((k[0].id, k[1]))
    return out


class Prog:
    def __init__(self, nc):
        self.nc = nc
        self.ops = []
        self.stack = ExitStack()

    def sb(self, name, shape, dtype):
        t = self.stack.enter_context(self.nc.sbuf_tensor("sb_" + name, list(shape), dtype))
        return Buf(name, t)

    def ps(self, name, shape, dtype=F32):
        t = self.stack.enter_context(self.nc.psum_tensor("ps_" + name, list(shape), dtype))
        return Buf(name, t)

    def op(self, eng, fn, reads=(), writes=(), dma=False):
        o = Op(eng, fn, _norm(reads), _norm(writes), dma)
        o.idx = len(self.ops)
        self.ops.append(o)
        return o

    def dma(self, eng, out, in_, reads=(), writes=(), **kw):
        return self.op(eng, lambda e: e.dma_start(out=out, in_=in_, **kw), reads, writes, dma=True)

    def finalize(self):
        nc = self.nc
        ops = self.ops
        last_w = {}
        readers = {}
        by_buf = {}

        def related(key):
            b, s = key
            subs = by_buf.setdefault(b, set())
            subs.add(s)
            if s is None:
                return [(b, x) for x in subs]
            return [(b, s), (b, None)] if None in subs else [(b, s)]

        deps = [None] * len(ops)
        for o in ops:
            d = {}
            for k in o.reads:
                for r in related(k):
                    w = last_w.get(r)
                    if w is not None:
                        d[w] = True
            for k in o.writes:
                for r in related(k):
                    w = last_w.get(r)
                    if w is not None:
                        d.setdefault(w, False)
                    for rd in readers.get(r, ()):
                        d.setdefault(rd, False)
            d.pop(o.idx, None)
            deps[o.idx] = d
            for k in o.reads:
                readers.setdefault(k, []).append(o.idx)
            for k in o.writes:
                last_w[k] = o.idx
                readers[k] = []
                if k[1] is None:
                    for s in by_buf.get(k[0], ()):
                        if s is not None:
                            last_w[(k[0], s)] = o.idx
                            readers[(k[0], s)] = []

        for o in ops:
            for j, raw in deps[o.idx].items():
                p = ops[j]
                if p.dma:
                    p.need_sig = True
                elif p.eng != o.eng:
                    p.need_sig = True
                elif p.eng != "tensor" or o.dma:
                    p.need_sig = True
        LIM = 30000
        DLIM = 1800
        sems = {e: [] for e in ENGS}
        rings = {e: [[] for _ in range(DMA_RING)] for e in ("sync", "scalar", "gpsimd")}
        cnt = {e: 0 for e in ENGS}
        dcnt = {e: 0 for e in rings}
        self.nsem = 0
        def newsem(tag):
            self.nsem += 1
            return self.stack.enter_context(nc.semaphore("%s_%d" % (tag, self.nsem)))
        def dsig(e, n):
            r, u = n % DMA_RING, n // DMA_RING
            lst = rings[e][r]
            while len(lst) <= u // DLIM:
                lst.append(newsem("d" + e))
            return (lst[u // DLIM], 16 * (u % DLIM + 1), 16)
        last_dma = {e: {} for e in rings}
        for o in ops:
            if o.dma:
                n = dcnt[o.eng]
                dcnt[o.eng] += 1
                o.prewait = dsig(o.eng, n - DMA_RING)[:2] if n >= DMA_RING else None
                o.sig = dsig(o.eng, n)
                last_dma[o.eng][n % DMA_RING] = o.sig
            elif o.need_sig:
                n = cnt[o.eng]
                cnt[o.eng] += 1
                lst = sems[o.eng]
                while len(lst) <= n // LIM:
                    lst.append(newsem("s" + o.eng))
                o.sig = (lst[n // LIM], n % LIM + 1, 1)
        known = {e: {} for e in ENGS}
        for o in ops:
            kn = known[o.eng]
            need = {}
            if o.prewait is not None:
                need[o.prewait[0]] = o.prewait[1]
            for j, raw in deps[o.idx].items():
                p = ops[j]
                if not p.dma and not o.dma and p.eng == o.eng and p.eng == "tensor":
                    continue
                sem, val, _ = p.sig
                if need.get(sem, 0) < val:
                    need[sem] = val
            for sem, val in need.items():
                if kn.get(sem, 0) < val:
                    kn[sem] = val
                    o.waits.append((sem, val))
        finals = {e: [(sg[0], sg[1]) for sg in last_dma[e].values()] for e in rings}
        self.nsig = dict(cnt)
        self.ndma = dict(dcnt)

        with nc.Block() as block:
            def mk(ename):
                def body(eng):
                    for o in ops:
                        if o.eng != ename:
                            continue
                        for sem, val in o.waits:
                            eng.wait_ge(sem, val)
                        ins = o.fn(eng)
                        if o.sig is not None:
                            ins.then_inc(o.sig[0], o.sig[2])
                    if ename in finals:
                        for sem, val in finals[ename]:
                            eng.wait_ge(sem, val)
                return body
            block.tensor(mk("tensor"))
            block.vector(mk("vector"))
            block.scalar(mk("scalar"))
            block.gpsimd(mk("gpsimd"))
            block.sync(mk("sync"))
        self.stack.close()

import numpy as np, ml_dtypes
BF = ml_dtypes.bfloat16
def fm8(v):
    return np.ascontiguousarray(v.reshape(8, 128).T)
def rope_tables(pos, dim=16, base=10000.0):
    half = dim // 2
    inv = base ** (-np.arange(half, dtype=np.float32) / half)
    ang = pos.astype(np.float32)[:, None] * inv[None, :]
    return np.cos(ang), np.sin(ang)
def cossin_table(tpos, rope_on=True):
    n = len(tpos)
    out = np.zeros((128, 2, n), np.float32)
    if not rope_on:
        out[:, 0, :] = 1.0
        return out
    cr, sr = rope_tables(tpos // 64); cc, sc = rope_tables(tpos % 64)
    for p in range(128):
        i = p % 32
        if i < 16:
            out[p, 0] = cr[:, i % 8]; out[p, 1] = sr[:, i % 8]
        else:
            out[p, 0] = cc[:, i % 8]; out[p, 1] = sc[:, i % 8]
    return out
def cmats():
    m = np.zeros((128, 4, 128), np.float32)
    m[:, 0, :] = 1.0
    for p in range(128):
        m[p, 1, (p // 64) * 64:(p // 64 + 1) * 64] = 1.0
        m[p, 2, (p // 32) * 32:(p // 32 + 1) * 32] = 1.0
    Pm = np.zeros((128, 128), np.float32)
    for blk in range(8):
        o = blk * 16
        for i in range(8):
            Pm[o + i, o + i + 8] = -1.0
            Pm[o + i + 8, o + i] = 1.0
    m[:, 3, :] = Pm.T
    return m
def dft_ch():
    c = np.arange(256)[:, None].astype(np.float64); mm = np.arange(256)[None, :].astype(np.float64)
    ang = 2 * np.pi * c * mm / 256
    return (np.concatenate([np.cos(ang), -np.sin(ang)], axis=1) / 16.0).astype(np.float32)

def na_bias_table(rpb, J):
    kb = int(np.clip(2 * J - 4, 0, 246))
    kr = np.arange(10)[:, None, None, None] + kb
    kc = np.arange(64)[None, :, None, None]
    qr = np.arange(2)[None, None, :, None] + 2 * J
    qc = np.arange(64)[None, None, None, :]
    rs = np.clip(qr - 4, 0, 248); cs = np.clip(qc - 8, 0, 48)
    valid = (kr >= rs) & (kr < rs + 8) & (kc >= cs) & (kc < cs + 16)
    ri = np.clip(kr - qr + 7, 0, 14); cidx = np.clip(kc - qc + 15, 0, 30)
    ri, cidx, valid = np.broadcast_arrays(ri, cidx, valid)
    tab = rpb[:, ri, cidx]
    tab = np.where(valid[None], tab, np.float32(-30000.0)).astype(np.float32)
    return kb, tab.reshape(4, 640, 128)

def na_core_inputs(rpb, R0, kT_full, v_full):
    J0 = R0 // 2
    kT = np.zeros((256, 74 * 64), kT_full.dtype); v = np.zeros((74 * 64, 256), v_full.dtype)
    g0 = R0 - 4
    lo, hi = max(g0, 0), min(g0 + 74, 256)
    kT[:, (lo - g0) * 64:(hi - g0) * 64] = kT_full[:, lo * 64:hi * 64]
    v[(lo - g0) * 64:(hi - g0) * 64] = v_full[lo * 64:hi * 64]
    ks = np.zeros((4, 256, 640), kT_full.dtype); vs = np.zeros((4, 640, 256), v_full.dtype)
    bias = np.zeros((128, 5, 4, 5, 128), np.float32)
    for var, j in enumerate((0, 1, 2, 30, 31)):
        kb, tab = na_bias_table(rpb, J0 + j)
        bias[:, var] = tab.reshape(4, 5, 128, 128).transpose(2, 0, 1, 3)
        if j != 2:
            sp = {0: 0, 1: 1, 30: 2, 31: 3}[j]
            ks[sp] = kT_full[:, kb * 64:(kb + 10) * 64]; vs[sp] = v_full[kb * 64:(kb + 10) * 64]
    v = np.ascontiguousarray(v.reshape(37, 128, 256).transpose(1, 0, 2))
    vs = np.ascontiguousarray(vs.reshape(4, 5, 128, 256).transpose(2, 0, 1, 3))
    return dict(kT=kT, v=v, ks=ks, vs=vs, bias=bias)

def fft_consts():
    i = np.arange(128, dtype=np.float64)
    a1 = 2 * np.pi * np.outer(i, i) / 128.0
    C1, S1 = np.cos(a1), np.sin(a1)
    cm = np.stack([C1.T, S1.T, -S1.T, C1.T / 128.0, S1.T / 128.0], axis=1)
    at = 2 * np.pi * np.outer(i, i) / 16384.0
    tw = np.stack([np.tile(np.cos(at)[:, None, :], (1, 4, 1)), np.tile(np.sin(at)[:, None, :], (1, 4, 1))], axis=1)
    return cm.astype(BF), tw.astype(np.float32)
def fft_z_layout(zr, zi):
    CH = zr.shape[1]
    f = lambda a: a.reshape(128, 128, CH).transpose(0, 2, 1)
    return np.ascontiguousarray(np.stack([f(zr), f(zi)], axis=1))

POOL_WINDOWS = (2, 4, 8, 16)
def pool_band(gt, N):
    out = np.zeros((128, 4, 3, 128), np.float32)
    for gi, w in enumerate(POOL_WINDOWS):
        for to in range(128):
            t = gt * 128 + to
            lo = min(max(t - w // 2, 0), N); hi = min(max(t + w // 2, 0), N)
            cnt = hi - lo
            for ti in range(lo, hi):
                j = ti // 128 - gt + 1
                out[ti % 128, gi, j, to] += 1.0 / cnt
            out[to, gi, 1, to] -= 1.0
    return out
def pool_bands_for_core(gt_first, nt_local, N):
    b = np.zeros((128, 3, 4, 3, 128), np.float32)
    b[:, 0] = pool_band(gt_first, N)
    if nt_local > 1:
        b[:, 1] = pool_band(gt_first + 1, N) if nt_local > 2 else 0
        b[:, 2] = pool_band(gt_first + nt_local - 1, N)
    return b.astype(BF)
def pool_u_layout(u_full, t0, ntok):
    N = u_full.shape[0]
    nt = ntok // 128
    buf = np.zeros(((nt + 2) * 128, 256), u_full.dtype)
    lo, hi = max(t0 - 128, 0), min(t0 + ntok + 128, N)
    buf[lo - (t0 - 128):hi - (t0 - 128)] = u_full[lo:hi]
    return np.ascontiguousarray(buf.reshape(nt + 2, 128, 256).transpose(1, 0, 2))


EPS = 1e-6
C_NAQ, C_DFQ, C_POOL, C_FNET, C_GATE, C_NAK, C_NAV, C_DFK, C_DFV = 0, 256, 512, 768, 1024, 5120, 5376, 5632, 5888


def build_proj(ntok, G):
    NG = ntok // G
    TT = max(1, G // 128)
    TM = min(G, 128)
    nc = bass.Bass("TRN2", target_bir_lowering=False)
    D = lambda n, s, dt, k: nc.dram_tensor(n, list(s), dt, kind=k).ap()
    xT = D("xT", [1024, ntok], F32, "ExternalInput")
    w_in = D("w_in", [1024, 2048], F32, "ExternalInput")
    vecs = D("vecs", [128, 24], F32, "ExternalInput")
    gq = D("gq", [128, 4], F32, "ExternalInput")
    cmats = D("cmats", [128, 4, 128], F32, "ExternalInput")
    cossin = D("cossin", [128, 2, ntok], F32, "ExternalInput")
    dftm = D("dftm", [256, 512], F32, "ExternalInput")
    qkT = D("qkT", [1024, ntok], BF16, "ExternalOutput")
    tm = D("tm", [ntok, 1280], BF16, "ExternalOutput")

    P = Prog(nc)
    hT = P.sb("hT", [128, 8, ntok], BF16)
    xs = [P.sb("xs%d" % i, [128, 8, G], F32) for i in range(2)]
    sq = P.sb("sq", [128, 8, G], BF16)
    tmpf = [P.sb("tmpf%d" % i, [128, G], F32) for i in range(2)]
    rstd = P.sb("rstd", [128, G], F32)
    sd = P.sb("sd", [128, G], F32)
    vec_sb = P.sb("vec_sb", [128, 24], F32)
    A_sb = P.sb("A_sb", [128, 8], F32)
    gq_sb = P.sb("gq_sb", [128, 4], F32)
    cm_f = P.sb("cm_f", [128, 4, 128], F32)
    cm = P.sb("cm", [128, 4, 128], BF16)
    dft_f = P.sb("dft_f", [128, 2, 512], F32)
    dft = P.sb("dft", [128, 2, 512], BF16)
    css = [P.sb("cs%d" % i, [128, 2, G], F32) for i in range(2)]
    wst = xs if G == 512 else [P.sb("wst%d" % i, [128, 8, 512], F32) for i in range(2)]
    wb = [P.sb("wb%d" % i, [128, 8, 512], BF16) for i in range(2)]
    outs = [P.sb("o%d" % i, [128, 512], BF16) for i in range(4)]
    sqh = P.sb("sqh", [128, G], BF16)
    xn = P.sb("xn", [128, G], BF16)
    t1 = P.sb("t1", [128, G], F32)
    t2 = P.sb("t2", [128, G], F32)
    uT = P.sb("uT", [128, 2, G], BF16)
    pm = [P.ps("pm%d" % i, [128, 512]) for i in range(3)]
    pst = P.ps("pst", [128, 512])
    prot = P.ps("prot", [128, 512])
    epsb = P.sb("epsb", [128, 1], F32)

    P.dma("sync", vec_sb[:], vecs, writes=[vec_sb])
    P.dma("sync", gq_sb[:], gq, writes=[gq_sb])
    P.dma("sync", cm_f[:], cmats, writes=[cm_f])
    P.dma("sync", dft_f[:], dftm.rearrange("(k p) n -> p k n", p=128), writes=[dft_f])
    P.op("vector", lambda e: e.tensor_copy(cm[:], cm_f[:]), [cm_f], [cm])
    P.op("vector", lambda e: e.tensor_copy(dft[:], dft_f[:]), [dft_f], [dft])
    P.op("vector", lambda e: e.memset(epsb[:], EPS), [], [epsb])
    P.op("vector", lambda e: e.tensor_scalar(A_sb[:], vec_sb[:, 8:16], 1.0, None, ALU.add), [vec_sb], [A_sb])
    P.op("vector", lambda e: e.tensor_tensor(A_sb[:], A_sb[:], vec_sb[:, 16:24], ALU.mult), [A_sb, vec_sb], [A_sb])

    def load_x(g):
        b = xs[g % 2]
        P.dma("sync", b[:], xT[:, g * G:(g + 1) * G].rearrange("(k p) t -> p k t", p=128), writes=[b])
    load_x(0)
    for g in range(NG):
        if g + 1 < NG:
            load_x(g + 1)
        b = xs[g % 2]
        P.op("scalar", lambda e, b=b: e.activation(sq[:], b[:], AF.Square), [b], [sq])
        for k in range(8):
            P.op("tensor", lambda e, k=k: e.matmul(pst[:, :G], cm[:, 0, :], sq[:, k, :], start=(k == 0), stop=(k == 7)),
                 [cm, sq], [pst])
        P.op("scalar", lambda e: e.activation(sd[:], pst[:, :G], AF.Sqrt, bias=epsb[:], scale=1.0 / 1024), [pst, epsb], [sd])
        P.op("vector", lambda e: e.reciprocal(rstd[:], sd[:]), [sd], [rstd])
        for k in range(8):
            tf = tmpf[k % 2]
            P.op("vector", lambda e, k=k, tf=tf, b=b: e.scalar_tensor_tensor(tf[:], b[:, k, :], A_sb[:, k:k + 1], rstd[:], ALU.mult, ALU.mult),
                 [b, A_sb, rstd], [tf])
            P.op("scalar", lambda e, k=k, tf=tf, g=g: e.activation(hT[:, k, g * G:(g + 1) * G], tf[:], AF.Identity, bias=vec_sb[:, k:k + 1], scale=1.0),
                 [tf, vec_sb], [hT.sub(g)])

    NB = 12
    def load_w(cb):
        st = wst[cb % 2]
        P.dma("sync", st[:], w_in[:, cb * 512:(cb + 1) * 512].rearrange("(k p) n -> p k n", p=128), writes=[st])
    def cast_w(cb):
        st, w = wst[cb % 2], wb[cb % 2]
        for k in range(8):
            eng = ("vector", "gpsimd")[k % 2]
            P.op(eng, lambda e, k=k, st=st, w=w: e.tensor_copy(w[:, k, :], st[:, k, :]), [st], [w.sub(k)])
    CBL = [0, 1, 2, 3]
    load_w(CBL[0])
    cast_w(CBL[0])
    cnt = {"pm": 0, "o": 0, "st": 0, "cs": 0}
    def store(dst, src, rd):
        eng = ("gpsimd", "sync")[cnt["st"] % 2]
        cnt["st"] += 1
        P.dma(eng, dst, src, reads=rd)

    def fm_chunk(w, jj, g, kind, row0):
        ps = pm[cnt["pm"] % 3]; cnt["pm"] += 1
        ts = slice(g * G, (g + 1) * G)
        for k in range(8):
            P.op("tensor", lambda e, k=k, ps=ps: e.matmul(ps[:, :G], w[:, k, jj * 128:(jj + 1) * 128], hT[:, k, ts], start=(k == 0), stop=(k == 7)),
                 [w.sub(k), hT.sub(g)], [ps])
        if kind == "gate":
            o = outs[cnt["o"] % 4]; cnt["o"] += 1
            P.op("scalar", lambda e: e.activation(o[:, :G], ps[:, :G], AF.Sigmoid), [ps], [o])
            store(gT[row0:row0 + 128, ts], o[:, :G], [o])
            return
        if kind == "fnet":
            c = jj % 2
            P.op("scalar", lambda e: e.copy(uT[:, c, :], ps[:, :G]), [ps], [uT.sub(c)])
            return
        hd, ci, gcol = {"naq": (64, 1, 0), "nak": (64, 1, 2), "dfq": (32, 2, 1), "dfk": (32, 2, 3)}[kind]
        P.op("scalar", lambda e: e.activation(sqh[:], ps[:, :G], AF.Square), [ps], [sqh])
        P.op("tensor", lambda e: e.matmul(pst[:, :G], cm[:, ci, :], sqh[:], start=True, stop=True), [cm, sqh], [pst])
        P.op("scalar", lambda e: e.activation(sd[:], pst[:, :G], AF.Sqrt, bias=epsb[:], scale=1.0 / hd), [pst, epsb], [sd])
        P.op("vector", lambda e: e.reciprocal(rstd[:], sd[:]), [sd], [rstd])
        o = outs[cnt["o"] % 4]; cnt["o"] += 1
        if kind in ("naq", "nak"):
            P.op("vector", lambda e: e.scalar_tensor_tensor(o[:, :G], ps[:, :G], gq_sb[:, gcol:gcol + 1], rstd[:], ALU.mult, ALU.mult),
                 [ps, gq_sb, rstd], [o])
        else:
            cs = css[cnt["cs"] % 2]; cnt["cs"] += 1
            P.dma("sync", cs[:], cossin[:, :, ts], writes=[cs])
            P.op("vector", lambda e: e.scalar_tensor_tensor(xn[:], ps[:, :G], gq_sb[:, gcol:gcol + 1], rstd[:], ALU.mult, ALU.mult),
                 [ps, gq_sb, rstd], [xn])
            P.op("tensor", lambda e: e.matmul(prot[:, :G], cm[:, 3, :], xn[:], start=True, stop=True), [cm, xn], [prot])
            P.op("vector", lambda e: e.tensor_tensor(t1[:], xn[:], cs[:, 0, :], ALU.mult), [xn, cs], [t1])
            P.op("vector", lambda e: e.tensor_tensor(t2[:], prot[:, :G], cs[:, 1, :], ALU.mult), [prot, cs], [t2])
            P.op("gpsimd", lambda e: e.tensor_tensor(o[:, :G], t1[:], t2[:], ALU.add), [t1, t2], [o])
        store(qkT[row0:row0 + 128, ts], o[:, :G], [o])

    def tm_block(w, c0, ncols, g, dcol):
        for tt in range(TT):
            ps = pm[cnt["pm"] % 3]; cnt["pm"] += 1
            t0 = g * G + tt * TM
            for k in range(8):
                P.op("tensor", lambda e, k=k, ps=ps, t0=t0: e.matmul(ps[:TM, :ncols], hT[:, k, t0:t0 + TM], w[:, k, c0:c0 + ncols], start=(k == 0), stop=(k == 7)),
                     [w.sub(k), hT.sub(g)], [ps])
            o = outs[cnt["o"] % 4]; cnt["o"] += 1
            P.op("vector", lambda e, ps=ps, o=o: e.tensor_copy(o[:TM, :ncols], ps[:TM, :ncols]), [ps], [o])
            store(tm[t0:t0 + TM, dcol:dcol + ncols], o[:TM, :ncols], [o])

    def fft_ab(g):
        for tt in range(TT):
            ps = pm[cnt["pm"] % 3]; cnt["pm"] += 1
            t0 = tt * TM
            for c in range(2):
                P.op("tensor", lambda e, c=c, ps=ps, t0=t0: e.matmul(ps[:TM, :], uT[:, c, t0:t0 + TM], dft[:, c, :], start=(c == 0), stop=(c == 1)),
                     [uT, dft], [ps])
            o = outs[cnt["o"] % 4]; cnt["o"] += 1
            P.op("vector", lambda e, ps=ps, o=o: e.tensor_copy(o[:TM, :], ps[:TM, :]), [ps], [o])
            store(tm[g * G + t0:g * G + t0 + TM, 256:768], o[:TM, :], [o])

    for ci_, cb in enumerate(CBL):
        if ci_ + 1 < len(CBL):
            load_w(CBL[ci_ + 1])
        w = wb[cb % 2]
        c0 = cb * 512
        for g in range(NG):
            if c0 == 0:
                for jj in range(4):
                    fm_chunk(w, jj, g, "naq" if jj < 2 else "dfq", jj * 128)
            elif c0 == 512:
                tm_block(w, 0, 256, g, 0)
                for jj in (2, 3):
                    fm_chunk(w, jj, g, "fnet", 0)
                fft_ab(g)
            elif c0 == 1024:
                for jj in range(2):
                    fm_chunk(w, jj, g, "nak", 512 + jj * 128)
                tm_block(w, 256, 256, g, 768)
            else:
                for jj in range(2):
                    fm_chunk(w, jj, g, "dfk", 768 + jj * 128)
                tm_block(w, 256, 256, g, 1024)
        if ci_ + 1 < len(CBL):
            cast_w(CBL[ci_ + 1])
    P.finalize()
    return nc


NKR = 74
NKT = NKR * 64 // 128


def var_of_j(j):
    return {0: 0, 1: 1, 30: 3, 31: 4}.get(j, 2)


def emit_na(P, nc, pre, qT_d, kT_d, v_d, ks_d, vs_d, kcT_d, vc_d, bias_d, ident_d, y_d):
    qT = P.sb(pre + "qT", [128, 2, 4096], BF16)
    kT = P.sb(pre + "kT", [128, 2, NKT * 128], BF16)
    va = P.sb(pre + "va", [128, NKT, 4, 128], BF16)
    kTs = P.sb(pre + "kTs", [128, 2, 4, 640], BF16)
    vas = P.sb(pre + "vas", [128, 4, 5, 4, 128], BF16)
    kcT = P.sb(pre + "kcT", [128, 2, 256], BF16)
    vca = P.sb(pre + "vca", [128, 2, 4, 128], BF16)
    bias = P.sb(pre + "bias", [128, 5, 4, 5, 128], F32)
    ident = P.sb(pre + "ident", [128, 128], F32)
    ts_ = [P.sb(pre + "t%d" % i, [128, 640], F32) for i in range(2)]
    pts = [P.sb(pre + "pt%d" % i, [128, 896], BF16) for i in range(2)]
    accs = [P.sb(pre + "accs%d" % i, [128, 512], F32) for i in range(2)]
    rl = P.sb(pre + "rl", [128, 4], F32)
    yo = [P.sb(pre + "yo%d" % i, [128, 256], BF16) for i in range(2)]
    pss = [P.ps(pre + "pss%d" % i, [128, 1024]) for i in range(2)]
    pacc = [P.ps(pre + "pacc%d" % i, [128, 512]) for i in range(2)]
    ptr = P.ps(pre + "ptr", [128, 512])

    P.dma("sync", qT[:], qT_d.rearrange("(c p) t -> p c t", p=128), writes=[qT])
    P.dma("sync", kT[:], kT_d.rearrange("(c p) t -> p c t", p=128), writes=[kT])
    P.dma("sync", kcT[:], kcT_d.rearrange("(c p) t -> p c t", p=128), writes=[kcT])
    vst = P.sb(pre + "vst", [128, NKT, 256], BF16)
    vcst = P.sb(pre + "vcst", [128, 2, 256], BF16)
    vsst = P.sb(pre + "vsst", [128, 4, 5, 256], BF16)
    P.dma("gpsimd", vst[:], v_d, writes=[vst])
    P.dma("gpsimd", vcst[:], vc_d, writes=[vcst])
    P.dma("gpsimd", vsst[:], vs_d, writes=[vsst])
    for h in range(4):
        eng = ("vector", "gpsimd")[h % 2]
        P.op(eng, lambda e, h=h: e.tensor_copy(va[:, :, h, 0:64], vst[:, :, h * 64:(h + 1) * 64]), [vst], [va.sub("v%d" % h)])
        P.op(eng, lambda e, h=h: e.tensor_copy(vca[:, :, h, 0:64], vcst[:, :, h * 64:(h + 1) * 64]), [vcst], [vca.sub("v%d" % h)])
        P.op(eng, lambda e, h=h: e.tensor_copy(vas[:, :, :, h, 0:64].rearrange("p a b d -> p (a b) d"), vsst[:, :, :, h * 64:(h + 1) * 64].rearrange("p a b d -> p (a b) d")), [vsst], [vas.sub("v%d" % h)])
    P.op("gpsimd", lambda e: e.memset(va[:, :, :, 64:128], 1.0), [], [va.sub("o")])
    P.op("gpsimd", lambda e: e.memset(vca[:, :, :, 64:128], 1.0), [], [vca.sub("o")])
    for sp in range(4):
        P.dma("sync", kTs[:, :, sp, :], ks_d[sp].rearrange("(c p) t -> p c t", p=128), writes=[kTs.sub(sp)])
    P.op("gpsimd", lambda e: e.memset(vas[:, :, :, :, 64:128].rearrange("p a b c d -> p (a b c) d"), 1.0), [], [vas.sub("o")])
    P.dma("sync", bias[:], bias_d, writes=[bias])
    P.dma("sync", ident[:], ident_d, writes=[ident])

    def s_part(j, h, ps, t, pt, pa):
        var = var_of_j(j)
        sp = {0: 0, 1: 1, 30: 2, 31: 3}.get(j)
        kt0 = j
        c, p0 = h // 2, (h % 2) * 64
        for i in range(5):
            if sp is None:
                kl = lambda i=i: kT[p0:p0 + 64, c, (kt0 + i) * 128:(kt0 + i + 1) * 128]
            else:
                kl = lambda i=i: kTs[p0:p0 + 64, c, sp, i * 128:(i + 1) * 128]
            P.op("tensor", lambda e, i=i, kl=kl: e.matmul(ps[:, i * 128:(i + 1) * 128], kl(), qT[p0:p0 + 64, c, j * 128:(j + 1) * 128], start=True, stop=True),
                 [kT, kTs, qT], [ps])
        for i in range(2):
            P.op("tensor", lambda e, i=i: e.matmul(ps[:, 640 + i * 128:640 + (i + 1) * 128], kcT[p0:p0 + 64, c, i * 128:(i + 1) * 128],
                                                   qT[p0:p0 + 64, c, j * 128:(j + 1) * 128], start=True, stop=True), [kcT, qT], [ps])

    def av_part(j, h, ps, t, pt, pa):
        var = var_of_j(j)
        sp = {0: 0, 1: 1, 30: 2, 31: 3}.get(j)
        kt0 = j
        P.op("vector", lambda e: e.scalar_tensor_tensor(t[:], ps[:, 0:640], 0.125, bias[:, var, h, :, :].rearrange("p a b -> p (a b)"), ALU.mult, ALU.add),
             [ps, bias], [t])
        P.op("scalar", lambda e: e.activation(pt[:, 0:640], t[:], AF.Exp), [t], [pt.sub(0)])
        P.op("scalar", lambda e: e.activation(pt[:, 640:896], ps[:, 640:896], AF.Exp, scale=0.125), [ps], [pt.sub(1)])
        for i in range(7):
            if i < 5 and sp is not None:
                lhs = lambda i=i: vas[:, sp, i, h, :]
            elif i < 5:
                lhs = lambda i=i: va[:, kt0 + i, h, :]
            else:
                lhs = lambda i=i: vca[:, i - 5, h, :]
            P.op("tensor", lambda e, lhs=lhs, i=i: e.matmul(pa[:, h * 128:(h + 1) * 128], lhs(), pt[:, i * 128:(i + 1) * 128], start=(i == 0), stop=(i == 6)),
                 [va, vas, vca, pt], [pa.sub(h)])

    def post_part(j, pa, ac, y):
        P.op("scalar", lambda e: e.copy(ac[:], pa[:]), [pa], [ac])
        for h in range(4):
            P.op("tensor", lambda e, h=h: e.transpose(ptr[:, h * 128:(h + 1) * 128], ac[:, h * 128:(h + 1) * 128], ident[:]), [ac, ident], [ptr])
        for h in range(4):
            P.op("vector", lambda e, h=h: e.reciprocal(rl[:, h:h + 1], ptr[:, h * 128 + 64:h * 128 + 65]), [ptr], [rl.sub(h)])
            P.op("vector", lambda e, h=h: e.tensor_scalar(y[:, h * 64:(h + 1) * 64], ptr[:, h * 128:h * 128 + 64], rl[:, h:h + 1], None, ALU.mult),
                 [ptr, rl.sub(h)], [y.sub(h)])
        P.dma("sync", y_d[j * 128:(j + 1) * 128, :], y[:], reads=[y])

    its = []
    for j in range(32):
        for h in range(4):
            k_ = len(its)
            its.append((j, h, pss[k_ % 2], ts_[k_ % 2], pts[k_ % 2], pacc[j % 2]))
    s_part(*its[0])
    for k_, it in enumerate(its):
        if k_ + 1 < len(its):
            s_part(*its[k_ + 1])
        av_part(*it)
        if it[1] == 3:
            post_part(it[0], it[5], accs[it[0] % 2], yo[it[0] % 2])


def build_na():
    nc = bass.Bass("TRN2", target_bir_lowering=False)
    D = lambda n, s, dt, k: nc.dram_tensor(n, list(s), dt, kind=k).ap()
    qT = D("qT", [256, 4096], BF16, "ExternalInput")
    kT = D("kT", [256, NKT * 128], BF16, "ExternalInput")
    v = D("v", [128, NKT, 256], BF16, "ExternalInput")
    ks = D("ks", [4, 256, 640], BF16, "ExternalInput")
    vs = D("vs", [128, 4, 5, 256], BF16, "ExternalInput")
    kcT = D("kcT", [256, 256], BF16, "ExternalInput")
    vc = D("vc", [128, 2, 256], BF16, "ExternalInput")
    bias = D("bias", [128, 5, 4, 5, 128], F32, "ExternalInput")
    ident = D("ident", [128, 128], F32, "ExternalInput")
    y = D("y", [4096, 256], BF16, "ExternalOutput")
    P = Prog(nc)
    emit_na(P, nc, "n_", qT, kT, v, ks, vs, kcT, vc, bias, ident, y)
    P.finalize()
    return nc


EPS = 1e-6


def emit_fattn(P, nc, pre, Tq, Tk, nmaps, dk, diff, qT_d, kT_d, v_d, lamv_d, gsub_d, cst_d, ident_d, y_d):
    KT = Tk // 128
    QG = min(512, Tq)
    NQ = Tq // QG
    TT = QG // 128
    R = nmaps * dk
    scale = float(dk) ** -0.5
    qT = P.sb(pre + "qT", [R, Tq], BF16)
    kT = P.sb(pre + "kT", [R, Tk], BF16)
    va = P.sb(pre + "va", [128, KT, 128], BF16)
    ident = P.sb(pre + "ident", [128, 128], F32)
    lamv = P.sb(pre + "lamv", [128, 128], F32)
    gsub = P.sb(pre + "gsub", [128, 64], F32)
    cst = P.sb(pre + "cst", [128, 2], F32)
    gsc = P.sb(pre + "gsc", [128, 64], F32)
    sm = P.sb(pre + "sm", [128, 8], F32)
    prod = P.sb(pre + "prod", [128, 64], F32)
    epsb = P.sb(pre + "epsb", [128, 1], F32)
    pts = [P.sb(pre + "pt%d" % i, [128, nmaps * 512], BF16) for i in range(3)]
    accs = [P.sb(pre + "accs%d" % m, [128, QG], F32) for m in range(nmaps)]
    om = [P.sb(pre + "om%d" % m, [128, 64], F32) for m in range(2)]
    rl = P.sb(pre + "rl", [128, 2], F32)
    ss = P.sb(pre + "ss", [128, 2], F32)
    junk = P.sb(pre + "junk", [128, 64], F32)
    yo = [P.sb(pre + "yo%d" % i, [128, 64], BF16) for i in range(2)]
    pss = [P.ps(pre + "pss%d" % i, [128, nmaps * 512]) for i in range(3)]
    pacc = [P.ps(pre + "pacc%d" % m, [128, 512]) for m in range(nmaps)]

    P.dma("sync", qT[:], qT_d, writes=[qT])
    P.dma("sync", kT[:], kT_d, writes=[kT])
    vst = P.sb(pre + "vst", [128, KT, 64], BF16)
    P.dma("sync", vst[:], v_d, writes=[vst])
    P.op("vector", lambda e: e.tensor_copy(va[:, :, 0:64], vst[:]), [vst], [va.sub("v")])
    P.op("gpsimd", lambda e: e.memset(va[:, :, 64:128], 1.0), [], [va.sub("o")])
    P.dma("sync", ident[:], ident_d, writes=[ident])
    P.op("vector", lambda e: e.memset(epsb[:], EPS), [], [epsb])
    if diff:
        P.dma("sync", lamv[:], lamv_d, writes=[lamv])
        P.dma("sync", gsub[:], gsub_d, writes=[gsub])
        P.dma("sync", cst[:], cst_d, writes=[cst])
        P.op("vector", lambda e: e.tensor_tensor(prod[:, 0:32], lamv[:, 0:32], lamv[:, 32:64], ALU.mult), [lamv], [prod])
        P.op("vector", lambda e: e.tensor_tensor(prod[:, 32:64], lamv[:, 64:96], lamv[:, 96:128], ALU.mult), [lamv], [prod])
        P.op("vector", lambda e: e.reduce_sum(sm[:, 0:1], prod[:, 0:32], AX.X), [prod], [sm])
        P.op("vector", lambda e: e.reduce_sum(sm[:, 1:2], prod[:, 32:64], AX.X), [prod, sm], [sm])
        P.op("scalar", lambda e: e.activation(sm[:, 2:4], sm[:, 0:2], AF.Exp), [sm], [sm])
        P.op("vector", lambda e: e.tensor_tensor(sm[:, 4:5], sm[:, 3:4], sm[:, 2:3], ALU.subtract), [sm], [sm])
        P.op("vector", lambda e: e.tensor_tensor(sm[:, 4:5], sm[:, 4:5], cst[:, 0:1], ALU.subtract), [sm, cst], [sm])
        P.op("vector", lambda e: e.tensor_scalar(gsc[:], gsub[:], cst[:, 1:2], None, ALU.mult), [gsub, cst], [gsc])

    ci = 0
    yi = 0
    LA = 2
    for qg in range(NQ):
        qs = slice(qg * QG, (qg + 1) * QG)
        its = []
        for kt in range(KT):
            ps = pss[ci % 3]
            pt = pts[ci % 3]
            ci += 1
            its.append((kt, ps, pt))
        def emit_s(kt, ps, pt):
            for m in range(nmaps):
                rs = slice(m * dk, (m + 1) * dk)
                P.op("tensor", lambda e, ps=ps, rs=rs, kt=kt, m=m, qs=qs: e.matmul(ps[:, m * 512:m * 512 + QG], kT[rs, kt * 128:(kt + 1) * 128], qT[rs, qs], start=True, stop=True),
                     [kT, qT], [ps])
        def emit_av(kt, ps, pt):
            if QG == 512:
                P.op("scalar", lambda e, ps=ps, pt=pt: e.activation(pt[:, 0:nmaps * 512], ps[:, 0:nmaps * 512], AF.Exp, scale=scale), [ps], [pt])
            else:
                for m in range(nmaps):
                    P.op("scalar", lambda e, ps=ps, pt=pt, m=m: e.activation(pt[:, m * 512:m * 512 + QG], ps[:, m * 512:m * 512 + QG], AF.Exp, scale=scale), [ps], [pt])
            for m in range(nmaps):
                P.op("tensor", lambda e, pt=pt, kt=kt, m=m: e.matmul(pacc[m][:, :QG], va[:, kt, :], pt[:, m * 512:m * 512 + QG], start=(kt == 0), stop=(kt == KT - 1)),
                     [va, pt], [pacc[m]])
        for idx in range(len(its) + LA):
            if idx < len(its):
                emit_s(*its[idx])
            if idx >= LA:
                emit_av(*its[idx - LA])
        for m in range(nmaps):
            eng = ("vector", "scalar")[m % 2]
            if eng == "vector":
                P.op("vector", lambda e, m=m: e.tensor_copy(accs[m][:], pacc[m][:, :QG]), [pacc[m]], [accs[m]])
            else:
                P.op("scalar", lambda e, m=m: e.copy(accs[m][:], pacc[m][:, :QG]), [pacc[m]], [accs[m]])
        for tt in range(TT):
            for m in range(nmaps):
                P.op("tensor", lambda e, m=m, tt=tt: e.transpose(pss[0][:, m * 128:(m + 1) * 128], accs[m][:, tt * 128:(tt + 1) * 128], ident[:]),
                     [accs[m], ident], [pss[0]])
            for m in range(nmaps):
                P.op("vector", lambda e, m=m: e.reciprocal(rl[:, m:m + 1], pss[0][:, m * 128 + 64:m * 128 + 65]), [pss[0]], [rl.sub(m)])
                P.op("vector", lambda e, m=m: e.tensor_scalar(om[m][:], pss[0][:, m * 128:m * 128 + 64], rl[:, m:m + 1], None, ALU.mult),
                     [pss[0], rl.sub(m)], [om[m]])
            y = yo[yi % 2]
            yi += 1
            if diff:
                P.op("vector", lambda e: e.scalar_tensor_tensor(om[0][:], om[1][:], sm[:, 4:5], om[0][:], ALU.mult, ALU.add),
                     [om[0], om[1], sm], [om[0]])
                P.op("scalar", lambda e: e.activation(junk[:], om[0][:], AF.Square, accum_out=ss[:, 0:1]), [om[0]], [junk, ss])
                P.op("scalar", lambda e: e.activation(ss[:, 1:2], ss[:, 0:1], AF.Sqrt, bias=epsb[:], scale=1.0 / 64), [ss, epsb], [ss])
                P.op("vector", lambda e: e.reciprocal(rl[:, 0:1], ss[:, 1:2]), [ss], [rl.sub(0)])
                P.op("vector", lambda e, y=y: e.scalar_tensor_tensor(y[:], om[0][:], rl[:, 0:1], gsc[:], ALU.mult, ALU.mult),
                     [om[0], rl.sub(0), gsc], [y])
            else:
                P.op("vector", lambda e, y=y: e.tensor_copy(y[:], om[0][:]), [om[0]], [y])
            t0 = qg * QG + tt * 128
            P.dma("gpsimd", y_d[t0:t0 + 128, :], y[:], reads=[y])


def build_fattn(Tq, Tk, nmaps, dk, diff):
    nc = bass.Bass("TRN2", target_bir_lowering=False)
    D = lambda n, s, dt, k: nc.dram_tensor(n, list(s), dt, kind=k).ap()
    R = nmaps * dk
    qT = D("qT", [R, Tq], BF16, "ExternalInput")
    kT = D("kT", [R, Tk], BF16, "ExternalInput")
    v = D("v", [128, Tk // 128, 64], BF16, "ExternalInput")
    lamv = D("lamv", [128, 128], F32, "ExternalInput")
    gsub = D("gsub", [128, 64], F32, "ExternalInput")
    cst = D("cst", [128, 2], F32, "ExternalInput")
    ident = D("ident", [128, 128], F32, "ExternalInput")
    y = D("y", [Tq, 64], BF16, "ExternalOutput")
    P = Prog(nc)
    emit_fattn(P, nc, "a_", Tq, Tk, nmaps, dk, diff, qT, kT, v, lamv, gsub, cst, ident, y)
    P.finalize()
    return nc


def emit_fft(P, nc, pre, z_d, cm_d, tw_d, fT_d, CH=64):
    z = P.sb(pre + "z", [128, 2, CH, 128], BF16)
    cm = P.sb(pre + "cm", [128, 5, 128], BF16)
    tw = P.sb(pre + "tw", [128, 2, 4, 128], F32)
    tt = [P.sb(pre + "tt%d" % i, [128, 512], F32) for i in range(4)]
    ypr = [P.sb(pre + "ypr%d" % i, [128, 512], BF16) for i in range(2)]
    ypi = [P.sb(pre + "ypi%d" % i, [128, 512], BF16) for i in range(2)]
    xo = [P.sb(pre + "xo%d" % i, [128, 512], BF16) for i in range(2)]
    pyr = [P.ps(pre + "pyr%d" % i, [128, 512]) for i in range(2)]
    pyi = [P.ps(pre + "pyi%d" % i, [128, 512]) for i in range(2)]
    px = [P.ps(pre + "px%d" % i, [128, 512]) for i in range(2)]
    P.dma("sync", z[:, 0], z_d[:, 0], writes=[z.sub(0)])
    P.dma("gpsimd", z[:, 1], z_d[:, 1], writes=[z.sub(1)])
    P.dma("sync", cm[:], cm_d, writes=[cm])
    P.dma("sync", tw[:], tw_d, writes=[tw])
    ctf = tw[:, 0].rearrange("p a b -> p (a b)")
    stf = tw[:, 1].rearrange("p a b -> p (a b)")
    for g in range(CH // 4):
        yr, yi = pyr[g % 2], pyi[g % 2]
        for cc in range(4):
            c = g * 4 + cc
            o = slice(cc * 128, (cc + 1) * 128)
            P.op("tensor", lambda e, c=c, o=o, yr=yr: e.matmul(yr[:, o], z[:, 0, c, :], cm[:, 0, :], start=True, stop=False), [z, cm], [yr])
            P.op("tensor", lambda e, c=c, o=o, yr=yr: e.matmul(yr[:, o], z[:, 1, c, :], cm[:, 1, :], start=False, stop=True), [z, cm], [yr])
            P.op("tensor", lambda e, c=c, o=o, yi=yi: e.matmul(yi[:, o], z[:, 1, c, :], cm[:, 0, :], start=True, stop=False), [z, cm], [yi])
            P.op("tensor", lambda e, c=c, o=o, yi=yi: e.matmul(yi[:, o], z[:, 0, c, :], cm[:, 2, :], start=False, stop=True), [z, cm], [yi])
        a, b = ypr[g % 2], ypi[g % 2]
        P.op("vector", lambda e, yr=yr: e.tensor_tensor(tt[0][:], yr[:], ctf, ALU.mult), [yr, tw], [tt[0]])
        P.op("vector", lambda e, yi=yi: e.tensor_tensor(tt[1][:], yi[:], stf, ALU.mult), [yi, tw], [tt[1]])
        P.op("gpsimd", lambda e, a=a: e.tensor_tensor(a[:], tt[0][:], tt[1][:], ALU.add), [tt[0], tt[1]], [a])
        P.op("vector", lambda e, yi=yi: e.tensor_tensor(tt[2][:], yi[:], ctf, ALU.mult), [yi, tw], [tt[2]])
        P.op("vector", lambda e, yr=yr: e.tensor_tensor(tt[3][:], yr[:], stf, ALU.mult), [yr, tw], [tt[3]])
        P.op("gpsimd", lambda e, b=b: e.tensor_tensor(b[:], tt[2][:], tt[3][:], ALU.subtract), [tt[2], tt[3]], [b])
        x = px[g % 2]
        P.op("tensor", lambda e, x=x, a=a: e.matmul(x[:], cm[:, 3, :], a[:], start=True, stop=False), [cm, a], [x])
        P.op("tensor", lambda e, x=x, b=b: e.matmul(x[:], cm[:, 4, :], b[:], start=False, stop=True), [cm, b], [x])
        o_ = xo[g % 2]
        P.op("scalar", lambda e, x=x, o_=o_: e.copy(o_[:], x[:]), [x], [o_])
        P.dma(("sync", "gpsimd")[g % 2], fT_d[g * 4:(g + 1) * 4, :].rearrange("c (k2 k1) -> k2 c k1", k1=128),
              o_[:].rearrange("p (c k) -> p c k", c=4), reads=[o_])


def build_fft(CH=64):
    nc = bass.Bass("TRN2", target_bir_lowering=False)
    D = lambda n, s, dt, k: nc.dram_tensor(n, list(s), dt, kind=k).ap()
    z = D("z", [128, 2, CH, 128], BF16, "ExternalInput")
    cm = D("cm", [128, 5, 128], BF16, "ExternalInput")
    tw = D("tw", [128, 2, 4, 128], F32, "ExternalInput")
    fT = D("fT", [CH, 16384], BF16, "ExternalOutput")
    P = Prog(nc)
    emit_fft(P, nc, "f_", z, cm, tw, fT, CH)
    P.finalize()
    return nc


def build_fft256(CH=64):
    nc = bass.Bass("TRN2", target_bir_lowering=False)
    D = lambda n, s, dt, k: nc.dram_tensor(n, list(s), dt, kind=k).ap()
    z_d = D("z", [128, 2, 2, CH], BF16, "ExternalInput")
    cn_d = D("cn", [128, 2, 2, 256], BF16, "ExternalInput")
    fT = D("fT", [CH, 256], BF16, "ExternalOutput")
    P = Prog(nc)
    z = P.sb("z", [128, 2, 2, CH], BF16)
    cn = P.sb("cn", [128, 2, 2, 256], BF16)
    o = P.sb("o", [CH, 256], BF16)
    ps = P.ps("ps", [128, 512])
    P.dma("sync", z[:], z_d, writes=[z])
    P.dma("sync", cn[:], cn_d, writes=[cn])
    n = 0
    for ri in range(2):
        for t in range(2):
            P.op("tensor", lambda e, ri=ri, t=t, n=n: e.matmul(ps[:CH, :256], z[:, ri, t, :], cn[:, ri, t, :], start=(n == 0), stop=(n == 3)), [z, cn], [ps])
            n += 1
    P.op("vector", lambda e: e.tensor_copy(o[:], ps[:CH, :256]), [ps], [o])
    P.dma("sync", fT, o[:], reads=[o])
    P.finalize()
    return nc


EPS = 1e-6


def build_merge(ntok, G):
    NG = ntok // G
    TT = G // 128
    NT = ntok // 128
    nc = bass.Bass("TRN2", target_bir_lowering=False)
    D = lambda n, s, dt, k: nc.dram_tensor(n, list(s), dt, kind=k).ap()
    xT = D("xT", [1024, ntok], F32, "ExternalInput")
    vecs = D("vecs", [128, 56], F32, "ExternalInput")
    wg_d = D("wg", [1024, 4096], F32, "ExternalInput")
    wbr_d = D("wbr", [4, 256, 1024], F32, "ExternalInput")
    wo_d = D("wo", [1024, 1024], F32, "ExternalInput")
    wr_d = D("wr", [1024, 16], F32, "ExternalInput")
    fw_d = D("fw", [256, 256], F32, "ExternalInput")
    pw_d = D("pw", [4, 64, 64], F32, "ExternalInput")
    psc_d = D("psc", [64, 4], F32, "ExternalInput")
    band_d = D("band", [128, 3, 4, 3, 128], BF16, "ExternalInput")
    ones_d = D("ones", [128, 128], BF16, "ExternalInput")
    yT_d = D("yT", [3, 256, ntok], BF16, "ExternalInput")
    u_d = D("u", [128, NT + 2, 256], BF16, "ExternalInput")
    xmT = D("xmT", [1024, ntok], F32, "ExternalOutput")
    h2T = D("h2T", [1024, ntok], BF16, "ExternalOutput")
    aff = D("aff", [ntok, 16], F32, "ExternalOutput")

    P = Prog(nc)
    wg = P.sb("wg", [128, 8, 4096], BF16)
    wo = P.sb("wo", [128, 8, 1024], BF16)
    wb = P.sb("wb", [128, 3, 2, 1024], BF16)
    wb2 = P.sb("wb2", [64, 4, 1024], BF16)
    fw = P.sb("fw", [128, 2, 256], BF16)
    pw = P.sb("pw", [64, 4, 64], BF16)
    wr = P.sb("wr", [128, 8, 16], F32)
    psc = P.sb("psc", [64, 4], F32)
    band = P.sb("band", [128, 3, 4, 3, 128], BF16)
    ones = P.sb("ones", [128, 128], BF16)
    vec = P.sb("vec", [128, 56], F32)
    A1 = P.sb("A1", [128, 8], F32)
    A2 = P.sb("A2", [128, 8], F32)
    epsb = P.sb("epsb", [128, 1], F32)
    xs = P.sb("xs", [128, 8, G], F32)
    stg = P.sb("stg", [128, 8, 512], F32)
    hT = P.sb("hT", [128, 8, G], BF16)
    sqb = P.sb("sqb", [128, 8, G], BF16)
    acc = P.sb("acc", [128, 8, G], F32)
    yt = P.sb("yt", [128, 3, 2, G], BF16)
    yfn = P.sb("yfn", [128, 2, G], BF16)
    ypl = P.sb("ypl", [64, 4, G], BF16)
    pld = P.sb("pld", [64, 4, G], BF16)
    ub = P.sb("ub", [128, TT + 2, 256], BF16)
    gsb = [P.sb("gsb%d" % i, [128, G], BF16) for i in range(2)]
    tmp = [P.sb("tmp%d" % i, [128, G], F32) for i in range(2)]
    rstd = P.sb("rstd", [128, G], F32)
    sd = P.sb("sd", [128, G], F32)
    lg = P.sb("lg", [128, 16], F32)
    ex = P.sb("ex", [128, 16], F32)
    sm = P.sb("sm", [128, 4], F32)
    ao = [P.sb("ao%d" % i, [128, 16], F32) for i in range(2)]
    pg = [P.ps("pg%d" % i, [128, 512]) for i in range(2)]
    pz = [P.ps("pz%d" % i, [128, 512]) for i in range(2)]
    pst = P.ps("pst", [128, 512])
    pp = P.ps("pp", [64, 4, 128])
    pl = P.ps("pl", [128, 16])
    h2f = stg

    P.dma("sync", vec[:], vecs, writes=[vec])
    P.dma("sync", band[:], band_d, writes=[band])
    P.dma("sync", ones[:], ones_d, writes=[ones])
    P.dma("sync", psc[:], psc_d, writes=[psc])
    P.dma("sync", wr[:], wr_d.rearrange("(k p) n -> p k n", p=128), writes=[wr])
    P.op("vector", lambda e: e.memset(epsb[:], EPS), [], [epsb])
    P.op("vector", lambda e: e.tensor_scalar(A1[:], vec[:, 8:16], 1.0, None, ALU.add), [vec], [A1])
    P.op("vector", lambda e: e.tensor_tensor(A1[:], A1[:], vec[:, 16:24], ALU.mult), [A1, vec], [A1])
    P.op("vector", lambda e: e.tensor_scalar(A2[:], vec[:, 40:48], 1.0, None, ALU.add), [vec], [A2])
    P.op("vector", lambda e: e.tensor_tensor(A2[:], A2[:], vec[:, 48:56], ALU.mult), [A2, vec], [A2])
    ce = [0]
    def cast(dst, src, rd, wr_):
        eng = ("vector", "gpsimd", "scalar")[ce[0] % 3]
        ce[0] += 1
        if eng == "scalar":
            P.op("scalar", lambda e: e.copy(dst, src), rd, wr_)
        else:
            P.op(eng, lambda e: e.tensor_copy(dst, src), rd, wr_)
    for cb in range(8):
        P.dma("sync", stg[:], wg_d[:, cb * 512:(cb + 1) * 512].rearrange("(k p) n -> p k n", p=128), writes=[stg])
        for k in range(8):
            cast(wg[:, k, cb * 512:(cb + 1) * 512], stg[:, k, :], [stg], [wg.sub((cb, k))])
    for cb in range(2):
        P.dma("sync", stg[:], wo_d[:, cb * 512:(cb + 1) * 512].rearrange("(k p) n -> p k n", p=128), writes=[stg])
        for k in range(8):
            cast(wo[:, k, cb * 512:(cb + 1) * 512], stg[:, k, :], [stg], [wo.sub((cb, k))])
    for cb in range(2):
        P.dma("sync", stg[:], wbr_d[:, :, cb * 512:(cb + 1) * 512].rearrange("i (c p) n -> p (i c) n", p=128), writes=[stg])
        for bi, i in enumerate((0, 1, 3)):
            for c in range(2):
                cast(wb[:, bi, c, cb * 512:(cb + 1) * 512], stg[:, i * 2 + c, :], [stg], [wb.sub((bi, c, cb))])
    for cb in range(2):
        P.dma("sync", stg[0:64, 0:4, :], wbr_d[2, :, cb * 512:(cb + 1) * 512].rearrange("(g p) n -> p g n", p=64), writes=[stg])
        cast(wb2[:, :, cb * 512:(cb + 1) * 512], stg[0:64, 0:4, :], [stg], [wb2.sub(cb)])
    P.dma("sync", stg[:, 0:2, 0:256], fw_d.rearrange("(c p) n -> p c n", p=128), writes=[stg])
    cast(fw[:], stg[:, 0:2, 0:256], [stg], [fw])
    P.dma("sync", stg[0:64, 0:4, 0:64], pw_d.rearrange("g p n -> p g n"), writes=[stg])
    cast(pw[:], stg[0:64, 0:4, 0:64], [stg], [pw])

    def norm(src, A, shcol, dst_bf, dst_f32, rd):
        P.op("scalar", lambda e: e.activation(sqb[:], src[:], AF.Square), [src], [sqb])
        for k in range(8):
            P.op("tensor", lambda e, k=k: e.matmul(pst[:, :G], ones[:], sqb[:, k, :], start=(k == 0), stop=(k == 7)), [ones, sqb], [pst])
        P.op("scalar", lambda e: e.activation(sd[:], pst[:, :G], AF.Sqrt, bias=epsb[:], scale=1.0 / 1024), [pst, epsb], [sd])
        P.op("vector", lambda e: e.reciprocal(rstd[:], sd[:]), [sd], [rstd])
        for k in range(8):
            tf = tmp[k % 2]
            P.op("vector", lambda e, k=k, tf=tf: e.scalar_tensor_tensor(tf[:], src[:, k, :], A[:, k:k + 1], rstd[:], ALU.mult, ALU.mult),
                 [src, A, rstd], [tf])
            if dst_f32 is None:
                P.op("scalar", lambda e, k=k, tf=tf: e.activation(dst_bf[:, k, :], tf[:], AF.Identity, bias=vec[:, shcol + k:shcol + k + 1], scale=1.0),
                     [tf, vec], [dst_bf.sub(k)])
            else:
                P.op("scalar", lambda e, k=k, tf=tf: e.activation(dst_f32[:, k, :G], tf[:], AF.Identity, bias=vec[:, shcol + k:shcol + k + 1], scale=1.0),
                     [tf, vec], [dst_f32.sub(k)])
                P.op("gpsimd", lambda e, k=k: e.tensor_copy(dst_bf[:, k, :], dst_f32[:, k, :G]), [dst_f32.sub(k)], [dst_bf.sub(k)])

    ci = [0]
    for g in range(NG):
        ts = slice(g * G, (g + 1) * G)
        P.dma("sync", xs[:], xT[:, ts].rearrange("(k p) t -> p k t", p=128), writes=[xs])
        P.dma("gpsimd", yt[:].rearrange("p i c t -> p (i c) t"), yT_d[:, :, ts].rearrange("i (c p) t -> p (i c) t", p=128), writes=[yt])
        P.dma("gpsimd", ub[:], u_d[:, g * TT:g * TT + TT + 2, :], writes=[ub])
        norm(xs, A1, 0, hT, None, None)
        for mo in range(2):
            ps = pz[ci[0] % 2]; ci[0] += 1
            for c in range(2):
                P.op("tensor", lambda e, ps=ps, c=c, mo=mo: e.matmul(ps[:, :G], fw[:, c, mo * 128:(mo + 1) * 128], yt[:, 2, c, :], start=(c == 0), stop=(c == 1)),
                     [fw, yt], [ps])
            P.op("scalar", lambda e, ps=ps, mo=mo: e.copy(yfn[:, mo, :], ps[:, :G]), [ps], [yfn.sub(mo)])
        for tt in range(TT):
            li = g * TT + tt
            var = 0 if li == 0 else (2 if li == NT - 1 else 1)
            for gr in range(4):
                for j in range(3):
                    P.op("tensor", lambda e, gr=gr, j=j, tt=tt, var=var: e.matmul(pp[:, gr, :], ub[:, tt + j, gr * 64:(gr + 1) * 64], band[:, var, gr, j, :],
                                                                              start=(j == 0), stop=(j == 2)), [ub, band], [pp])
            P.op("vector", lambda e, tt=tt: e.tensor_copy(pld[:, :, tt * 128:(tt + 1) * 128], pp[:]), [pp], [pld.sub(tt)])
        for gr in range(4):
            ps = pz[ci[0] % 2]; ci[0] += 1
            P.op("tensor", lambda e, ps=ps, gr=gr: e.matmul(ps[0:64, :G], pw[:, gr, :], pld[:, gr, :], start=True, stop=True), [pw, pld], [ps])
            P.op("scalar", lambda e, ps=ps, gr=gr: e.activation(ypl[:, gr, :], ps[0:64, :G], AF.Identity, scale=psc[:, gr:gr + 1]), [ps, psc], [ypl.sub(gr)])
        for dc in range(8):
            ds_ = slice(dc * 128, (dc + 1) * 128)
            for i in range(4):
                pgt = pg[ci[0] % 2]; pzt = pz[ci[0] % 2]; gs = gsb[ci[0] % 2]; ci[0] += 1
                for k in range(8):
                    P.op("tensor", lambda e, pgt=pgt, k=k, i=i, dc=dc: e.matmul(pgt[:, :G], wg[:, k, i * 1024 + dc * 128:i * 1024 + (dc + 1) * 128], hT[:, k, :],
                                                                              start=(k == 0), stop=(k == 7)), [wg, hT], [pgt])
                if i == 2:
                    for gr in range(4):
                        P.op("tensor", lambda e, pzt=pzt, gr=gr, ds_=ds_: e.matmul(pzt[:, :G], wb2[:, gr, ds_], ypl[:, gr, :], start=(gr == 0), stop=(gr == 3)),
                             [wb2, ypl], [pzt])
                else:
                    bi = {0: 0, 1: 1, 3: 2}[i]
                    for c in range(2):
                        rhs = (lambda c=c, i=i: yt[:, i, c, :]) if i < 2 else (lambda c=c: yfn[:, c, :])
                        P.op("tensor", lambda e, pzt=pzt, c=c, bi=bi, ds_=ds_, rhs=rhs: e.matmul(pzt[:, :G], wb[:, bi, c, ds_], rhs(), start=(c == 0), stop=(c == 1)),
                             [wb, yt, yfn], [pzt])
                P.op("scalar", lambda e, pgt=pgt, gs=gs: e.activation(gs[:], pgt[:, :G], AF.Sigmoid), [pgt], [gs])
                if i == 0:
                    P.op("vector", lambda e, pzt=pzt, gs=gs, dc=dc: e.tensor_tensor(acc[:, dc, :], pzt[:, :G], gs[:], ALU.mult), [pzt, gs], [acc.sub(dc)])
                else:
                    tf = tmp[i % 2]
                    P.op("vector", lambda e, pzt=pzt, gs=gs, tf=tf: e.tensor_tensor(tf[:], pzt[:, :G], gs[:], ALU.mult), [pzt, gs], [tf])
                    if i < 3:
                        P.op("gpsimd", lambda e, tf=tf, dc=dc: e.tensor_tensor(acc[:, dc, :], acc[:, dc, :], tf[:], ALU.add), [acc.sub(dc), tf], [acc.sub(dc)])
                    else:
                        P.op("gpsimd", lambda e, tf=tf, dc=dc: e.tensor_tensor(sqb[:, dc, :], acc[:, dc, :], tf[:], ALU.add), [acc.sub(dc), tf], [sqb.sub(dc)])
        for d2 in range(8):
            ps = pz[ci[0] % 2]; ci[0] += 1
            for dc in range(8):
                P.op("tensor", lambda e, ps=ps, dc=dc, d2=d2: e.matmul(ps[:, :G], wo[:, dc, d2 * 128:(d2 + 1) * 128], sqb[:, dc, :], start=(dc == 0), stop=(dc == 7)),
                     [wo, sqb], [ps])
            P.op("vector", lambda e, ps=ps, d2=d2: e.scalar_tensor_tensor(xs[:, d2, :], ps[:, :G], vec[:, 24 + d2:25 + d2], xs[:, d2, :], ALU.mult, ALU.add),
                 [ps, vec, xs.sub(d2)], [xs.sub(d2)])
        P.dma("sync", xmT[:, ts].rearrange("(k p) t -> p k t", p=128), xs[:], reads=[xs])
        norm(xs, A2, 32, hT, h2f, None)
        P.dma("gpsimd", h2T[:, ts].rearrange("(k p) t -> p k t", p=128), hT[:], reads=[hT])
        for tt in range(TT):
            for k in range(8):
                P.op("tensor", lambda e, k=k, tt=tt: e.matmul(pl[:], h2f[:, k, tt * 128:(tt + 1) * 128], wr[:, k, :], start=(k == 0), stop=(k == 7)),
                     [h2f, wr], [pl])
            a = ao[tt % 2]
            P.op("vector", lambda e: e.tensor_copy(lg[:], pl[:]), [pl], [lg])
            P.op("vector", lambda e: e.reduce_max(sm[:, 0:1], lg[:], AX.X), [lg], [sm.sub(0)])
            P.op("vector", lambda e: e.tensor_scalar(sm[:, 1:2], sm[:, 0:1], -1.0, None, ALU.mult), [sm.sub(0)], [sm.sub(1)])
            P.op("scalar", lambda e: e.activation(ex[:], lg[:], AF.Exp, bias=sm[:, 1:2], scale=1.0, accum_out=sm[:, 2:3]), [lg, sm.sub(1)], [ex, sm.sub(2)])
            P.op("vector", lambda e: e.reciprocal(sm[:, 3:4], sm[:, 2:3]), [sm.sub(2)], [sm.sub(3)])
            P.op("vector", lambda e, a=a: e.tensor_scalar(a[:], ex[:], sm[:, 3:4], None, ALU.mult), [ex, sm.sub(3)], [a])
            t0 = g * G + tt * 128
            P.dma("sync", aff[t0:t0 + 128, :], a[:], reads=[a])
    P.finalize()
    return nc


FF = 1408
NITER = 26


def build_experts(NI, CAP, NE=4):
    T = NI * 128
    SR = min(128, CAP)
    NSL = CAP // SR
    HS = min(1024, CAP)
    NH = CAP // HS
    SG = min(512, HS)
    NSG = HS // SG
    STH = HS // SR
    CH = min(512, NI * 16)
    NCH = NI * 16 // CH
    IPC = CH // 16
    nc = bass.Bass("TRN2", target_bir_lowering=False)
    D = lambda n, s, dt, k: nc.dram_tensor(n, list(s), dt, kind=k).ap()
    aff_d = D("aff", [128, NI, 16], F32, "ExternalInput")
    h2_d = D("h2", [T, 1024], BF16, "ExternalInput")
    wgate_d = D("wgate", [NE, 1024, FF], F32, "ExternalInput")
    wup_d = D("wup", [NE, 1024, FF], F32, "ExternalInput")
    wdown_d = D("wdown", [NE, FF, 1024], F32, "ExternalInput")
    cst_d = D("cst", [128, 3, 128], BF16, "ExternalInput")
    pos_d = D("pos", [128, NI, 16], I32, "ExternalOutput")
    ye_d = D("ye", [NE, CAP, 1024], BF16, "ExternalOutput")
    xe_d = nc.dram_tensor("xe", [NE * CAP, 1024], BF16, kind="Internal").ap()

    P = Prog(nc)
    aff = P.sb("aff", [128, NI, 16], F32)
    cmpb = P.sb("cmpb", [128, NI, 16], BF16)
    M = P.sb("M", [128, NI, 16], BF16)
    S = P.sb("S", [128, NI, 16], F32)
    W = P.sb("W", [128, NI, 16], F32)
    incl = P.sb("incl", [128, NI, 16], F32)
    zer = P.sb("zer", [128, NI], F32)
    posi = P.sb("posi", [128, NI, 16], I32)
    tau = P.sb("tau", [128, 16], F32)
    mid = P.sb("mid", [128, 16], F32)
    inc = P.sb("inc", [128, 16], F32)
    cntp = P.sb("cntp", [128, 16], BF16)
    cst = P.sb("cst", [128, 3, 128], BF16)
    pc = P.ps("pc", [128, 16])
    pw_ = [P.ps("pw%d" % i, [128, 512]) for i in range(2)]
    ones, ltri, ident = (lambda: cst[:, 0, :]), (lambda: cst[:, 1, :]), (lambda: cst[:, 2, :])

    P.dma("sync", aff[:], aff_d, writes=[aff])
    P.dma("sync", cst[:], cst_d, writes=[cst])
    P.op("vector", lambda e: e.memset(tau[:], 0.0), [], [tau])
    P.op("gpsimd", lambda e: e.memset(zer[:], 0.0), [], [zer])
    for it in range(NITER):
        step = 2.0 ** -(it + 1)
        P.op("vector", lambda e, step=step: e.tensor_scalar(mid[:], tau[:], step, None, ALU.add), [tau], [mid])
        P.op("vector", lambda e: e.tensor_tensor(cmpb[:], aff[:], mid[:].unsqueeze(1).to_broadcast([128, NI, 16]), ALU.is_ge), [aff, mid], [cmpb])
        def red(e):
            with nc.allow_low_precision(reason="exact small integer counts"):
                return e.tensor_reduce(cntp[:], cmpb[:].rearrange("p i e -> p e i"), AX.X, ALU.add)
        P.op("vector", red, [cmpb], [cntp])
        P.op("tensor", lambda e: e.matmul(pc[:], ones(), cntp[:], start=True, stop=True), [cst, cntp], [pc])
        P.op("vector", lambda e, step=step: e.tensor_scalar(inc[:], pc[:], CAP - 0.5, step, ALU.is_ge, ALU.mult), [pc], [inc])
        P.op("vector", lambda e: e.tensor_tensor(tau[:], tau[:], inc[:], ALU.add), [tau, inc], [tau])
    P.op("vector", lambda e: e.tensor_tensor(M[:], aff[:], tau[:].unsqueeze(1).to_broadcast([128, NI, 16]), ALU.is_ge), [aff, tau], [M])
    Mf = lambda c: M[:].rearrange("p i e -> p (i e)")[:, c * CH:(c + 1) * CH]
    for c in range(NCH):
        a, b_ = pw_[0], pw_[1]
        P.op("tensor", lambda e, c=c: e.matmul(a[:, :CH], ltri(), Mf(c), start=True, stop=True), [cst, M], [a])
        P.op("tensor", lambda e, c=c: e.matmul(b_[:, :CH], ones(), Mf(c), start=True, stop=True), [cst, M], [b_])
        P.op("vector", lambda e, c=c: e.tensor_copy(W[:].rearrange("p i e -> p (i e)")[:, c * CH:(c + 1) * CH], a[:, :CH]), [a], [W.sub(c)])
        P.op("scalar", lambda e, c=c: e.copy(S[:].rearrange("p i e -> p (i e)")[:, c * CH:(c + 1) * CH], b_[:, :CH]), [b_], [S.sub(c)])
    for ex in range(16):
        P.op("vector", lambda e, ex=ex: e.tensor_tensor_scan(incl[:, :, ex], S[:, :, ex], zer[:], 0.0, ALU.add, ALU.add), [S, zer], [incl.sub(ex)])
    P.op("vector", lambda e: e.tensor_tensor(W[:], W[:], incl[:], ALU.add), [W, incl], [W])
    P.op("vector", lambda e: e.tensor_tensor(W[:], W[:], S[:], ALU.subtract), [W, S], [W])
    BIG = float(2 ** 20)
    P.op("vector", lambda e: e.tensor_scalar(W[:], W[:], -BIG, None, ALU.add), [W], [W])
    P.op("vector", lambda e: e.tensor_tensor(W[:], W[:], M[:], ALU.mult), [W, M], [W])
    P.op("vector", lambda e: e.tensor_scalar(W[:], W[:], BIG, None, ALU.add), [W], [W])
    P.op("vector", lambda e: e.tensor_copy(posi[:], W[:]), [W], [posi])
    P.dma("sync", pos_d, posi[:], reads=[posi])
    padj = P.sb("padj", [128, NI, NE], I32)
    for e_ in range(NE):
        P.op("vector", lambda e, e_=e_: e.tensor_scalar(padj[:, :, e_], W[:, :, e_], float(e_ * CAP), None, ALU.add), [W], [padj.sub(e_)])

    return nc, P, dict(pw_=pw_, posi=padj, h2_d=h2_d, xe_d=xe_d, ye_d=ye_d, wgate_d=wgate_d, wup_d=wup_d, wdown_d=wdown_d, cst=cst, ident=ident,
                       NI=NI, CAP=CAP, NE=NE, SR=SR, NSL=NSL, HS=HS, NH=NH, SG=SG, NSG=NSG, STH=STH)


def emit_expert_ffn(nc, P, d, e_off):
    NI, CAP, NE, SR, NSL, HS, NH, SG, NSG, STH = (d[k] for k in ("NI", "CAP", "NE", "SR", "NSL", "HS", "NH", "SG", "NSG", "STH"))
    posi, h2_d, xe_d, ye_d, cst, ident = d["posi"], d["h2_d"], d["xe_d"], d["ye_d"], d["cst"], d["ident"]
    h2t = [P.sb("h2t%d" % i, [128, 1024], BF16) for i in range(3)]
    xe_tm = P.sb("xe_tm", [128, STH, 1024], BF16)
    xeT = P.sb("xeT", [128, 8, HS], BF16)
    hidT = P.sb("hidT", [128, 11, HS], BF16)
    wgu = P.sb("wgu", [128, 8, 2 * FF], BF16)
    wd = P.sb("wd", [128, 11, 1024], BF16)
    stg = [P.sb("stg%d" % i, [128, FF], F32) for i in range(3)]
    sg_ = [P.sb("sg%d" % i, [128, SG], BF16) for i in range(2)]
    yo = [P.sb("yo%d" % i, [128, 1024], BF16) for i in range(2)]
    ptr = [P.ps("ptr%d" % i, [128, 4, 128], BF16) for i in range(2)]
    pga = d["pw_"]
    pup = [P.ps("pup%d" % i, [128, 512]) for i in range(2)]
    xeB = Buf("xe_dram")
    breg = {}
    def mkreg(e):
        breg["r"] = e.alloc_register("bchk")
        return e.reg_mov(breg["r"], NE * CAP - 1)
    P.op("gpsimd", mkreg, [], [])
    hi_ = [0]
    def scatter(e_):
        for i in range(NI):
            ht = h2t[hi_[0] % 3]; hi_[0] += 1
            P.dma("sync", ht[:], h2_d[i * 128:(i + 1) * 128, :], writes=[ht])
            P.op("gpsimd", lambda e, ht=ht, i=i, e_=e_: e.indirect_dma_start(
                out=xe_d, out_offset=bass.IndirectOffsetOnAxis(ap=posi[:, i, e_off + e_:e_off + e_ + 1], axis=0),
                in_=ht[:], in_offset=None, bounds_check=breg["r"], oob_is_err=False), [ht, posi], [xeB.sub((e_, i))], dma=True)
    scatter(0)
    ce = [0]
    def cast(dst, src, rd, wr_, psum=False):
        eng = ("vector", "scalar")[ce[0] % 2] if psum else ("vector", "gpsimd", "scalar")[ce[0] % 3]
        ce[0] += 1
        if eng == "scalar":
            P.op("scalar", lambda e: e.copy(dst, src), rd, wr_)
        else:
            P.op(eng, lambda e: e.tensor_copy(dst, src), rd, wr_)
    si = [0]
    ci = [0]
    for e_ in range(NE):
        if e_ + 1 < NE:
            scatter(e_ + 1)
        for k in range(8):
            for gu, wsrc in enumerate((d["wgate_d"], d["wup_d"])):
                s = stg[si[0] % 3]; si[0] += 1
                P.dma("sync", s[:, 0:FF], wsrc[e_, k * 128:(k + 1) * 128, :], writes=[s])
                cast(wgu[:, k, gu * FF:(gu + 1) * FF], s[:, :], [s], [wgu.sub((k, gu))])
        for f in range(11):
            s = stg[si[0] % 3]; si[0] += 1
            P.dma("sync", s[:, 0:1024], d["wdown_d"][e_, f * 128:(f + 1) * 128, :], writes=[s])
            cast(wd[:, f, :], s[:, 0:1024], [s], [wd.sub(f)])
        for hh in range(NH):
            P.dma("gpsimd", xe_tm[:SR], xe_d[e_ * CAP + hh * HS:e_ * CAP + (hh + 1) * HS, :].rearrange("(s p) d -> p s d", p=SR), reads=[xeB.sub((e_, i)) for i in range(NI)], writes=[xe_tm])
            for st in range(STH):
                for kq in range(2):
                    pt = ptr[ci[0] % 2]; ci[0] += 1
                    for kk in range(4):
                        k = kq * 4 + kk
                        P.op("tensor", lambda e, pt=pt, kk=kk, k=k, st=st: e.transpose(pt[:, kk, :SR], xe_tm[:SR, st, k * 128:(k + 1) * 128], cst[:SR, 2, :SR]),
                             [xe_tm, cst], [pt])
                    cast(xeT[:, kq * 4:(kq + 1) * 4, st * SR:(st + 1) * SR], pt[:, :, :SR], [pt], [xeT.sub((st, kq))], psum=True)
            for f in range(11):
                for sgi in range(NSG):
                    ss = slice(sgi * SG, (sgi + 1) * SG)
                    a, b_ = pga[ci[0] % 2], pup[ci[0] % 2]; sgt = sg_[ci[0] % 2]; ci[0] += 1
                    for k in range(8):
                        P.op("tensor", lambda e, a=a, k=k, f=f, ss=ss: e.matmul(a[:, :SG], wgu[:, k, f * 128:(f + 1) * 128], xeT[:, k, ss], start=(k == 0), stop=(k == 7)),
                             [wgu, xeT], [a])
                    for k in range(8):
                        P.op("tensor", lambda e, b_=b_, k=k, f=f, ss=ss: e.matmul(b_[:, :SG], wgu[:, k, FF + f * 128:FF + (f + 1) * 128], xeT[:, k, ss], start=(k == 0), stop=(k == 7)),
                             [wgu, xeT], [b_])
                    P.op("scalar", lambda e, a=a, sgt=sgt: e.activation(sgt[:], a[:, :SG], AF.Silu), [a], [sgt])
                    P.op("vector", lambda e, b_=b_, sgt=sgt, f=f, ss=ss: e.tensor_tensor(hidT[:, f, ss], b_[:, :SG], sgt[:], ALU.mult), [b_, sgt], [hidT.sub((f, sgi))])
            for st in range(STH):
                y = yo[st % 2]
                for dh in range(2):
                    a = pga[ci[0] % 2]; ci[0] += 1
                    for f in range(11):
                        P.op("tensor", lambda e, a=a, f=f, st=st, dh=dh: e.matmul(a[:SR, :], hidT[:, f, st * SR:(st + 1) * SR], wd[:, f, dh * 512:(dh + 1) * 512], start=(f == 0), stop=(f == 10)),
                             [hidT, wd], [a])
                    if dh == 0:
                        P.op("scalar", lambda e, a=a, y=y: e.copy(y[:SR, 0:512], a[:SR, :]), [a], [y.sub(0)])
                    else:
                        P.op("vector", lambda e, a=a, y=y: e.tensor_copy(y[:SR, 512:1024], a[:SR, :]), [a], [y.sub(1)])
                r0 = hh * HS + st * SR
                P.dma("sync", ye_d[e_, r0:r0 + SR, :], y[:SR, :], reads=[y])


def build_experts_full(NI, CAP, NE=4):
    nc, P, d = build_experts(NI, CAP, NE)
    emit_expert_ffn(nc, P, d, 0)
    P.finalize()
    return nc


def build_combine(ntok, NEXP_CAP):
    NT = ntok // 128
    nc = bass.Bass("TRN2", target_bir_lowering=False)
    D = lambda n, s, dt, k: nc.dram_tensor(n, list(s), dt, kind=k).ap()
    xm_d = D("xm", [ntok, 1024], F32, "ExternalInput")
    pos_d = D("pos", [128, NT, 16], I32, "ExternalInput")
    aff_d = D("aff", [128, NT, 16], F32, "ExternalInput")
    eoff_d = D("eoff", [128, 16], F32, "ExternalInput")
    gt2_d = D("gt2", [128, 1024], F32, "ExternalInput")
    ye_d = D("ye", [NEXP_CAP, 1024], BF16, "ExternalInput")
    xo_d = D("xo", [ntok, 1024], F32, "ExternalOutput")
    P = Prog(nc)
    posi = P.sb("posi", [128, NT, 16], I32)
    posf = P.sb("posf", [128, NT, 16], F32)
    idx = P.sb("idx", [128, NT, 16], I32)
    aff = P.sb("aff", [128, NT, 16], F32)
    eoff = P.sb("eoff", [128, 16], F32)
    gt2 = P.sb("gt2", [128, 1024], F32)
    xm = [P.sb("xm%d" % i, [128, 1024], F32) for i in range(2)]
    acc = [P.sb("acc%d" % i, [128, 1024], F32) for i in range(2)]
    R = [P.sb("R%d" % i, [128, 1024], BF16) for i in range(6)]
    P.dma("sync", posi[:], pos_d, writes=[posi])
    P.dma("sync", aff[:], aff_d, writes=[aff])
    P.dma("sync", eoff[:], eoff_d, writes=[eoff])
    P.dma("sync", gt2[:], gt2_d, writes=[gt2])
    P.op("vector", lambda e: e.tensor_copy(posf[:], posi[:]), [posi], [posf])
    msk = P.sb("msk", [128, NT, 16], F32)
    P.op("vector", lambda e: e.tensor_single_scalar(msk[:], posf[:], 524288.0, ALU.is_lt), [posf], [msk])
    P.op("vector", lambda e: e.tensor_tensor(aff[:], aff[:], msk[:], ALU.mult), [aff, msk], [aff])
    for r_ in R:
        P.op("gpsimd", lambda e, r_=r_: e.memset(r_[:], 0.0), [], [r_])
    P.op("vector", lambda e: e.tensor_tensor(posf[:], posf[:], eoff[:].unsqueeze(1).to_broadcast([128, NT, 16]), ALU.add), [posf, eoff], [posf])
    P.op("vector", lambda e: e.tensor_copy(idx[:], posf[:]), [posf], [idx])
    ri = 0
    breg = {}
    def mkreg(e):
        breg["r"] = e.alloc_register("bchk")
        return e.reg_mov(breg["r"], NEXP_CAP - 1)
    P.op("gpsimd", mkreg, [], [])
    for i in range(NT):
        x = xm[i % 2]; a = acc[i % 2]
        P.dma("sync", x[:], xm_d[i * 128:(i + 1) * 128, :], writes=[x])
        for ex in range(16):
            r = R[ri % 6]; ri += 1
            P.op("gpsimd", lambda e, r=r, i=i, ex=ex: e.indirect_dma_start(
                out=r[:], out_offset=None, in_=ye_d, in_offset=bass.IndirectOffsetOnAxis(ap=idx[:, i, ex:ex + 1], axis=0),
                bounds_check=breg["r"], oob_is_err=False), [idx], [r], dma=True)
            if ex == 0:
                P.op("vector", lambda e, r=r, a=a, i=i, ex=ex: e.tensor_scalar(a[:], r[:], aff[:, i, ex:ex + 1], None, ALU.mult), [r, aff], [a])
            else:
                P.op("vector", lambda e, r=r, a=a, i=i, ex=ex: e.scalar_tensor_tensor(a[:], r[:], aff[:, i, ex:ex + 1], a[:], ALU.mult, ALU.add), [r, aff, a], [a])
        P.op("vector", lambda e, a=a: e.tensor_tensor(a[:], a[:], gt2[:], ALU.mult), [a, gt2], [a])
        P.op("gpsimd", lambda e, a=a, x=x: e.tensor_tensor(a[:], a[:], x[:], ALU.add), [a, x], [a])
        P.dma("sync", xo_d[i * 128:(i + 1) * 128, :], a[:], reads=[a])
    P.finalize()
    return nc


def build_ada(ncols):
    NCH = ncols // 128
    nc = bass.Bass("TRN2", target_bir_lowering=False)
    D = lambda n, s, dt, k: nc.dram_tensor(n, list(s), dt, kind=k).ap()
    wa_d = D("wa", [1024, ncols], F32, "ExternalInput")
    ba_d = D("ba", [128, NCH], F32, "ExternalInput")
    cv_d = D("cv", [128, 8, 3], F32, "ExternalInput")
    out_d = D("modT", [128, NCH, 3], F32, "ExternalOutput")
    P = Prog(nc)
    wa = P.sb("wa", [128, 8, ncols], F32)
    ba = P.sb("ba", [128, NCH], F32)
    cv = P.sb("cv", [128, 8, 3], F32)
    sv = P.sb("sv", [128, 8, 3], F32)
    o = P.sb("o", [128, NCH, 3], F32)
    pm = [P.ps("pm%d" % i, [128, 4]) for i in range(2)]
    for k in range(8):
        P.dma(("sync", "gpsimd")[k % 2], wa[:, k, :], wa_d[k * 128:(k + 1) * 128, :], writes=[wa.sub(k)])
    P.dma("sync", ba[:], ba_d, writes=[ba])
    P.dma("sync", cv[:], cv_d, writes=[cv])
    P.op("scalar", lambda e: e.activation(sv[:], cv[:], AF.Silu), [cv], [sv])
    for j in range(NCH):
        p = pm[j % 2]
        for k in range(8):
            P.op("tensor", lambda e, p=p, k=k, j=j: e.matmul(p[:, 0:3], wa[:, k, j * 128:(j + 1) * 128], sv[:, k, :], start=(k == 0), stop=(k == 7)), [wa, sv], [p])
        P.op("vector", lambda e, p=p, j=j: e.tensor_scalar(o[:, j, :], p[:, 0:3], ba[:, j:j + 1], None, ALU.add), [p, ba], [o.sub(j)])
    P.dma("sync", out_d, o[:], reads=[o])
    P.finalize()
    return nc

import math
import numpy as np

_PROGS = {}


def _prog(key, fn):
    if key not in _PROGS:
        _PROGS[key] = fn()
    return _PROGS[key]


def _run(nc, maps):
    return run_bass_kernel_spmd(nc, maps, core_ids=list(range(8))).results


def _c(a):
    return np.ascontiguousarray(a)


def _lay_pie(a, ni):
    return _c(a.reshape(ni, 128, 16).transpose(1, 0, 2))


def _expert_csts():
    c = np.zeros((128, 3, 128), np.float32)
    c[:, 0] = 1.0
    c[:, 1] = np.triu(np.ones((128, 128), np.float32), 1)
    c[:, 2] = np.eye(128, dtype=np.float32)
    return c.astype(BF)


def _fft256_consts():
    n = np.arange(256, dtype=np.float64)
    ang = 2 * np.pi * np.outer(n, n) / 256.0
    cn = np.stack([np.cos(ang) / 16.0, np.sin(ang) / 16.0], axis=0)
    return _c(cn.reshape(2, 2, 128, 256).transpose(2, 0, 1, 3)).astype(BF)


def kernel(x, c, ctx, c_ctx, w_ada, b_ada, g_mix, g_ffn, w_in, na_q_g, na_k_g, na_rpb, df_q_g, df_k_g, df_lambda,
           df_subln_g, pool_w, pool_scale, fnet_w, w_branch, w_out, w_router, w_gate_e, w_up_e, w_down_e, _dbg=None):
    f32 = np.float32
    x = np.asarray(x, f32); ctx = np.asarray(ctx, f32)
    B, T, Dm = x.shape
    TC = ctx.shape[1]
    L = w_in.shape[0]
    dbg = _dbg or (lambda *a, **k: None)
    ident_f = np.eye(128, dtype=f32)
    ones_bf = np.ones((128, 128), BF)
    cm_ = cmats(); dftm_ = dft_ch(); fcm, ftw = fft_consts(); ecst = _expert_csts(); cn256 = _fft256_consts()

    c3 = np.stack([c[0], c[1], c_ctx], axis=1).astype(f32)
    cv = _c(c3.reshape(8, 128, 3).transpose(1, 0, 2))
    wall = np.concatenate([w_ada[l] for l in range(L)], axis=1)
    ball = np.concatenate([b_ada[l] for l in range(L)])
    ncol = wall.shape[1] // 8
    nc = _prog(("ada", ncol), lambda: build_ada(ncol))
    res = _run(nc, [dict(wa=_c(wall[:, k * ncol:(k + 1) * ncol]), ba=_c(ball[k * ncol:(k + 1) * ncol].reshape(ncol // 128, 128).T), cv=cv)
                    for k in range(8)])
    modT = np.concatenate([r["modT"].transpose(1, 0, 2).reshape(ncol, 3) for r in res], axis=0)
    mods = [modT[l * 6144:(l + 1) * 6144].T.copy() for l in range(L)]
    dbg("mod", mods)

    def seg(m, j):
        return m[j * 1024:(j + 1) * 1024]

    for l in range(L):
        last = l == L - 1
        lam_init = 0.8 - 0.6 * math.exp(-0.3 * l)
        gq = _c(np.stack([np.tile(na_q_g[l], 2), np.tile(df_q_g[l], 4), np.tile(na_k_g[l], 2), np.tile(df_k_g[l], 4)], axis=1).astype(f32))
        wc = _c(np.concatenate([w_in[l][:, 0:1024], w_in[l][:, 5120:6144]], axis=1))
        lamv = np.tile(df_lambda[l].reshape(1, 128), (128, 1)).astype(f32)
        gsub = np.tile(df_subln_g[l][None], (128, 1)).astype(f32)
        cst2 = np.tile(np.array([[lam_init, 1 - lam_init]], f32), (128, 1))

        def proj(tok_arrays, modrows, tposs, rope_on, ntok, G):
            nc = _prog(("proj", ntok, G), lambda: build_proj(ntok, G))
            maps = []
            for k in range(8):
                m = modrows[k]
                vec = np.concatenate([fm8(seg(m, 0)), fm8(seg(m, 1)), fm8(g_mix[l])], axis=1)
                maps.append(dict(xT=_c(tok_arrays[k].T), w_in=wc, vecs=vec, gq=gq, cmats=cm_, cossin=cossin_table(tposs[k], rope_on), dftm=dftm_))
            return _run(nc, maps)
        NQ = T // 4
        rl_ = proj([x[k // 4, (k % 4) * NQ:(k % 4 + 1) * NQ] for k in range(8)], [mods[l][k // 4] for k in range(8)],
                   [np.arange((k % 4) * NQ, (k % 4 + 1) * NQ) for k in range(8)], True, NQ, 512)
        CQ = TC // 4
        rc_ = proj([ctx[k // 4, (k % 4) * CQ:(k % 4 + 1) * CQ] for k in range(8)], [mods[l][2]] * 8,
                   [np.arange(CQ)] * 8, False, CQ, CQ)
        qkT = [np.concatenate([rl_[b * 4 + q]["qkT"] for q in range(4)], axis=1) for b in range(B)]
        tm = [np.concatenate([rl_[b * 4 + q]["tm"] for q in range(4)], axis=0) for b in range(B)]
        cqkT = [np.concatenate([rc_[b * 4 + q]["qkT"] for q in range(4)], axis=1) for b in range(B)]
        ctm = [np.concatenate([rc_[b * 4 + q]["tm"] for q in range(4)], axis=0) for b in range(B)]
        dbg("proj", l, qkT, tm, cqkT, ctm)

        nc = _prog("na", build_na)
        maps = []
        for k in range(8):
            b, q = k // 4, k % 4
            d = na_core_inputs(na_rpb[l], q * 64, qkT[b][512:768], tm[b][:, 768:1024])
            d.update(qT=_c(qkT[b][0:256, q * NQ:(q + 1) * NQ]), kcT=_c(cqkT[b][512:768]),
                     vc=_c(ctm[b][:, 768:1024].reshape(2, 128, 256).transpose(1, 0, 2)), ident=ident_f)
            maps.append(d)
        r = _run(nc, maps)
        y_na = [np.concatenate([r[b * 4 + q]["y"] for q in range(4)], axis=0) for b in range(B)]

        TK = T + TC
        nc = _prog(("fattn", T, TK, 2, 32, True), lambda: build_fattn(T, TK, 2, 32, True))
        maps = []
        for k in range(8):
            b, h = k // 4, k % 4
            hs = slice(h * 64, (h + 1) * 64)
            kT = np.concatenate([qkT[b][768:1024][hs], cqkT[b][768:1024][hs]], axis=1)
            v = np.concatenate([tm[b][:, 1024:1280][:, hs], ctm[b][:, 1024:1280][:, hs]], axis=0)
            maps.append(dict(qT=_c(qkT[b][256:512][hs]), kT=_c(kT), v=_c(v.reshape(TK // 128, 128, 64).transpose(1, 0, 2)),
                             lamv=lamv, gsub=gsub, cst=cst2, ident=ident_f))
        r = _run(nc, maps)
        y_df = [np.concatenate([r[b * 4 + h]["y"] for h in range(4)], axis=1) for b in range(B)]

        nc = _prog("fft", build_fft)
        maps = []
        for k in range(8):
            b, cg = k // 4, k % 4
            Z = tm[b][:, 256:768]
            maps.append(dict(z=fft_z_layout(Z[:, cg * 64:(cg + 1) * 64], Z[:, 256 + cg * 64:256 + (cg + 1) * 64]), cm=fcm, tw=ftw))
        r = _run(nc, maps)
        fT = [np.concatenate([r[b * 4 + cg]["fT"] for cg in range(4)], axis=0) for b in range(B)]
        dbg("mix", l, y_na, y_df, fT)

        if not last:
            nc = _prog(("fattn", TC, TC, 1, 64, False), lambda: build_fattn(TC, TC, 1, 64, False))
            maps = []
            for k in range(8):
                b, h = k // 4, k % 4
                hs = slice(h * 64, (h + 1) * 64)
                maps.append(dict(qT=_c(cqkT[b][0:256][hs]), kT=_c(cqkT[b][512:768][hs]),
                                 v=_c(ctm[b][:, 768:1024][:, hs].reshape(TC // 128, 128, 64).transpose(1, 0, 2)),
                                 lamv=lamv, gsub=gsub, cst=cst2, ident=ident_f))
            r = _run(nc, maps)
            yc_na = [np.concatenate([r[b * 4 + h]["y"] for h in range(4)], axis=1) for b in range(B)]
            nc = _prog(("fattn", TC, TC, 2, 32, True), lambda: build_fattn(TC, TC, 2, 32, True))
            maps = []
            for k in range(8):
                b, h = k // 4, k % 4
                hs = slice(h * 64, (h + 1) * 64)
                maps.append(dict(qT=_c(cqkT[b][256:512][hs]), kT=_c(cqkT[b][768:1024][hs]),
                                 v=_c(ctm[b][:, 1024:1280][:, hs].reshape(TC // 128, 128, 64).transpose(1, 0, 2)),
                                 lamv=lamv, gsub=gsub, cst=cst2, ident=ident_f))
            r = _run(nc, maps)
            yc_df = [np.concatenate([r[b * 4 + h]["y"] for h in range(4)], axis=1) for b in range(B)]
            nc = _prog("fft256", build_fft256)
            maps = []
            for k in range(8):
                b, cg = k // 4, k % 4
                Z = ctm[b][:, 256:768]
                lay = lambda a: a.reshape(2, 128, 64).transpose(1, 0, 2)
                z = _c(np.stack([lay(Z[:, cg * 64:(cg + 1) * 64]), lay(Z[:, 256 + cg * 64:256 + (cg + 1) * 64])], axis=1))
                maps.append(dict(z=z, cn=cn256))
            r = _run(nc, maps)
            fcT = [np.concatenate([r[b * 4 + cg]["fT"] for cg in range(4)], axis=0) for b in range(B)]
            dbg("cmix", l, yc_na, yc_df, fcT)

        def merge(xs_tok, modrows, yTs, us, bands, ntok, G):
            nc = _prog(("merge", ntok, G), lambda: build_merge(ntok, G))
            maps = []
            for k in range(8):
                m = modrows[k]
                vec = np.concatenate([fm8(seg(m, 0)), fm8(seg(m, 1)), fm8(g_mix[l]), fm8(seg(m, 2)),
                                      fm8(seg(m, 3)), fm8(seg(m, 4)), fm8(g_ffn[l])], axis=1)
                maps.append(dict(xT=_c(xs_tok[k].T), vecs=vec, wg=_c(w_in[l][:, 1024:5120]), wbr=w_branch[l], wo=w_out[l], wr=w_router[l],
                                 fw=fnet_w[l], pw=pool_w[l], psc=_c(pool_scale[l].reshape(4, 64).T), band=bands[k], ones=ones_bf,
                                 yT=yTs[k], u=us[k]))
            return _run(nc, maps)
        sl = lambda q: slice(q * NQ, (q + 1) * NQ)
        r = merge([x[k // 4, sl(k % 4)] for k in range(8)], [mods[l][k // 4] for k in range(8)],
                  [_c(np.stack([y_na[k // 4][sl(k % 4)].T, y_df[k // 4][sl(k % 4)].T, fT[k // 4][:, sl(k % 4)]])) for k in range(8)],
                  [pool_u_layout(tm[k // 4][:, 0:256], (k % 4) * NQ, NQ) for k in range(8)],
                  [pool_bands_for_core((k % 4) * (NQ // 128), NQ // 128, T) for k in range(8)], NQ, 512)
        x_mid = [np.concatenate([r[b * 4 + q]["xmT"].T for q in range(4)], axis=0) for b in range(B)]
        h2 = [np.concatenate([r[b * 4 + q]["h2T"].T for q in range(4)], axis=0) for b in range(B)]
        aff = [np.concatenate([r[b * 4 + q]["aff"] for q in range(4)], axis=0) for b in range(B)]
        dbg("merge", l, x_mid, aff)
        if not last:
            cidx = [(k % 4) // 2 for k in range(8)], [(k % 4) % 2 for k in range(8)]
            csl = lambda t: slice(t * 128, (t + 1) * 128)
            r = merge([ctx[cidx[0][k], csl(cidx[1][k])] for k in range(8)], [mods[l][2]] * 8,
                      [_c(np.stack([yc_na[cidx[0][k]][csl(cidx[1][k])].T, yc_df[cidx[0][k]][csl(cidx[1][k])].T, fcT[cidx[0][k]][:, csl(cidx[1][k])]])) for k in range(8)],
                      [pool_u_layout(ctm[cidx[0][k]][:, 0:256], cidx[1][k] * 128, 128) for k in range(8)],
                      [pool_bands_for_core(cidx[1][k], 1, TC) for k in range(8)], 128, 128)
            c_mid = [np.concatenate([r[b * 2 + t]["xmT"].T for t in range(2)], axis=0) for b in range(B)]
            c_h2 = [np.concatenate([r[b * 2 + t]["h2T"].T for t in range(2)], axis=0) for b in range(B)]
            c_aff = [np.concatenate([r[b * 2 + t]["aff"] for t in range(2)], axis=0) for b in range(B)]
            dbg("cmerge", l, c_mid, c_aff)

        def experts(affs, h2s, NI, CAP):
            nc = _prog(("exp", NI, CAP), lambda: build_experts_full(NI, CAP))
            maps = []
            for k in range(8):
                b, eg = k // 4, k % 4
                perm = np.roll(np.arange(16), -4 * eg)
                es = slice(4 * eg, 4 * eg + 4)
                maps.append(dict(aff=_lay_pie(affs[b][:, perm], NI), h2=_c(h2s[b]), wgate=_c(w_gate_e[l][es]), wup=_c(w_up_e[l][es]),
                                 wdown=_c(w_down_e[l][es]), cst=ecst))
            r = _run(nc, maps)
            pos = [r[b * 4]["pos"] for b in range(B)]
            ye = [np.concatenate([r[b * 4 + eg]["ye"].reshape(4 * CAP, 1024) for eg in range(4)], axis=0) for b in range(B)]
            return pos, ye
        CAP = max(1, 2 * T // 16)
        pos, ye = experts(aff, h2, T // 128, CAP)
        dbg("experts", l, pos, ye)

        def combine(xms, poss, affs, yes, gt2s, ntok, CAPx):
            nc = _prog(("comb", ntok, CAPx), lambda: build_combine(ntok, 16 * CAPx))
            eoff = np.tile((np.arange(16) * CAPx).astype(f32)[None], (128, 1))
            maps = [dict(xm=_c(xms[k]), pos=_c(poss[k]), aff=affs[k], eoff=eoff, gt2=np.tile(gt2s[k][None].astype(f32), (128, 1)), ye=yes[k])
                    for k in range(8)]
            return _run(nc, maps)
        nt = NQ // 128
        r = combine([x_mid[k // 4][sl(k % 4)] for k in range(8)], [pos[k // 4][:, (k % 4) * nt:(k % 4 + 1) * nt, :] for k in range(8)],
                    [_lay_pie(aff[k // 4][sl(k % 4)], nt) for k in range(8)], [ye[k // 4] for k in range(8)],
                    [seg(mods[l][k // 4], 5) for k in range(8)], NQ, CAP)
        x = np.stack([np.concatenate([r[b * 4 + q]["xo"] for q in range(4)], axis=0) for b in range(B)])
        dbg("xout", l, x)
        if not last:
            CAPc = max(1, 2 * TC // 16)
            cpos, cye = experts(c_aff, c_h2, TC // 128, CAPc)
            r = combine([c_mid[cidx[0][k]][csl(cidx[1][k])] for k in range(8)], [cpos[cidx[0][k]][:, cidx[1][k]:cidx[1][k] + 1, :] for k in range(8)],
                        [_lay_pie(c_aff[cidx[0][k]][csl(cidx[1][k])], 1) for k in range(8)], [cye[cidx[0][k]] for k in range(8)],
                        [seg(mods[l][2], 5)] * 8, 128, CAPc)
            ctx = np.stack([np.concatenate([r[b * 2 + t]["xo"] for t in range(2)], axis=0) for b in range(B)])
            dbg("ctxout", l, ctx)
    return x.astype(np.float32)
```

```python
import numpy as np
from contextlib import ExitStack
import concourse.bass as bass
import concourse.mybir as mybir
from concourse.bass_utils import run_bass_kernel_spmd

F32 = mybir.dt.float32
BF16 = mybir.dt.bfloat16
I32 = mybir.dt.int32
AF = mybir.ActivationFunctionType
ALU = mybir.AluOpType
AX = mybir.AxisListType

ENGS = ["tensor", "vector", "scalar", "gpsimd", "sync"]
DMA_RING = 6


class Buf:
    _n = 0

    def __init__(self, name, t=None):
        Buf._n += 1
        self.id = Buf._n
        self.name = name
        self.t = t

    def __getitem__(self, idx):
        return self.t[idx]

    def sub(self, s):
        return (self, s)


class Op:
    __slots__ = ("eng", "fn", "reads", "writes", "dma", "idx", "sig", "waits", "need_sig", "prewait")

    def __init__(self, eng, fn, reads, writes, dma):
        self.eng, self.fn, self.reads, self.writes, self.dma = eng, fn, reads, writes, dma
        self.sig = None
        self.waits = []
        self.need_sig = False
        self.prewait = None


def _norm(keys):
    out = []
    for k in keys:
        if isinstance(k, Buf):
            out.append((k.id, None))
        else:
            out.append((k[0].id, k[1]))
    return out


class Prog:
    def __init__(self, nc):
        self.nc = nc
        self.ops = []
        self.stack = ExitStack()

    def sb(self, name, shape, dtype):
        t = self.stack.enter_context(self.nc.sbuf_tensor("sb_" + name, list(shape), dtype))
        return Buf(name, t)

    def ps(self, name, shape, dtype=F32):
        t = self.stack.enter_context(self.nc.psum_tensor("ps_" + name, list(shape), dtype))
        return Buf(name, t)

    def op(self, eng, fn, reads=(), writes=(), dma=False):
        o = Op(eng, fn, _norm(reads), _norm(writes), dma)
        o.idx = len(self.ops)
        self.ops.append(o)
        return o

    def dma(self, eng, out, in_, reads=(), writes=(), **kw):
        return self.op(eng, lambda e: e.dma_start(out=out, in_=in_, **kw), reads, writes, dma=True)

    def finalize(self):
        nc = self.nc
        ops = self.ops
        last_w = {}
        readers = {}
        by_buf = {}

        def related(key):
            b, s = key
            subs = by_buf.setdefault(b, set())
            subs.add(s)
            if s is None:
                return [(b, x) for x in subs]
            return [(b, s), (b, None)] if None in subs else [(b, s)]

        deps = [None] * len(ops)
        for o in ops:
            d = {}
            for k in o.reads:
                for r in related(k):
                    w = last_w.get(r)
                    if w is not None:
                        d[w] = True
            for k in o.writes:
                for r in related(k):
                    w = last_w.get(r)
                    if w is not None:
                        d.setdefault(w, False)
                    for rd in readers.get(r, ()):
                        d.setdefault(rd, False)
            d.pop(o.idx, None)
            deps[o.idx] = d
            for k in o.reads:
                readers.setdefault(k, []).append(o.idx)
            for k in o.writes:
                last_w[k] = o.idx
                readers[k] = []
                if k[1] is None:
                    for s in by_buf.get(k[0], ()):
                        if s is not None:
                            last_w[(k[0], s)] = o.idx
                            readers[(k[0], s)] = []

        for o in ops:
            for j, raw in deps[o.idx].items():
                p = ops[j]
                if p.dma:
                    p.need_sig = True
                elif p.eng != o.eng:
                    p.need_sig = True
                elif p.eng != "tensor" or o.dma:
                    p.need_sig = True
        LIM = 30000
        DLIM = 1800
        sems = {e: [] for e in ENGS}
        rings = {e: [[] for _ in range(DMA_RING)] for e in ("sync", "scalar", "gpsimd")}
        cnt = {e: 0 for e in ENGS}
        dcnt = {e: 0 for e in rings}
        self.nsem = 0
        def newsem(tag):
            self.nsem += 1
            return self.stack.enter_context(nc.semaphore("%s_%d" % (tag, self.nsem)))
        def dsig(e, n):
            r, u = n % DMA_RING, n // DMA_RING
            lst = rings[e][r]
            while len(lst) <= u // DLIM:
                lst.append(newsem("d" + e))
            return (lst[u // DLIM], 16 * (u % DLIM + 1), 16)
        last_dma = {e: {} for e in rings}
        for o in ops:
            if o.dma:
                n = dcnt[o.eng]
                dcnt[o.eng] += 1
                o.prewait = dsig(o.eng, n - DMA_RING)[:2] if n >= DMA_RING else None
                o.sig = dsig(o.eng, n)
                last_dma[o.eng][n % DMA_RING] = o.sig
            elif o.need_sig:
                n = cnt[o.eng]
                cnt[o.eng] += 1
                lst = sems[o.eng]
                while len(lst) <= n // LIM:
                    lst.append(newsem("s" + o.eng))
                o.sig = (lst[n // LIM], n % LIM + 1, 1)
        known = {e: {} for e in ENGS}
        for o in ops:
            kn = known[o.eng]
            need = {}
            if o.prewait is not None:
                need[o.prewait[0]] = o.prewait[1]
            for j, raw in deps[o.idx].items():
                p = ops[j]
                if not p.dma and not o.dma and p.eng == o.eng and p.eng == "tensor":
                    continue
                sem, val, _ = p.sig
                if need.get(sem, 0) < val:
                    need[sem] = val
            for sem, val in need.items():
                if kn.get(sem, 0) < val:
                    kn[sem] = val
                    o.waits.append((sem, val))
        finals = {e: [(sg[0], sg[1]) for sg in last_dma[e].values()] for e in rings}
        self.nsig = dict(cnt)
        self.ndma = dict(dcnt)

        with nc.Block() as block:
            def mk(ename):
                def body(eng):
                    for o in ops:
                        if o.eng != ename:
                            continue
                        for sem, val in o.waits:
                            eng.wait_ge(sem, val)
                        ins = o.fn(eng)
                        if o.sig is not None:
                            ins.then_inc(o.sig[0], o.sig[2])
                    if ename in finals:
                        for sem, val in finals[ename]:
                            eng.wait_ge(sem, val)
                return body
            block.tensor(mk("tensor"))
            block.vector(mk("vector"))
            block.scalar(mk("scalar"))
            block.gpsimd(mk("gpsimd"))
            block.sync(mk("sync"))
        self.stack.close()

import numpy as np, ml_dtypes
BF = ml_dtypes.bfloat16
def fm8(v):
    return np.ascontiguousarray(v.reshape(8, 128).T)
def rope_tables(pos, dim=16, base=10000.0):
    half = dim // 2
    inv = base ** (-np.arange(half, dtype=np.float32) / half)
    ang = pos.astype(np.float32)[:, None] * inv[None, :]
    return np.cos(ang), np.sin(ang)
def cossin_table(tpos, rope_on=True):
    n = len(tpos)
    out = np.zeros((128, 2, n), np.float32)
    if not rope_on:
        out[:, 0, :] = 1.0
        return out
    cr, sr = rope_tables(tpos // 64); cc, sc = rope_tables(tpos % 64)
    for p in range(128):
        i = p % 32
        if i < 16:
            out[p, 0] = cr[:, i % 8]; out[p, 1] = sr[:, i % 8]
        else:
            out[p, 0] = cc[:, i % 8]; out[p, 1] = sc[:, i % 8]
    return out
def cmats():
    m = np.zeros((128, 4, 128), np.float32)
    m[:, 0, :] = 1.0
    for p in range(128):
        m[p, 1, (p // 64) * 64:(p // 64 + 1) * 64] = 1.0
        m[p, 2, (p // 32) * 32:(p // 32 + 1) * 32] = 1.0
    Pm = np.zeros((128, 128), np.float32)
    for blk in range(8):
        o = blk * 16
        for i in range(8):
            Pm[o + i, o + i + 8] = -1.0
            Pm[o + i + 8, o + i] = 1.0
    m[:, 3, :] = Pm.T
    return m
def dft_ch():
    c = np.arange(256)[:, None].astype(np.float64); mm = np.arange(256)[None, :].astype(np.float64)
    ang = 2 * np.pi * c * mm / 256
    return (np.concatenate([np.cos(ang), -np.sin(ang)], axis=1) / 16.0).astype(np.float32)

def na_bias_table(rpb, J):
    kb = int(np.clip(2 * J - 4, 0, 246))
    kr = np.arange(10)[:, None, None, None] + kb
    kc = np.arange(64)[None, :, None, None]
    qr = np.arange(2)[None, None, :, None] + 2 * J
    qc = np.arange(64)[None, None, None, :]
    rs = np.clip(qr - 4, 0, 248); cs = np.clip(qc - 8, 0, 48)
    valid = (kr >= rs) & (kr < rs + 8) & (kc >= cs) & (kc < cs + 16)
    ri = np.clip(kr - qr + 7, 0, 14); cidx = np.clip(kc - qc + 15, 0, 30)
    ri, cidx, valid = np.broadcast_arrays(ri, cidx, valid)
    tab = rpb[:, ri, cidx]
    tab = np.where(valid[None], tab, np.float32(-30000.0)).astype(np.float32)
    return kb, tab.reshape(4, 640, 128)

def na_core_inputs(rpb, R0, kT_full, v_full):
    J0 = R0 // 2
    kT = np.zeros((256, 74 * 64), kT_full.dtype); v = np.zeros((74 * 64, 256), v_full.dtype)
    g0 = R0 - 4
    lo, hi = max(g0, 0), min(g0 + 74, 256)
    kT[:, (lo - g0) * 64:(hi - g0) * 64] = kT_full[:, lo * 64:hi * 64]
    v[(lo - g0) * 64:(hi - g0) * 64] = v_full[lo * 64:hi * 64]
    ks = np.zeros((4, 256, 640), kT_full.dtype); vs = np.zeros((4, 640, 256), v_full.dtype)
    bias = np.zeros((128, 5, 4, 5, 128), np.float32)
    for var, j in enumerate((0, 1, 2, 30, 31)):
        kb, tab = na_bias_table(rpb, J0 + j)
        bias[:, var] = tab.reshape(4, 5, 128, 128).transpose(2, 0, 1, 3)
        if j != 2:
            sp = {0: 0, 1: 1, 30: 2, 31: 3}[j]
            ks[sp] = kT_full[:, kb * 64:(kb + 10) * 64]; vs[sp] = v_full[kb * 64:(kb + 10) * 64]
    v = np.ascontiguousarray(v.reshape(37, 128, 256).transpose(1, 0, 2))
    vs = np.ascontiguousarray(vs.reshape(4, 5, 128, 256).transpose(2, 0, 1, 3))
    return dict(kT=kT, v=v, ks=ks, vs=vs, bias=bias)

def fft_consts():
    i = np.arange(128, dtype=np.float64)
    a1 = 2 * np.pi * np.outer(i, i) / 128.0
    C1, S1 = np.cos(a1), np.sin(a1)
    cm = np.stack([C1.T, S1.T, -S1.T, C1.T / 128.0, S1.T / 128.0], axis=1)
    at = 2 * np.pi * np.outer(i, i) / 16384.0
    tw = np.stack([np.tile(np.cos(at)[:, None, :], (1, 4, 1)), np.tile(np.sin(at)[:, None, :], (1, 4, 1))], axis=1)
    return cm.astype(BF), tw.astype(np.float32)
def fft_z_layout(zr, zi):
    CH = zr.shape[1]
    f = lambda a: a.reshape(128, 128, CH).transpose(0, 2, 1)
    return np.ascontiguousarray(np.stack([f(zr), f(zi)], axis=1))

POOL_WINDOWS = (2, 4, 8, 16)
def pool_band(gt, N):
    out = np.zeros((128, 4, 3, 128), np.float32)
    for gi, w in enumerate(POOL_WINDOWS):
        for to in range(128):
            t = gt * 128 + to
            lo = min(max(t - w // 2, 0), N); hi = min(max(t + w // 2, 0), N)
            cnt = hi - lo
            for ti in range(lo, hi):
                j = ti // 128 - gt + 1
                out[ti % 128, gi, j, to] += 1.0 / cnt
            out[to, gi, 1, to] -= 1.0
    return out
def pool_bands_for_core(gt_first, nt_local, N):
    b = np.zeros((128, 3, 4, 3, 128), np.float32)
    b[:, 0] = pool_band(gt_first, N)
    if nt_local > 1:
        b[:, 1] = pool_band(gt_first + 1, N) if nt_local > 2 else 0
        b[:, 2] = pool_band(gt_first + nt_local - 1, N)
    return b.astype(BF)
def pool_u_layout(u_full, t0, ntok):
    N = u_full.shape[0]
    nt = ntok // 128
    buf = np.zeros(((nt + 2) * 128, 256), u_full.dtype)
    lo, hi = max(t0 - 128, 0), min(t0 + ntok + 128, N)
    buf[lo - (t0 - 128):hi - (t0 - 128)] = u_full[lo:hi]
    return np.ascontiguousarray(buf.reshape(nt + 2, 128, 256).transpose(1, 0, 2))


EPS = 1e-6
C_NAQ, C_DFQ, C_POOL, C_FNET, C_GATE, C_NAK, C_NAV, C_DFK, C_DFV = 0, 256, 512, 768, 1024, 5120, 5376, 5632, 5888


def build_proj(ntok, G):
    NG = ntok // G
    TT = max(1, G // 128)
    TM = min(G, 128)
    nc = bass.Bass("TRN2", target_bir_lowering=False)
    D = lambda n, s, dt, k: nc.dram_tensor(n, list(s), dt, kind=k).ap()
    xT = D("xT", [1024, ntok], F32, "ExternalInput")
    w_in = D("w_in", [1024, 2048], F32, "ExternalInput")
    vecs = D("vecs", [128, 24], F32, "ExternalInput")
    gq = D("gq", [128, 4], F32, "ExternalInput")
    cmats = D("cmats", [128, 4, 128], F32, "ExternalInput")
    cossin = D("cossin", [128, 2, ntok], F32, "ExternalInput")
    dftm = D("dftm", [256, 512], F32, "ExternalInput")
    qkT = D("qkT", [1024, ntok], BF16, "ExternalOutput")
    tm = D("tm", [ntok, 1280], BF16, "ExternalOutput")

    P = Prog(nc)
    hT = P.sb("hT", [128, 8, ntok], BF16)
    xs = [P.sb("xs%d" % i, [128, 8, G], F32) for i in range(2)]
    sq = P.sb("sq", [128, 8, G], BF16)
    tmpf = [P.sb("tmpf%d" % i, [128, G], F32) for i in range(2)]
    rstd = P.sb("rstd", [128, G], F32)
    sd = P.sb("sd", [128, G], F32)
    vec_sb = P.sb("vec_sb", [128, 24], F32)
    A_sb = P.sb("A_sb", [128, 8], F32)
    gq_sb = P.sb("gq_sb", [128, 4], F32)
    cm_f = P.sb("cm_f", [128, 4, 128], F32)
    cm = P.sb("cm", [128, 4, 128], BF16)
    dft_f = P.sb("dft_f", [128, 2, 512], F32)
    dft = P.sb("dft", [128, 2, 512], BF16)
    css = [P.sb("cs%d" % i, [128, 2, G], F32) for i in range(2)]
    wst = xs if G == 512 else [P.sb("wst%d" % i, [128, 8, 512], F32) for i in range(2)]
    wb = [P.sb("wb%d" % i, [128, 8, 512], BF16) for i in range(2)]
    outs = [P.sb("o%d" % i, [128, 512], BF16) for i in range(4)]
    sqh = P.sb("sqh", [128, G], BF16)
    xn = P.sb("xn", [128, G], BF16)
    t1 = P.sb("t1", [128, G], F32)
    t2 = P.sb("t2", [128, G], F32)
    uT = P.sb("uT", [128, 2, G], BF16)
    pm = [P.ps("pm%d" % i, [128, 512]) for i in range(3)]
    pst = P.ps("pst", [128, 512])
    prot = P.ps("prot", [128, 512])
    epsb = P.sb("epsb", [128, 1], F32)

    P.dma("sync", vec_sb[:], vecs, writes=[vec_sb])
    P.dma("sync", gq_sb[:], gq, writes=[gq_sb])
    P.dma("sync", cm_f[:], cmats, writes=[cm_f])
    P.dma("sync", dft_f[:], dftm.rearrange("(k p) n -> p k n", p=128), writes=[dft_f])
    P.op("vector", lambda e: e.tensor_copy(cm[:], cm_f[:]), [cm_f], [cm])
    P.op("vector", lambda e: e.tensor_copy(dft[:], dft_f[:]), [dft_f], [dft])
    P.op("vector", lambda e: e.memset(epsb[:], EPS), [], [epsb])
    P.op("vector", lambda e: e.tensor_scalar(A_sb[:], vec_sb[:, 8:16], 1.0, None, ALU.add), [vec_sb], [A_sb])
    P.op("vector", lambda e: e.tensor_tensor(A_sb[:], A_sb[:], vec_sb[:, 16:24], ALU.mult), [A_sb, vec_sb], [A_sb])

    def load_x(g):
        b = xs[g % 2]
        P.dma("sync", b[:], xT[:, g * G:(g + 1) * G].rearrange("(k p) t -> p k t", p=128), writes=[b])
    load_x(0)
    for g in range(NG):
        if g + 1 < NG:
            load_x(g + 1)
        b = xs[g % 2]
        P.op("scalar", lambda e, b=b: e.activation(sq[:], b[:], AF.Square), [b], [sq])
        for k in range(8):
            P.op("tensor", lambda e, k=k: e.matmul(pst[:, :G], cm[:, 0, :], sq[:, k, :], start=(k == 0), stop=(k == 7)),
                 [cm, sq], [pst])
        P.op("scalar", lambda e: e.activation(sd[:], pst[:, :G], AF.Sqrt, bias=epsb[:], scale=1.0 / 1024), [pst, epsb], [sd])
        P.op("vector", lambda e: e.reciprocal(rstd[:], sd[:]), [sd], [rstd])
        for k in range(8):
            tf = tmpf[k % 2]
            P.op("vector", lambda e, k=k, tf=tf, b=b: e.scalar_tensor_tensor(tf[:], b[:, k, :], A_sb[:, k:k + 1], rstd[:], ALU.mult, ALU.mult),
                 [b, A_sb, rstd], [tf])
            P.op("scalar", lambda e, k=k, tf=tf, g=g: e.activation(hT[:, k, g * G:(g + 1) * G], tf[:], AF.Identity, bias=vec_sb[:, k:k + 1], scale=1.0),
                 [tf, vec_sb], [hT.sub(g)])

    NB = 12
    def load_w(cb):
        st = wst[cb % 2]
        P.dma("sync", st[:], w_in[:, cb * 512:(cb + 1) * 512].rearrange("(k p) n -> p k n", p=128), writes=[st])
    def cast_w(cb):
        st, w = wst[cb % 2], wb[cb % 2]
        for k in range(8):
            eng = ("vector", "gpsimd")[k % 2]
            P.op(eng, lambda e, k=k, st=st, w=w: e.tensor_copy(w[:, k, :], st[:, k, :]), [st], [w.sub(k)])
    CBL = [0, 1, 2, 3]
    load_w(CBL[0])
    cast_w(CBL[0])
    cnt = {"pm": 0, "o": 0, "st": 0, "cs": 0}
    def store(dst, src, rd):
        eng = ("gpsimd", "sync")[cnt["st"] % 2]
        cnt["st"] += 1
        P.dma(eng, dst, src, reads=rd)

    def fm_chunk(w, jj, g, kind, row0):
        ps = pm[cnt["pm"] % 3]; cnt["pm"] += 1
        ts = slice(g * G, (g + 1) * G)
        for k in range(8):
            P.op("tensor", lambda e, k=k, ps=ps: e.matmul(ps[:, :G], w[:, k, jj * 128:(jj + 1) * 128], hT[:, k, ts], start=(k == 0), stop=(k == 7)),
                 [w.sub(k), hT.sub(g)], [ps])
        if kind == "gate":
            o = outs[cnt["o"] % 4]; cnt["o"] += 1
            P.op("scalar", lambda e: e.activation(o[:, :G], ps[:, :G], AF.Sigmoid), [ps], [o])
            store(gT[row0:row0 + 128, ts], o[:, :G], [o])
            return
        if kind == "fnet":
            c = jj % 2
            P.op("scalar", lambda e: e.copy(uT[:, c, :], ps[:, :G]), [ps], [uT.sub(c)])
            return
        hd, ci, gcol = {"naq": (64, 1, 0), "nak": (64, 1, 2), "dfq": (32, 2, 1), "dfk": (32, 2, 3)}[kind]
        P.op("scalar", lambda e: e.activation(sqh[:], ps[:, :G], AF.Square), [ps], [sqh])
        P.op("tensor", lambda e: e.matmul(pst[:, :G], cm[:, ci, :], sqh[:], start=True, stop=True), [cm, sqh], [pst])
        P.op("scalar", lambda e: e.activation(sd[:], pst[:, :G], AF.Sqrt, bias=epsb[:], scale=1.0 / hd), [pst, epsb], [sd])
        P.op("vector", lambda e: e.reciprocal(rstd[:], sd[:]), [sd], [rstd])
        o = outs[cnt["o"] % 4]; cnt["o"] += 1
        if kind in ("naq", "nak"):
            P.op("vector", lambda e: e.scalar_tensor_tensor(o[:, :G], ps[:, :G], gq_sb[:, gcol:gcol + 1], rstd[:], ALU.mult, ALU.mult),
                 [ps, gq_sb, rstd], [o])
        else:
            cs = css[cnt["cs"] % 2]; cnt["cs"] += 1
            P.dma("sync", cs[:], cossin[:, :, ts], writes=[cs])
            P.op("vector", lambda e: e.scalar_tensor_tensor(xn[:], ps[:, :G], gq_sb[:, gcol:gcol + 1], rstd[:], ALU.mult, ALU.mult),
                 [ps, gq_sb, rstd], [xn])
            P.op("tensor", lambda e: e.matmul(prot[:, :G], cm[:, 3, :], xn[:], start=True, stop=True), [cm, xn], [prot])
            P.op("vector", lambda e: e.tensor_tensor(t1[:], xn[:], cs[:, 0, :], ALU.mult), [xn, cs], [t1])
            P.op("vector", lambda e: e.tensor_tensor(t2[:], prot[:, :G], cs[:, 1, :], ALU.mult), [prot, cs], [t2])
            P.op("gpsimd", lambda e: e.tensor_tensor(o[:, :G], t1[:], t2[:], ALU.add), [t1, t2], [o])
        store(qkT[row0:row0 + 128, ts], o[:, :G], [o])

    def tm_block(w, c0, ncols, g, dcol):
        for tt in range(TT):
            ps = pm[cnt["pm"] % 3]; cnt["pm"] += 1
            t0 = g * G + tt * TM
            for k in range(8):
                P.op("tensor", lambda e, k=k, ps=ps, t0=t0: e.matmul(ps[:TM, :ncols], hT[:, k, t0:t0 + TM], w[:, k, c0:c0 + ncols], start=(k == 0), stop=(k == 7)),
                     [w.sub(k), hT.sub(g)], [ps])
            o = outs[cnt["o"] % 4]; cnt["o"] += 1
            P.op("vector", lambda e, ps=ps, o=o: e.tensor_copy(o[:TM, :ncols], ps[:TM, :ncols]), [ps], [o])
            store(tm[t0:t0 + TM, dcol:dcol + ncols], o[:TM, :ncols], [o])

    def fft_ab(g):
        for tt in range(TT):
            ps = pm[cnt["pm"] % 3]; cnt["pm"] += 1
            t0 = tt * TM
            for c in range(2):
                P.op("tensor", lambda e, c=c, ps=ps, t0=t0: e.matmul(ps[:TM, :], uT[:, c, t0:t0 + TM], dft[:, c, :], start=(c == 0), stop=(c == 1)),
                     [uT, dft], [ps])
            o = outs[cnt["o"] % 4]; cnt["o"] += 1
            P.op("vector", lambda e, ps=ps, o=o: e.tensor_copy(o[:TM, :], ps[:TM, :]), [ps], [o])
            store(tm[g * G + t0:g * G + t0 + TM, 256:768], o[:TM, :], [o])

    for ci_, cb in enumerate(CBL):
        if ci_ + 1 < len(CBL):
            load_w(CBL[ci_ + 1])
        w = wb[cb % 2]
        c0 = cb * 512
        for g in range(NG):
            if c0 == 0:
                for jj in range(4):
                    fm_chunk(w, jj, g, "naq" if jj < 2 else "dfq", jj * 128)
            elif c0 == 512:
                tm_block(w, 0, 256, g, 0)
                for jj in (2, 3):
                    fm_chunk(w, jj, g, "fnet", 0)
                fft_ab(g)
            elif c0 == 1024:
                for jj in range(2):
                    fm_chunk(w, jj, g, "nak", 512 + jj * 128)
                tm_block(w, 256, 256, g, 768)
            else:
                for jj in range(2):
                    fm_chunk(w, jj, g, "dfk", 768 + jj * 128)
                tm_block(w, 256, 256, g, 1024)
        if ci_ + 1 < len(CBL):
            cast_w(CBL[ci_ + 1])
    P.finalize()
    return nc


NKR = 74
NKT = NKR * 64 // 128


def var_of_j(j):
    return {0: 0, 1: 1, 30: 3, 31: 4}.get(j, 2)


def emit_na(P, nc, pre, qT_d, kT_d, v_d, ks_d, vs_d, kcT_d, vc_d, bias_d, ident_d, y_d):
    qT = P.sb(pre + "qT", [128, 2, 4096], BF16)
    kT = P.sb(pre + "kT", [128, 2, NKT * 128], BF16)
    va = P.sb(pre + "va", [128, NKT, 4, 128], BF16)
    kTs = P.sb(pre + "kTs", [128, 2, 4, 640], BF16)
    vas = P.sb(pre + "vas", [128, 4, 5, 4, 128], BF16)
    kcT = P.sb(pre + "kcT", [128, 2, 256], BF16)
    vca = P.sb(pre + "vca", [128, 2, 4, 128], BF16)
    bias = P.sb(pre + "bias", [128, 5, 4, 5, 128], F32)
    ident = P.sb(pre + "ident", [128, 128], F32)
    ts_ = [P.sb(pre + "t%d" % i, [128, 640], F32) for i in range(2)]
    pts = [P.sb(pre + "pt%d" % i, [128, 896], BF16) for i in range(2)]
    accs = [P.sb(pre + "accs%d" % i, [128, 512], F32) for i in range(2)]
    rl = P.sb(pre + "rl", [128, 4], F32)
    yo = [P.sb(pre + "yo%d" % i, [128, 256], BF16) for i in range(2)]
    pss = [P.ps(pre + "pss%d" % i, [128, 1024]) for i in range(2)]
    pacc = [P.ps(pre + "pacc%d" % i, [128, 512]) for i in range(2)]
    ptr = P.ps(pre + "ptr", [128, 512])

    P.dma("sync", qT[:], qT_d.rearrange("(c p) t -> p c t", p=128), writes=[qT])
    P.dma("sync", kT[:], kT_d.rearrange("(c p) t -> p c t", p=128), writes=[kT])
    P.dma("sync", kcT[:], kcT_d.rearrange("(c p) t -> p c t", p=128), writes=[kcT])
    vst = P.sb(pre + "vst", [128, NKT, 256], BF16)
    vcst = P.sb(pre + "vcst", [128, 2, 256], BF16)
    vsst = P.sb(pre + "vsst", [128, 4, 5, 256], BF16)
    P.dma("gpsimd", vst[:], v_d, writes=[vst])
    P.dma("gpsimd", vcst[:], vc_d, writes=[vcst])
    P.dma("gpsimd", vsst[:], vs_d, writes=[vsst])
    for h in range(4):
        eng = ("vector", "gpsimd")[h % 2]
        P.op(eng, lambda e, h=h: e.tensor_copy(va[:, :, h, 0:64], vst[:, :, h * 64:(h + 1) * 64]), [vst], [va.sub("v%d" % h)])
        P.op(eng, lambda e, h=h: e.tensor_copy(vca[:, :, h, 0:64], vcst[:, :, h * 64:(h + 1) * 64]), [vcst], [vca.sub("v%d" % h)])
        P.op(eng, lambda e, h=h: e.tensor_copy(vas[:, :, :, h, 0:64].rearrange("p a b d -> p (a b) d"), vsst[:, :, :, h * 64:(h + 1) * 64].rearrange("p a b d -> p (a b) d")), [vsst], [vas.sub("v%d" % h)])
    P.op("gpsimd", lambda e: e.memset(va[:, :, :, 64:128], 1.0), [], [va.sub("o")])
    P.op("gpsimd", lambda e: e.memset(vca[:, :, :, 64:128], 1.0), [], [vca.sub("o")])
    for sp in range(4):
        P.dma("sync", kTs[:, :, sp, :], ks_d[sp].rearrange("(c p) t -> p c t", p=128), writes=[kTs.sub(sp)])
    P.op("gpsimd", lambda e: e.memset(vas[:, :, :, :, 64:128].rearrange("p a b c d -> p (a b c) d"), 1.0), [], [vas.sub("o")])
    P.dma("sync", bias[:], bias_d, writes=[bias])
    P.dma("sync", ident[:], ident_d, writes=[ident])

    def s_part(j, h, ps, t, pt, pa):
        var = var_of_j(j)
        sp = {0: 0, 1: 1, 30: 2, 31: 3}.get(j)
        kt0 = j
        c, p0 = h // 2, (h % 2) * 64
        for i in range(5):
            if sp is None:
                kl = lambda i=i: kT[p0:p0 + 64, c, (kt0 + i) * 128:(kt0 + i + 1) * 128]
            else:
                kl = lambda i=i: kTs[p0:p0 + 64, c, sp, i * 128:(i + 1) * 128]
            P.op("tensor", lambda e, i=i, kl=kl: e.matmul(ps[:, i * 128:(i + 1) * 128], kl(), qT[p0:p0 + 64, c, j * 128:(j + 1) * 128], start=True, stop=True),
                 [kT, kTs, qT], [ps])
        for i in range(2):
            P.op("tensor", lambda e, i=i: e.matmul(ps[:, 640 + i * 128:640 + (i + 1) * 128], kcT[p0:p0 + 64, c, i * 128:(i + 1) * 128],
                                                   qT[p0:p0 + 64, c, j * 128:(j + 1) * 128], start=True, stop=True), [kcT, qT], [ps])

    def av_part(j, h, ps, t, pt, pa):
        var = var_of_j(j)
        sp = {0: 0, 1: 1, 30: 2, 31: 3}.get(j)
        kt0 = j
        P.op("vector", lambda e: e.scalar_tensor_tensor(t[:], ps[:, 0:640], 0.125, bias[:, var, h, :, :].rearrange("p a b -> p (a b)"), ALU.mult, ALU.add),
             [ps, bias], [t])
        P.op("scalar", lambda e: e.activation(pt[:, 0:640], t[:], AF.Exp), [t], [pt.sub(0)])
        P.op("scalar", lambda e: e.activation(pt[:, 640:896], ps[:, 640:896], AF.Exp, scale=0.125), [ps], [pt.sub(1)])
        for i in range(7):
            if i < 5 and sp is not None:
                lhs = lambda i=i: vas[:, sp, i, h, :]
            elif i < 5:
                lhs = lambda i=i: va[:, kt0 + i, h, :]
            else:
                lhs = lambda i=i: vca[:, i - 5, h, :]
            P.op("tensor", lambda e, lhs=lhs, i=i: e.matmul(pa[:, h * 128:(h + 1) * 128], lhs(), pt[:, i * 128:(i + 1) * 128], start=(i == 0), stop=(i == 6)),
                 [va, vas, vca, pt], [pa.sub(h)])

    def post_part(j, pa, ac, y):
        P.op("scalar", lambda e: e.copy(ac[:], pa[:]), [pa], [ac])
        for h in range(4):
            P.op("tensor", lambda e, h=h: e.transpose(ptr[:, h * 128:(h + 1) * 128], ac[:, h * 128:(h + 1) * 128], ident[:]), [ac, ident], [ptr])
        for h in range(4):
            P.op("vector", lambda e, h=h: e.reciprocal(rl[:, h:h + 1], ptr[:, h * 128 + 64:h * 128 + 65]), [ptr], [rl.sub(h)])
            P.op("vector", lambda e, h=h: e.tensor_scalar(y[:, h * 64:(h + 1) * 64], ptr[:, h * 128:h * 128 + 64], rl[:, h:h + 1], None, ALU.mult),
                 [ptr, rl.sub(h)], [y.sub(h)])
        P.dma("sync", y_d[j * 128:(j + 1) * 128, :], y[:], reads=[y])

    its = []
    for j in range(32):
        for h in range(4):
            k_ = len(its)
            its.append((j, h, pss[k_ % 2], ts_[k_ % 2], pts[k_ % 2], pacc[j % 2]))
    s_part(*its[0])
    for k_, it in enumerate(its):
        if k_ + 1 < len(its):
            s_part(*its[k_ + 1])
        av_part(*it)
        if it[1] == 3:
            post_part(it[0], it[5], accs[it[0] % 2], yo[it[0] % 2])


def build_na():
    nc = bass.Bass("TRN2", target_bir_lowering=False)
    D = lambda n, s, dt, k: nc.dram_tensor(n, list(s), dt, kind=k).ap()
    qT = D("qT", [256, 4096], BF16, "ExternalInput")
    kT = D("kT", [256, NKT * 128], BF16, "ExternalInput")
    v = D("v", [128, NKT, 256], BF16, "ExternalInput")
    ks = D("ks", [4, 256, 640], BF16, "ExternalInput")
    vs = D("vs", [128, 4, 5, 256], BF16, "ExternalInput")
    kcT = D("kcT", [256, 256], BF16, "ExternalInput")
    vc = D("vc", [128, 2, 256], BF16, "ExternalInput")
    bias = D("bias", [128, 5, 4, 5, 128], F32, "ExternalInput")
    ident = D("ident", [128, 128], F32, "ExternalInput")
    y = D("y", [4096, 256], BF16, "ExternalOutput")
    P = Prog(nc)
    emit_na(P, nc, "n_", qT, kT, v, ks, vs, kcT, vc, bias, ident, y)
    P.finalize()
    return nc


EPS = 1e-6


import os
EXPT = os.environ.get('K4EXPT', '')

def emit_fattn(P, nc, pre, Tq, Tk, nmaps, dk, diff, qT_d, kT_d, v_d, lamv_d, gsub_d, cst_d, ident_d, y_d):
    KT = Tk // 128
    QG = min(512, Tq)
    NQ = Tq // QG
    TT = QG // 128
    R = nmaps * dk
    scale = float(dk) ** -0.5
    qT = P.sb(pre + "qT", [128, nmaps, Tq], BF16)
    kT = P.sb(pre + "kT", [128, Tk], BF16)
    va = P.sb(pre + "va", [128, KT, 128], BF16)
    ident = P.sb(pre + "ident", [128, 128], F32)
    lamv = P.sb(pre + "lamv", [128, 128], F32)
    gsub = P.sb(pre + "gsub", [128, 64], F32)
    cst = P.sb(pre + "cst", [128, 2], F32)
    gsc = P.sb(pre + "gsc", [128, 64], F32)
    sm = P.sb(pre + "sm", [128, 8], F32)
    prod = P.sb(pre + "prod", [128, 64], F32)
    epsb = P.sb(pre + "epsb", [128, 1], F32)
    pts = [P.sb(pre + "pt%d" % i, [128, nmaps * 512], BF16) for i in range(3)]
    accs = [P.sb(pre + "accs%d" % m, [128, QG], F32) for m in range(nmaps)]
    om = [P.sb(pre + "om%d" % m, [128, 64], F32) for m in range(2)]
    rl = P.sb(pre + "rl", [128, 2], F32)
    ss = P.sb(pre + "ss", [128, 2], F32)
    junk = P.sb(pre + "junk", [128, 64], F32)
    yo = [P.sb(pre + "yo%d" % i, [128, 64], BF16) for i in range(2)]
    pss = [P.ps(pre + "pss%d" % i, [128, nmaps * 512]) for i in range(3)]
    pacc = [P.ps(pre + "pacc%d" % m, [128, 512]) for m in range(nmaps)]

    P.op("gpsimd", lambda e: e.memset(kT[:], 0.0), [], [kT])
    P.op("vector", lambda e: e.memset(qT[:], 0.0), [], [qT])
    P.dma("sync", kT[0:R, :], kT_d, writes=[kT])
    for m in range(nmaps):
        P.dma("sync", qT[m * dk:(m + 1) * dk, m, :], qT_d[m * dk:(m + 1) * dk, :], writes=[qT])
    vst = P.sb(pre + "vst", [128, KT, 64], BF16)
    P.dma("sync", vst[:], v_d, writes=[vst])
    P.op("vector", lambda e: e.tensor_copy(va[:, :, 0:64], vst[:]), [vst], [va.sub("v")])
    P.op("gpsimd", lambda e: e.memset(va[:, :, 64:128], 1.0), [], [va.sub("o")])
    P.dma("sync", ident[:], ident_d, writes=[ident])
    P.op("vector", lambda e: e.memset(epsb[:], EPS), [], [epsb])
    if diff:
        P.dma("sync", lamv[:], lamv_d, writes=[lamv])
        P.dma("sync", gsub[:], gsub_d, writes=[gsub])
        P.dma("sync", cst[:], cst_d, writes=[cst])
        P.op("vector", lambda e: e.tensor_tensor(prod[:, 0:32], lamv[:, 0:32], lamv[:, 32:64], ALU.mult), [lamv], [prod])
        P.op("vector", lambda e: e.tensor_tensor(prod[:, 32:64], lamv[:, 64:96], lamv[:, 96:128], ALU.mult), [lamv], [prod])
        P.op("vector", lambda e: e.reduce_sum(sm[:, 0:1], prod[:, 0:32], AX.X), [prod], [sm])
        P.op("vector", lambda e: e.reduce_sum(sm[:, 1:2], prod[:, 32:64], AX.X), [prod, sm], [sm])
        P.op("scalar", lambda e: e.activation(sm[:, 2:4], sm[:, 0:2], AF.Exp), [sm], [sm])
        P.op("vector", lambda e: e.tensor_tensor(sm[:, 4:5], sm[:, 3:4], sm[:, 2:3], ALU.subtract), [sm], [sm])
        P.op("vector", lambda e: e.tensor_tensor(sm[:, 4:5], sm[:, 4:5], cst[:, 0:1], ALU.subtract), [sm, cst], [sm])
        P.op("vector", lambda e: e.tensor_scalar(gsc[:], gsub[:], cst[:, 1:2], None, ALU.mult), [gsub, cst], [gsc])

    ci = 0
    yi = 0
    LA = 2
    for qg in range(NQ):
        qs = slice(qg * QG, (qg + 1) * QG)
        its = []
        for kt in range(KT):
            ps = pss[ci % 3]
            pt = pts[ci % 3]
            ci += 1
            its.append((kt, ps, pt))
        def emit_s(kt, ps, pt):
            for m in range(nmaps):
                rs = slice(m * dk, (m + 1) * dk)
                P.op("tensor", lambda e, ps=ps, rs=rs, kt=kt, m=m, qs=qs: e.matmul(ps[:, m * 512:m * 512 + QG], kT[:, kt * 128:(kt + 1) * 128], qT[:, m, qs], start=True, stop=True),
                     [kT, qT], [ps])
        def emit_av(kt, ps, pt):
            if QG == 512:
                W_ = 256 if EXPT == "halfact" else nmaps * 512
                P.op("scalar", lambda e, ps=ps, pt=pt: e.activation(pt[:, 0:W_], ps[:, 0:W_], AF.Exp, scale=scale), [ps], [pt])
            else:
                for m in range(nmaps):
                    P.op("scalar", lambda e, ps=ps, pt=pt, m=m: e.activation(pt[:, m * 512:m * 512 + QG], ps[:, m * 512:m * 512 + QG], AF.Exp, scale=scale), [ps], [pt])
            for m in range(nmaps if EXPT != "halfav" else 1):
                P.op("tensor", lambda e, pt=pt, kt=kt, m=m: e.matmul(pacc[m][:, :QG], va[:, kt, :], pt[:, m * 512:m * 512 + QG], start=(kt == 0), stop=(kt == KT - 1)),
                     [va, pt], [pacc[m]])
        for idx in range(len(its) + LA):
            if idx < len(its):
                emit_s(*its[idx])
            if idx >= LA:
                emit_av(*its[idx - LA])
        for m in range(nmaps):
            eng = ("vector", "scalar")[m % 2]
            if eng == "vector":
                P.op("vector", lambda e, m=m: e.tensor_copy(accs[m][:], pacc[m][:, :QG]), [pacc[m]], [accs[m]])
            else:
                P.op("scalar", lambda e, m=m: e.copy(accs[m][:], pacc[m][:, :QG]), [pacc[m]], [accs[m]])
        for tt in range(TT):
            for m in range(nmaps):
                P.op("tensor", lambda e, m=m, tt=tt: e.transpose(pss[0][:, m * 128:(m + 1) * 128], accs[m][:, tt * 128:(tt + 1) * 128], ident[:]),
                     [accs[m], ident], [pss[0]])
            for m in range(nmaps):
                P.op("vector", lambda e, m=m: e.reciprocal(rl[:, m:m + 1], pss[0][:, m * 128 + 64:m * 128 + 65]), [pss[0]], [rl.sub(m)])
                P.op("vector", lambda e, m=m: e.tensor_scalar(om[m][:], pss[0][:, m * 128:m * 128 + 64], rl[:, m:m + 1], None, ALU.mult),
                     [pss[0], rl.sub(m)], [om[m]])
            y = yo[yi % 2]
            yi += 1
            if diff:
                P.op("vector", lambda e: e.scalar_tensor_tensor(om[0][:], om[1][:], sm[:, 4:5], om[0][:], ALU.mult, ALU.add),
                     [om[0], om[1], sm], [om[0]])
                P.op("scalar", lambda e: e.activation(junk[:], om[0][:], AF.Square, accum_out=ss[:, 0:1]), [om[0]], [junk, ss])
                P.op("scalar", lambda e: e.activation(ss[:, 1:2], ss[:, 0:1], AF.Sqrt, bias=epsb[:], scale=1.0 / 64), [ss, epsb], [ss])
                P.op("vector", lambda e: e.reciprocal(rl[:, 0:1], ss[:, 1:2]), [ss], [rl.sub(0)])
                P.op("vector", lambda e, y=y: e.scalar_tensor_tensor(y[:], om[0][:], rl[:, 0:1], gsc[:], ALU.mult, ALU.mult),
                     [om[0], rl.sub(0), gsc], [y])
            else:
                P.op("vector", lambda e, y=y: e.tensor_copy(y[:], om[0][:]), [om[0]], [y])
            t0 = qg * QG + tt * 128
            P.dma("gpsimd", y_d[t0:t0 + 128, :], y[:], reads=[y])


def build_fattn(Tq, Tk, nmaps, dk, diff):
    nc = bass.Bass("TRN2", target_bir_lowering=False)
    D = lambda n, s, dt, k: nc.dram_tensor(n, list(s), dt, kind=k).ap()
    R = nmaps * dk
    qT = D("qT", [R, Tq], BF16, "ExternalInput")
    kT = D("kT", [R, Tk], BF16, "ExternalInput")
    v = D("v", [128, Tk // 128, 64], BF16, "ExternalInput")
    lamv = D("lamv", [128, 128], F32, "ExternalInput")
    gsub = D("gsub", [128, 64], F32, "ExternalInput")
    cst = D("cst", [128, 2], F32, "ExternalInput")
    ident = D("ident", [128, 128], F32, "ExternalInput")
    y = D("y", [Tq, 64], BF16, "ExternalOutput")
    P = Prog(nc)
    emit_fattn(P, nc, "a_", Tq, Tk, nmaps, dk, diff, qT, kT, v, lamv, gsub, cst, ident, y)
    P.finalize()
    return nc


def emit_fft(P, nc, pre, z_d, cm_d, tw_d, fT_d, CH=64):
    z = P.sb(pre + "z", [128, 2, CH, 128], BF16)
    cm = P.sb(pre + "cm", [128, 5, 128], BF16)
    tw = P.sb(pre + "tw", [128, 2, 4, 128], F32)
    tt = [P.sb(pre + "tt%d" % i, [128, 512], F32) for i in range(4)]
    ypr = [P.sb(pre + "ypr%d" % i, [128, 512], BF16) for i in range(2)]
    ypi = [P.sb(pre + "ypi%d" % i, [128, 512], BF16) for i in range(2)]
    xo = [P.sb(pre + "xo%d" % i, [128, 512], BF16) for i in range(2)]
    pyr = [P.ps(pre + "pyr%d" % i, [128, 512]) for i in range(2)]
    pyi = [P.ps(pre + "pyi%d" % i, [128, 512]) for i in range(2)]
    px = [P.ps(pre + "px%d" % i, [128, 512]) for i in range(2)]
    P.dma("sync", z[:, 0], z_d[:, 0], writes=[z.sub(0)])
    P.dma("gpsimd", z[:, 1], z_d[:, 1], writes=[z.sub(1)])
    P.dma("sync", cm[:], cm_d, writes=[cm])
    P.dma("sync", tw[:], tw_d, writes=[tw])
    ctf = tw[:, 0].rearrange("p a b -> p (a b)")
    stf = tw[:, 1].rearrange("p a b -> p (a b)")
    for g in range(CH // 4):
        yr, yi = pyr[g % 2], pyi[g % 2]
        for cc in range(4):
            c = g * 4 + cc
            o = slice(cc * 128, (cc + 1) * 128)
            P.op("tensor", lambda e, c=c, o=o, yr=yr: e.matmul(yr[:, o], z[:, 0, c, :], cm[:, 0, :], start=True, stop=False), [z, cm], [yr])
            P.op("tensor", lambda e, c=c, o=o, yr=yr: e.matmul(yr[:, o], z[:, 1, c, :], cm[:, 1, :], start=False, stop=True), [z, cm], [yr])
            P.op("tensor", lambda e, c=c, o=o, yi=yi: e.matmul(yi[:, o], z[:, 1, c, :], cm[:, 0, :], start=True, stop=False), [z, cm], [yi])
            P.op("tensor", lambda e, c=c, o=o, yi=yi: e.matmul(yi[:, o], z[:, 0, c, :], cm[:, 2, :], start=False, stop=True), [z, cm], [yi])
        a, b = ypr[g % 2], ypi[g % 2]
        P.op("vector", lambda e, yr=yr: e.tensor_tensor(tt[0][:], yr[:], ctf, ALU.mult), [yr, tw], [tt[0]])
        P.op("vector", lambda e, yi=yi: e.tensor_tensor(tt[1][:], yi[:], stf, ALU.mult), [yi, tw], [tt[1]])
        P.op("gpsimd", lambda e, a=a: e.tensor_tensor(a[:], tt[0][:], tt[1][:], ALU.add), [tt[0], tt[1]], [a])
        P.op("vector", lambda e, yi=yi: e.tensor_tensor(tt[2][:], yi[:], ctf, ALU.mult), [yi, tw], [tt[2]])
        P.op("vector", lambda e, yr=yr: e.tensor_tensor(tt[3][:], yr[:], stf, ALU.mult), [yr, tw], [tt[3]])
        P.op("gpsimd", lambda e, b=b: e.tensor_tensor(b[:], tt[2][:], tt[3][:], ALU.subtract), [tt[2], tt[3]], [b])
        x = px[g % 2]
        P.op("tensor", lambda e, x=x, a=a: e.matmul(x[:], cm[:, 3, :], a[:], start=True, stop=False), [cm, a], [x])
        P.op("tensor", lambda e, x=x, b=b: e.matmul(x[:], cm[:, 4, :], b[:], start=False, stop=True), [cm, b], [x])
        o_ = xo[g % 2]
        P.op("scalar", lambda e, x=x, o_=o_: e.copy(o_[:], x[:]), [x], [o_])
        P.dma(("sync", "gpsimd")[g % 2], fT_d[g * 4:(g + 1) * 4, :].rearrange("c (k2 k1) -> k2 c k1", k1=128),
              o_[:].rearrange("p (c k) -> p c k", c=4), reads=[o_])


def build_fft(CH=64):
    nc = bass.Bass("TRN2", target_bir_lowering=False)
    D = lambda n, s, dt, k: nc.dram_tensor(n, list(s), dt, kind=k).ap()
    z = D("z", [128, 2, CH, 128], BF16, "ExternalInput")
    cm = D("cm", [128, 5, 128], BF16, "ExternalInput")
    tw = D("tw", [128, 2, 4, 128], F32, "ExternalInput")
    fT = D("fT", [CH, 16384], BF16, "ExternalOutput")
    P = Prog(nc)
    emit_fft(P, nc, "f_", z, cm, tw, fT, CH)
    P.finalize()
    return nc


def build_fft256(CH=64):
    nc = bass.Bass("TRN2", target_bir_lowering=False)
    D = lambda n, s, dt, k: nc.dram_tensor(n, list(s), dt, kind=k).ap()
    z_d = D("z", [128, 2, 2, CH], BF16, "ExternalInput")
    cn_d = D("cn", [128, 2, 2, 256], BF16, "ExternalInput")
    fT = D("fT", [CH, 256], BF16, "ExternalOutput")
    P = Prog(nc)
    z = P.sb("z", [128, 2, 2, CH], BF16)
    cn = P.sb("cn", [128, 2, 2, 256], BF16)
    o = P.sb("o", [CH, 256], BF16)
    ps = P.ps("ps", [128, 512])
    P.dma("sync", z[:], z_d, writes=[z])
    P.dma("sync", cn[:], cn_d, writes=[cn])
    n = 0
    for ri in range(2):
        for t in range(2):
            P.op("tensor", lambda e, ri=ri, t=t, n=n: e.matmul(ps[:CH, :256], z[:, ri, t, :], cn[:, ri, t, :], start=(n == 0), stop=(n == 3)), [z, cn], [ps])
            n += 1
    P.op("vector", lambda e: e.tensor_copy(o[:], ps[:CH, :256]), [ps], [o])
    P.dma("sync", fT, o[:], reads=[o])
    P.finalize()
    return nc


EPS = 1e-6


def build_merge(ntok, G):
    NG = ntok // G
    TT = G // 128
    NT = ntok // 128
    nc = bass.Bass("TRN2", target_bir_lowering=False)
    D = lambda n, s, dt, k: nc.dram_tensor(n, list(s), dt, kind=k).ap()
    xT = D("xT", [1024, ntok], F32, "ExternalInput")
    vecs = D("vecs", [128, 56], F32, "ExternalInput")
    wg_d = D("wg", [1024, 4096], F32, "ExternalInput")
    wbr_d = D("wbr", [4, 256, 1024], F32, "ExternalInput")
    wo_d = D("wo", [1024, 1024], F32, "ExternalInput")
    wr_d = D("wr", [1024, 16], F32, "ExternalInput")
    fw_d = D("fw", [256, 256], F32, "ExternalInput")
    pw_d = D("pw", [4, 64, 64], F32, "ExternalInput")
    psc_d = D("psc", [64, 4], F32, "ExternalInput")
    band_d = D("band", [128, 3, 4, 3, 128], BF16, "ExternalInput")
    ones_d = D("ones", [128, 128], BF16, "ExternalInput")
    yT_d = D("yT", [3, 256, ntok], BF16, "ExternalInput")
    u_d = D("u", [128, NT + 2, 256], BF16, "ExternalInput")
    xmT = D("xmT", [1024, ntok], F32, "ExternalOutput")
    h2T = D("h2T", [1024, ntok], BF16, "ExternalOutput")
    aff = D("aff", [ntok, 16], F32, "ExternalOutput")

    P = Prog(nc)
    wg = P.sb("wg", [128, 8, 4096], BF16)
    wo = P.sb("wo", [128, 8, 1024], BF16)
    wb = P.sb("wb", [128, 3, 2, 1024], BF16)
    wb2 = P.sb("wb2", [64, 4, 1024], BF16)
    fw = P.sb("fw", [128, 2, 256], BF16)
    pw = P.sb("pw", [64, 4, 64], BF16)
    wr = P.sb("wr", [128, 8, 16], F32)
    psc = P.sb("psc", [64, 4], F32)
    band = P.sb("band", [128, 3, 4, 3, 128], BF16)
    ones = P.sb("ones", [128, 128], BF16)
    vec = P.sb("vec", [128, 56], F32)
    A1 = P.sb("A1", [128, 8], F32)
    A2 = P.sb("A2", [128, 8], F32)
    epsb = P.sb("epsb", [128, 1], F32)
    xs = P.sb("xs", [128, 8, G], F32)
    stg = P.sb("stg", [128, 8, 512], F32)
    hT = P.sb("hT", [128, 8, G], BF16)
    sqb = P.sb("sqb", [128, 8, G], BF16)
    acc = P.sb("acc", [128, 8, G], F32)
    yt = P.sb("yt", [128, 3, 2, G], BF16)
    yfn = P.sb("yfn", [128, 2, G], BF16)
    ypl = P.sb("ypl", [64, 4, G], BF16)
    pld = P.sb("pld", [64, 4, G], BF16)
    ub = P.sb("ub", [128, TT + 2, 256], BF16)
    gsb = [P.sb("gsb%d" % i, [128, G], BF16) for i in range(2)]
    tmp = [P.sb("tmp%d" % i, [128, G], F32) for i in range(2)]
    rstd = P.sb("rstd", [128, G], F32)
    sd = P.sb("sd", [128, G], F32)
    lg = P.sb("lg", [128, 16], F32)
    ex = P.sb("ex", [128, 16], F32)
    sm = P.sb("sm", [128, 4], F32)
    ao = [P.sb("ao%d" % i, [128, 16], F32) for i in range(2)]
    pg = [P.ps("pg%d" % i, [128, 512]) for i in range(2)]
    pz = [P.ps("pz%d" % i, [128, 512]) for i in range(2)]
    pst = P.ps("pst", [128, 512])
    pp = P.ps("pp", [64, 4, 128])
    pl = P.ps("pl", [128, 16])
    h2f = stg

    P.dma("sync", vec[:], vecs, writes=[vec])
    P.dma("sync", band[:], band_d, writes=[band])
    P.dma("sync", ones[:], ones_d, writes=[ones])
    P.dma("sync", psc[:], psc_d, writes=[psc])
    P.dma("sync", wr[:], wr_d.rearrange("(k p) n -> p k n", p=128), writes=[wr])
    P.op("vector", lambda e: e.memset(epsb[:], EPS), [], [epsb])
    P.op("vector", lambda e: e.tensor_scalar(A1[:], vec[:, 8:16], 1.0, None, ALU.add), [vec], [A1])
    P.op("vector", lambda e: e.tensor_tensor(A1[:], A1[:], vec[:, 16:24], ALU.mult), [A1, vec], [A1])
    P.op("vector", lambda e: e.tensor_scalar(A2[:], vec[:, 40:48], 1.0, None, ALU.add), [vec], [A2])
    P.op("vector", lambda e: e.tensor_tensor(A2[:], A2[:], vec[:, 48:56], ALU.mult), [A2, vec], [A2])
    ce = [0]
    def cast(dst, src, rd, wr_):
        eng = ("vector", "gpsimd", "scalar")[ce[0] % 3]
        ce[0] += 1
        if eng == "scalar":
            P.op("scalar", lambda e: e.copy(dst, src), rd, wr_)
        else:
            P.op(eng, lambda e: e.tensor_copy(dst, src), rd, wr_)
    for cb in range(8):
        P.dma("sync", stg[:], wg_d[:, cb * 512:(cb + 1) * 512].rearrange("(k p) n -> p k n", p=128), writes=[stg])
        for k in range(8):
            cast(wg[:, k, cb * 512:(cb + 1) * 512], stg[:, k, :], [stg], [wg.sub((cb, k))])
    for cb in range(2):
        P.dma("sync", stg[:], wo_d[:, cb * 512:(cb + 1) * 512].rearrange("(k p) n -> p k n", p=128), writes=[stg])
        for k in range(8):
            cast(wo[:, k, cb * 512:(cb + 1) * 512], stg[:, k, :], [stg], [wo.sub((cb, k))])
    for cb in range(2):
        P.dma("sync", stg[:], wbr_d[:, :, cb * 512:(cb + 1) * 512].rearrange("i (c p) n -> p (i c) n", p=128), writes=[stg])
        for bi, i in enumerate((0, 1, 3)):
            for c in range(2):
                cast(wb[:, bi, c, cb * 512:(cb + 1) * 512], stg[:, i * 2 + c, :], [stg], [wb.sub((bi, c, cb))])
    for cb in range(2):
        P.dma("sync", stg[0:64, 0:4, :], wbr_d[2, :, cb * 512:(cb + 1) * 512].rearrange("(g p) n -> p g n", p=64), writes=[stg])
        cast(wb2[:, :, cb * 512:(cb + 1) * 512], stg[0:64, 0:4, :], [stg], [wb2.sub(cb)])
    P.dma("sync", stg[:, 0:2, 0:256], fw_d.rearrange("(c p) n -> p c n", p=128), writes=[stg])
    cast(fw[:], stg[:, 0:2, 0:256], [stg], [fw])
    P.dma("sync", stg[0:64, 0:4, 0:64], pw_d.rearrange("g p n -> p g n"), writes=[stg])
    cast(pw[:], stg[0:64, 0:4, 0:64], [stg], [pw])

    def norm(src, A, shcol, dst_bf, dst_f32, rd):
        P.op("scalar", lambda e: e.activation(sqb[:], src[:], AF.Square), [src], [sqb])
        for k in range(8):
            P.op("tensor", lambda e, k=k: e.matmul(pst[:, :G], ones[:], sqb[:, k, :], start=(k == 0), stop=(k == 7)), [ones, sqb], [pst])
        P.op("scalar", lambda e: e.activation(sd[:], pst[:, :G], AF.Sqrt, bias=epsb[:], scale=1.0 / 1024), [pst, epsb], [sd])
        P.op("vector", lambda e: e.reciprocal(rstd[:], sd[:]), [sd], [rstd])
        for k in range(8):
            tf = tmp[k % 2]
            P.op("vector", lambda e, k=k, tf=tf: e.scalar_tensor_tensor(tf[:], src[:, k, :], A[:, k:k + 1], rstd[:], ALU.mult, ALU.mult),
                 [src, A, rstd], [tf])
            if dst_f32 is None:
                P.op("scalar", lambda e, k=k, tf=tf: e.activation(dst_bf[:, k, :], tf[:], AF.Identity, bias=vec[:, shcol + k:shcol + k + 1], scale=1.0),
                     [tf, vec], [dst_bf.sub(k)])
            else:
                P.op("scalar", lambda e, k=k, tf=tf: e.activation(dst_f32[:, k, :G], tf[:], AF.Identity, bias=vec[:, shcol + k:shcol + k + 1], scale=1.0),
                     [tf, vec], [dst_f32.sub(k)])
                P.op("gpsimd", lambda e, k=k: e.tensor_copy(dst_bf[:, k, :], dst_f32[:, k, :G]), [dst_f32.sub(k)], [dst_bf.sub(k)])

    ci = [0]
    for g in range(NG):
        ts = slice(g * G, (g + 1) * G)
        P.dma("sync", xs[:], xT[:, ts].rearrange("(k p) t -> p k t", p=128), writes=[xs])
        P.dma("gpsimd", yt[:].rearrange("p i c t -> p (i c) t"), yT_d[:, :, ts].rearrange("i (c p) t -> p (i c) t", p=128), writes=[yt])
        P.dma("gpsimd", ub[:], u_d[:, g * TT:g * TT + TT + 2, :], writes=[ub])
        norm(xs, A1, 0, hT, None, None)
        for mo in range(2):
            ps = pz[ci[0] % 2]; ci[0] += 1
            for c in range(2):
                P.op("tensor", lambda e, ps=ps, c=c, mo=mo: e.matmul(ps[:, :G], fw[:, c, mo * 128:(mo + 1) * 128], yt[:, 2, c, :], start=(c == 0), stop=(c == 1)),
                     [fw, yt], [ps])
            P.op("scalar", lambda e, ps=ps, mo=mo: e.copy(yfn[:, mo, :], ps[:, :G]), [ps], [yfn.sub(mo)])
        for tt in range(TT):
            li = g * TT + tt
            var = 0 if li == 0 else (2 if li == NT - 1 else 1)
            for gr in range(4):
                for j in range(3):
                    P.op("tensor", lambda e, gr=gr, j=j, tt=tt, var=var: e.matmul(pp[:, gr, :], ub[:, tt + j, gr * 64:(gr + 1) * 64], band[:, var, gr, j, :],
                                                                              start=(j == 0), stop=(j == 2)), [ub, band], [pp])
            P.op("vector", lambda e, tt=tt: e.tensor_copy(pld[:, :, tt * 128:(tt + 1) * 128], pp[:]), [pp], [pld.sub(tt)])
        for gr in range(4):
            ps = pz[ci[0] % 2]; ci[0] += 1
            P.op("tensor", lambda e, ps=ps, gr=gr: e.matmul(ps[0:64, :G], pw[:, gr, :], pld[:, gr, :], start=True, stop=True), [pw, pld], [ps])
            P.op("scalar", lambda e, ps=ps, gr=gr: e.activation(ypl[:, gr, :], ps[0:64, :G], AF.Identity, scale=psc[:, gr:gr + 1]), [ps, psc], [ypl.sub(gr)])
        for dc in range(8):
            ds_ = slice(dc * 128, (dc + 1) * 128)
            for i in range(4):
                pgt = pg[ci[0] % 2]; pzt = pz[ci[0] % 2]; gs = gsb[ci[0] % 2]; ci[0] += 1
                for k in range(8):
                    P.op("tensor", lambda e, pgt=pgt, k=k, i=i, dc=dc: e.matmul(pgt[:, :G], wg[:, k, i * 1024 + dc * 128:i * 1024 + (dc + 1) * 128], hT[:, k, :],
                                                                              start=(k == 0), stop=(k == 7)), [wg, hT], [pgt])
                if i == 2:
                    for gr in range(4):
                        P.op("tensor", lambda e, pzt=pzt, gr=gr, ds_=ds_: e.matmul(pzt[:, :G], wb2[:, gr, ds_], ypl[:, gr, :], start=(gr == 0), stop=(gr == 3)),
                             [wb2, ypl], [pzt])
                else:
                    bi = {0: 0, 1: 1, 3: 2}[i]
                    for c in range(2):
                        rhs = (lambda c=c, i=i: yt[:, i, c, :]) if i < 2 else (lambda c=c: yfn[:, c, :])
                        P.op("tensor", lambda e, pzt=pzt, c=c, bi=bi, ds_=ds_, rhs=rhs: e.matmul(pzt[:, :G], wb[:, bi, c, ds_], rhs(), start=(c == 0), stop=(c == 1)),
                             [wb, yt, yfn], [pzt])
                P.op("scalar", lambda e, pgt=pgt, gs=gs: e.activation(gs[:], pgt[:, :G], AF.Sigmoid), [pgt], [gs])
                if i == 0:
                    P.op("vector", lambda e, pzt=pzt, gs=gs, dc=dc: e.tensor_tensor(acc[:, dc, :], pzt[:, :G], gs[:], ALU.mult), [pzt, gs], [acc.sub(dc)])
                else:
                    tf = tmp[i % 2]
                    P.op("vector", lambda e, pzt=pzt, gs=gs, tf=tf: e.tensor_tensor(tf[:], pzt[:, :G], gs[:], ALU.mult), [pzt, gs], [tf])
                    if i < 3:
                        P.op("gpsimd", lambda e, tf=tf, dc=dc: e.tensor_tensor(acc[:, dc, :], acc[:, dc, :], tf[:], ALU.add), [acc.sub(dc), tf], [acc.sub(dc)])
                    else:
                        P.op("gpsimd", lambda e, tf=tf, dc=dc: e.tensor_tensor(sqb[:, dc, :], acc[:, dc, :], tf[:], ALU.add), [acc.sub(dc), tf], [sqb.sub(dc)])
        for d2 in range(8):
            ps = pz[ci[0] % 2]; ci[0] += 1
            for dc in range(8):
                P.op("tensor", lambda e, ps=ps, dc=dc, d2=d2: e.matmul(ps[:, :G], wo[:, dc, d2 * 128:(d2 + 1) * 128], sqb[:, dc, :], start=(dc == 0), stop=(dc == 7)),
                     [wo, sqb], [ps])
            P.op("vector", lambda e, ps=ps, d2=d2: e.scalar_tensor_tensor(xs[:, d2, :], ps[:, :G], vec[:, 24 + d2:25 + d2], xs[:, d2, :], ALU.mult, ALU.add),
                 [ps, vec, xs.sub(d2)], [xs.sub(d2)])
        P.dma("sync", xmT[:, ts].rearrange("(k p) t -> p k t", p=128), xs[:], reads=[xs])
        norm(xs, A2, 32, hT, h2f, None)
        P.dma("gpsimd", h2T[:, ts].rearrange("(k p) t -> p k t", p=128), hT[:], reads=[hT])
        for tt in range(TT):
            for k in range(8):
                P.op("tensor", lambda e, k=k, tt=tt: e.matmul(pl[:], h2f[:, k, tt * 128:(tt + 1) * 128], wr[:, k, :], start=(k == 0), stop=(k == 7)),
                     [h2f, wr], [pl])
            a = ao[tt % 2]
            P.op("vector", lambda e: e.tensor_copy(lg[:], pl[:]), [pl], [lg])
            P.op("vector", lambda e: e.reduce_max(sm[:, 0:1], lg[:], AX.X), [lg], [sm.sub(0)])
            P.op("vector", lambda e: e.tensor_scalar(sm[:, 1:2], sm[:, 0:1], -1.0, None, ALU.mult), [sm.sub(0)], [sm.sub(1)])
            P.op("scalar", lambda e: e.activation(ex[:], lg[:], AF.Exp, bias=sm[:, 1:2], scale=1.0, accum_out=sm[:, 2:3]), [lg, sm.sub(1)], [ex, sm.sub(2)])
            P.op("vector", lambda e: e.reciprocal(sm[:, 3:4], sm[:, 2:3]), [sm.sub(2)], [sm.sub(3)])
            P.op("vector", lambda e, a=a: e.tensor_scalar(a[:], ex[:], sm[:, 3:4], None, ALU.mult), [ex, sm.sub(3)], [a])
            t0 = g * G + tt * 128
            P.dma("sync", aff[t0:t0 + 128, :], a[:], reads=[a])
    P.finalize()
    return nc


FF = 1408
NITER = 26


def build_experts(NI, CAP, NE=4):
    T = NI * 128
    SR = min(128, CAP)
    NSL = CAP // SR
    HS = min(1024, CAP)
    NH = CAP // HS
    SG = min(512, HS)
    NSG = HS // SG
    STH = HS // SR
    CH = min(512, NI * 16)
    NCH = NI * 16 // CH
    IPC = CH // 16
    nc = bass.Bass("TRN2", target_bir_lowering=False)
    D = lambda n, s, dt, k: nc.dram_tensor(n, list(s), dt, kind=k).ap()
    aff_d = D("aff", [128, NI, 16], F32, "ExternalInput")
    h2_d = D("h2", [T, 1024], BF16, "ExternalInput")
    wgate_d = D("wgate", [NE, 1024, FF], F32, "ExternalInput")
    wup_d = D("wup", [NE, 1024, FF], F32, "ExternalInput")
    wdown_d = D("wdown", [NE, FF, 1024], F32, "ExternalInput")
    cst_d = D("cst", [128, 3, 128], BF16, "ExternalInput")
    pos_d = D("pos", [128, NI, 16], I32, "ExternalOutput")
    ye_d = D("ye", [NE, CAP, 1024], BF16, "ExternalOutput")
    xe_d = nc.dram_tensor("xe", [NE * CAP, 1024], BF16, kind="Internal").ap()

    P = Prog(nc)
    aff = P.sb("aff", [128, NI, 16], F32)
    cmpb = P.sb("cmpb", [128, NI, 16], BF16)
    M = P.sb("M", [128, NI, 16], BF16)
    S = P.sb("S", [128, NI, 16], F32)
    W = P.sb("W", [128, NI, 16], F32)
    incl = P.sb("incl", [128, NI, 16], F32)
    zer = P.sb("zer", [128, NI], F32)
    posi = P.sb("posi", [128, NI, 16], I32)
    tau = P.sb("tau", [128, 16], F32)
    mid = P.sb("mid", [128, 16], F32)
    inc = P.sb("inc", [128, 16], F32)
    cntp = P.sb("cntp", [128, 16], BF16)
    cst = P.sb("cst", [128, 3, 128], BF16)
    pc = P.ps("pc", [128, 16])
    pw_ = [P.ps("pw%d" % i, [128, 512]) for i in range(2)]
    ones, ltri, ident = (lambda: cst[:, 0, :]), (lambda: cst[:, 1, :]), (lambda: cst[:, 2, :])

    P.dma("sync", aff[:], aff_d, writes=[aff])
    P.dma("sync", cst[:], cst_d, writes=[cst])
    P.op("vector", lambda e: e.memset(tau[:], 0.0), [], [tau])
    P.op("gpsimd", lambda e: e.memset(zer[:], 0.0), [], [zer])
    for it in range(NITER):
        step = 2.0 ** -(it + 1)
        P.op("vector", lambda e, step=step: e.tensor_scalar(mid[:], tau[:], step, None, ALU.add), [tau], [mid])
        P.op("vector", lambda e: e.tensor_tensor(cmpb[:], aff[:], mid[:].unsqueeze(1).to_broadcast([128, NI, 16]), ALU.is_ge), [aff, mid], [cmpb])
        def red(e):
            with nc.allow_low_precision(reason="exact small integer counts"):
                return e.tensor_reduce(cntp[:], cmpb[:].rearrange("p i e -> p e i"), AX.X, ALU.add)
        P.op("vector", red, [cmpb], [cntp])
        P.op("tensor", lambda e: e.matmul(pc[:], ones(), cntp[:], start=True, stop=True), [cst, cntp], [pc])
        P.op("vector", lambda e, step=step: e.tensor_scalar(inc[:], pc[:], CAP - 0.5, step, ALU.is_ge, ALU.mult), [pc], [inc])
        P.op("vector", lambda e: e.tensor_tensor(tau[:], tau[:], inc[:], ALU.add), [tau, inc], [tau])
    P.op("vector", lambda e: e.tensor_tensor(M[:], aff[:], tau[:].unsqueeze(1).to_broadcast([128, NI, 16]), ALU.is_ge), [aff, tau], [M])
    Mf = lambda c: M[:].rearrange("p i e -> p (i e)")[:, c * CH:(c + 1) * CH]
    for c in range(NCH):
        a, b_ = pw_[0], pw_[1]
        P.op("tensor", lambda e, c=c: e.matmul(a[:, :CH], ltri(), Mf(c), start=True, stop=True), [cst, M], [a])
        P.op("tensor", lambda e, c=c: e.matmul(b_[:, :CH], ones(), Mf(c), start=True, stop=True), [cst, M], [b_])
        P.op("vector", lambda e, c=c: e.tensor_copy(W[:].rearrange("p i e -> p (i e)")[:, c * CH:(c + 1) * CH], a[:, :CH]), [a], [W.sub(c)])
        P.op("scalar", lambda e, c=c: e.copy(S[:].rearrange("p i e -> p (i e)")[:, c * CH:(c + 1) * CH], b_[:, :CH]), [b_], [S.sub(c)])
    for ex in range(16):
        P.op("vector", lambda e, ex=ex: e.tensor_tensor_scan(incl[:, :, ex], S[:, :, ex], zer[:], 0.0, ALU.add, ALU.add), [S, zer], [incl.sub(ex)])
    P.op("vector", lambda e: e.tensor_tensor(W[:], W[:], incl[:], ALU.add), [W, incl], [W])
    P.op("vector", lambda e: e.tensor_tensor(W[:], W[:], S[:], ALU.subtract), [W, S], [W])
    BIG = float(2 ** 20)
    P.op("vector", lambda e: e.tensor_scalar(W[:], W[:], -BIG, None, ALU.add), [W], [W])
    P.op("vector", lambda e: e.tensor_tensor(W[:], W[:], M[:], ALU.mult), [W, M], [W])
    P.op("vector", lambda e: e.tensor_scalar(W[:], W[:], BIG, None, ALU.add), [W], [W])
    P.op("vector", lambda e: e.tensor_copy(posi[:], W[:]), [W], [posi])
    P.dma("sync", pos_d, posi[:], reads=[posi])
    padj = P.sb("padj", [128, NI, NE], I32)
    for e_ in range(NE):
        P.op("vector", lambda e, e_=e_: e.tensor_scalar(padj[:, :, e_], W[:, :, e_], float(e_ * CAP), None, ALU.add), [W], [padj.sub(e_)])

    return nc, P, dict(pw_=pw_, posi=padj, h2_d=h2_d, xe_d=xe_d, ye_d=ye_d, wgate_d=wgate_d, wup_d=wup_d, wdown_d=wdown_d, cst=cst, ident=ident,
                       NI=NI, CAP=CAP, NE=NE, SR=SR, NSL=NSL, HS=HS, NH=NH, SG=SG, NSG=NSG, STH=STH)


def emit_expert_ffn(nc, P, d, e_off):
    NI, CAP, NE, SR, NSL, HS, NH, SG, NSG, STH = (d[k] for k in ("NI", "CAP", "NE", "SR", "NSL", "HS", "NH", "SG", "NSG", "STH"))
    posi, h2_d, xe_d, ye_d, cst, ident = d["posi"], d["h2_d"], d["xe_d"], d["ye_d"], d["cst"], d["ident"]
    h2t = [P.sb("h2t%d" % i, [128, 1024], BF16) for i in range(3)]
    xe_tm = P.sb("xe_tm", [128, STH, 1024], BF16)
    xeT = P.sb("xeT", [128, 8, HS], BF16)
    hidT = P.sb("hidT", [128, 11, HS], BF16)
    wgu = P.sb("wgu", [128, 8, 2 * FF], BF16)
    wd = P.sb("wd", [128, 11, 1024], BF16)
    stg = [P.sb("stg%d" % i, [128, FF], F32) for i in range(3)]
    sg_ = [P.sb("sg%d" % i, [128, SG], BF16) for i in range(2)]
    yo = [P.sb("yo%d" % i, [128, 1024], BF16) for i in range(2)]
    ptr = [P.ps("ptr%d" % i, [128, 4, 128], BF16) for i in range(2)]
    pga = d["pw_"]
    pup = [P.ps("pup%d" % i, [128, 512]) for i in range(2)]
    xeB = Buf("xe_dram")
    breg = {}
    def mkreg(e):
        breg["r"] = e.alloc_register("bchk")
        return e.reg_mov(breg["r"], NE * CAP - 1)
    P.op("gpsimd", mkreg, [], [])
    hi_ = [0]
    def scatter(e_):
        for i in range(NI):
            ht = h2t[hi_[0] % 3]; hi_[0] += 1
            P.dma("sync", ht[:], h2_d[i * 128:(i + 1) * 128, :], writes=[ht])
            P.op("gpsimd", lambda e, ht=ht, i=i, e_=e_: e.indirect_dma_start(
                out=xe_d, out_offset=bass.IndirectOffsetOnAxis(ap=posi[:, i, e_off + e_:e_off + e_ + 1], axis=0),
                in_=ht[:], in_offset=None, bounds_check=breg["r"], oob_is_err=False), [ht, posi], [xeB.sub((e_, i))], dma=True)
    scatter(0)
    ce = [0]
    def cast(dst, src, rd, wr_, psum=False):
        eng = ("vector", "scalar")[ce[0] % 2] if psum else ("vector", "gpsimd", "scalar")[ce[0] % 3]
        ce[0] += 1
        if eng == "scalar":
            P.op("scalar", lambda e: e.copy(dst, src), rd, wr_)
        else:
            P.op(eng, lambda e: e.tensor_copy(dst, src), rd, wr_)
    si = [0]
    ci = [0]
    for e_ in range(NE):
        if e_ + 1 < NE:
            scatter(e_ + 1)
        for k in range(8):
            for gu, wsrc in enumerate((d["wgate_d"], d["wup_d"])):
                s = stg[si[0] % 3]; si[0] += 1
                P.dma("sync", s[:, 0:FF], wsrc[e_, k * 128:(k + 1) * 128, :], writes=[s])
                cast(wgu[:, k, gu * FF:(gu + 1) * FF], s[:, :], [s], [wgu.sub((k, gu))])
        for f in range(11):
            s = stg[si[0] % 3]; si[0] += 1
            P.dma("sync", s[:, 0:1024], d["wdown_d"][e_, f * 128:(f + 1) * 128, :], writes=[s])
            cast(wd[:, f, :], s[:, 0:1024], [s], [wd.sub(f)])
        for hh in range(NH):
            P.dma("gpsimd", xe_tm[:SR], xe_d[e_ * CAP + hh * HS:e_ * CAP + (hh + 1) * HS, :].rearrange("(s p) d -> p s d", p=SR), reads=[xeB.sub((e_, i)) for i in range(NI)], writes=[xe_tm])
            for st in range(STH):
                for kq in range(2):
                    pt = ptr[ci[0] % 2]; ci[0] += 1
                    for kk in range(4):
                        k = kq * 4 + kk
                        P.op("tensor", lambda e, pt=pt, kk=kk, k=k, st=st: e.transpose(pt[:, kk, :SR], xe_tm[:SR, st, k * 128:(k + 1) * 128], cst[:SR, 2, :SR]),
                             [xe_tm, cst], [pt])
                    cast(xeT[:, kq * 4:(kq + 1) * 4, st * SR:(st + 1) * SR], pt[:, :, :SR], [pt], [xeT.sub((st, kq))], psum=True)
            for f in range(11):
                for sgi in range(NSG):
                    ss = slice(sgi * SG, (sgi + 1) * SG)
                    a, b_ = pga[ci[0] % 2], pup[ci[0] % 2]; sgt = sg_[ci[0] % 2]; ci[0] += 1
                    for k in range(8):
                        P.op("tensor", lambda e, a=a, k=k, f=f, ss=ss: e.matmul(a[:, :SG], wgu[:, k, f * 128:(f + 1) * 128], xeT[:, k, ss], start=(k == 0), stop=(k == 7)),
                             [wgu, xeT], [a])
                    for k in range(8):
                        P.op("tensor", lambda e, b_=b_, k=k, f=f, ss=ss: e.matmul(b_[:, :SG], wgu[:, k, FF + f * 128:FF + (f + 1) * 128], xeT[:, k, ss], start=(k == 0), stop=(k == 7)),
                             [wgu, xeT], [b_])
                    P.op("scalar", lambda e, a=a, sgt=sgt: e.activation(sgt[:], a[:, :SG], AF.Silu), [a], [sgt])
                    P.op("vector", lambda e, b_=b_, sgt=sgt, f=f, ss=ss: e.tensor_tensor(hidT[:, f, ss], b_[:, :SG], sgt[:], ALU.mult), [b_, sgt], [hidT.sub((f, sgi))])
            for st in range(STH):
                y = yo[st % 2]
                for dh in range(2):
                    a = pga[ci[0] % 2]; ci[0] += 1
                    for f in range(11):
                        P.op("tensor", lambda e, a=a, f=f, st=st, dh=dh: e.matmul(a[:SR, :], hidT[:, f, st * SR:(st + 1) * SR], wd[:, f, dh * 512:(dh + 1) * 512], start=(f == 0), stop=(f == 10)),
                             [hidT, wd], [a])
                    if dh == 0:
                        P.op("scalar", lambda e, a=a, y=y: e.copy(y[:SR, 0:512], a[:SR, :]), [a], [y.sub(0)])
                    else:
                        P.op("vector", lambda e, a=a, y=y: e.tensor_copy(y[:SR, 512:1024], a[:SR, :]), [a], [y.sub(1)])
                r0 = hh * HS + st * SR
                P.dma("sync", ye_d[e_, r0:r0 + SR, :], y[:SR, :], reads=[y])


def build_experts_full(NI, CAP, NE=4):
    nc, P, d = build_experts(NI, CAP, NE)
    emit_expert_ffn(nc, P, d, 0)
    P.finalize()
    return nc


def build_combine(ntok, NEXP_CAP):
    NT = ntok // 128
    nc = bass.Bass("TRN2", target_bir_lowering=False)
    D = lambda n, s, dt, k: nc.dram_tensor(n, list(s), dt, kind=k).ap()
    xm_d = D("xm", [ntok, 1024], F32, "ExternalInput")
    pos_d = D("pos", [128, NT, 16], I32, "ExternalInput")
    aff_d = D("aff", [128, NT, 16], F32, "ExternalInput")
    eoff_d = D("eoff", [128, 16], F32, "ExternalInput")
    gt2_d = D("gt2", [128, 1024], F32, "ExternalInput")
    ye_d = D("ye", [NEXP_CAP, 1024], BF16, "ExternalInput")
    xo_d = D("xo", [ntok, 1024], F32, "ExternalOutput")
    P = Prog(nc)
    posi = P.sb("posi", [128, NT, 16], I32)
    posf = P.sb("posf", [128, NT, 16], F32)
    idx = P.sb("idx", [128, NT, 16], I32)
    aff = P.sb("aff", [128, NT, 16], F32)
    eoff = P.sb("eoff", [128, 16], F32)
    gt2 = P.sb("gt2", [128, 1024], F32)
    xm = [P.sb("xm%d" % i, [128, 1024], F32) for i in range(2)]
    acc = [P.sb("acc%d" % i, [128, 1024], F32) for i in range(2)]
    R = [P.sb("R%d" % i, [128, 1024], BF16) for i in range(6)]
    P.dma("sync", posi[:], pos_d, writes=[posi])
    P.dma("sync", aff[:], aff_d, writes=[aff])
    P.dma("sync", eoff[:], eoff_d, writes=[eoff])
    P.dma("sync", gt2[:], gt2_d, writes=[gt2])
    P.op("vector", lambda e: e.tensor_copy(posf[:], posi[:]), [posi], [posf])
    msk = P.sb("msk", [128, NT, 16], F32)
    P.op("vector", lambda e: e.tensor_single_scalar(msk[:], posf[:], 524288.0, ALU.is_lt), [posf], [msk])
    P.op("vector", lambda e: e.tensor_tensor(aff[:], aff[:], msk[:], ALU.mult), [aff, msk], [aff])
    for r_ in R:
        P.op("gpsimd", lambda e, r_=r_: e.memset(r_[:], 0.0), [], [r_])
    P.op("vector", lambda e: e.tensor_tensor(posf[:], posf[:], eoff[:].unsqueeze(1).to_broadcast([128, NT, 16]), ALU.add), [posf, eoff], [posf])
    P.op("vector", lambda e: e.tensor_copy(idx[:], posf[:]), [posf], [idx])
    ri = 0
    breg = {}
    def mkreg(e):
        breg["r"] = e.alloc_register("bchk")
        return e.reg_mov(breg["r"], NEXP_CAP - 1)
    P.op("gpsimd", mkreg, [], [])
    for i in range(NT):
        x = xm[i % 2]; a = acc[i % 2]
        P.dma("sync", x[:], xm_d[i * 128:(i + 1) * 128, :], writes=[x])
        for ex in range(16):
            r = R[ri % 6]; ri += 1
            P.op("gpsimd", lambda e, r=r, i=i, ex=ex: e.indirect_dma_start(
                out=r[:], out_offset=None, in_=ye_d, in_offset=bass.IndirectOffsetOnAxis(ap=idx[:, i, ex:ex + 1], axis=0),
                bounds_check=breg["r"], oob_is_err=False), [idx], [r], dma=True)
            if ex == 0:
                P.op("vector", lambda e, r=r, a=a, i=i, ex=ex: e.tensor_scalar(a[:], r[:], aff[:, i, ex:ex + 1], None, ALU.mult), [r, aff], [a])
            else:
                P.op("vector", lambda e, r=r, a=a, i=i, ex=ex: e.scalar_tensor_tensor(a[:], r[:], aff[:, i, ex:ex + 1], a[:], ALU.mult, ALU.add), [r, aff, a], [a])
        P.op("vector", lambda e, a=a: e.tensor_tensor(a[:], a[:], gt2[:], ALU.mult), [a, gt2], [a])
        P.op("gpsimd", lambda e, a=a, x=x: e.tensor_tensor(a[:], a[:], x[:], ALU.add), [a, x], [a])
        P.dma("sync", xo_d[i * 128:(i + 1) * 128, :], a[:], reads=[a])
    P.finalize()
    return nc


def build_ada(ncols):
    NCH = ncols // 128
    nc = bass.Bass("TRN2", target_bir_lowering=False)
    D = lambda n, s, dt, k: nc.dram_tensor(n, list(s), dt, kind=k).ap()
    wa_d = D("wa", [1024, ncols], F32, "ExternalInput")
    ba_d = D("ba", [128, NCH], F32, "ExternalInput")
    cv_d = D("cv", [128, 8, 3], F32, "ExternalInput")
    out_d = D("modT", [128, NCH, 3], F32, "ExternalOutput")
    P = Prog(nc)
    wa = P.sb("wa", [128, 8, ncols], F32)
    ba = P.sb("ba", [128, NCH], F32)
    cv = P.sb("cv", [128, 8, 3], F32)
    sv = P.sb("sv", [128, 8, 3], F32)
    o = P.sb("o", [128, NCH, 3], F32)
    pm = [P.ps("pm%d" % i, [128, 4]) for i in range(2)]
    for k in range(8):
        P.dma(("sync", "gpsimd")[k % 2], wa[:, k, :], wa_d[k * 128:(k + 1) * 128, :], writes=[wa.sub(k)])
    P.dma("sync", ba[:], ba_d, writes=[ba])
    P.dma("sync", cv[:], cv_d, writes=[cv])
    P.op("scalar", lambda e: e.activation(sv[:], cv[:], AF.Silu), [cv], [sv])
    for j in range(NCH):
        p = pm[j % 2]
        for k in range(8):
            P.op("tensor", lambda e, p=p, k=k, j=j: e.matmul(p[:, 0:3], wa[:, k, j * 128:(j + 1) * 128], sv[:, k, :], start=(k == 0), stop=(k == 7)), [wa, sv], [p])
        P.op("vector", lambda e, p=p, j=j: e.tensor_scalar(o[:, j, :], p[:, 0:3], ba[:, j:j + 1], None, ALU.add), [p, ba], [o.sub(j)])
    P.dma("sync", out_d, o[:], reads=[o])
    P.finalize()
    return nc

import math
import numpy as np

_PROGS = {}


def _prog(key, fn):
    if key not in _PROGS:
        _PROGS[key] = fn()
    return _PROGS[key]


def _run(nc, maps):
    return run_bass_kernel_spmd(nc, maps, core_ids=list(range(8))).results


def _c(a):
    return np.ascontiguousarray(a)


def _lay_pie(a, ni):
    return _c(a.reshape(ni, 128, 16).transpose(1, 0, 2))


def _expert_csts():
    c = np.zeros((128, 3, 128), np.float32)
    c[:, 0] = 1.0
    c[:, 1] = np.triu(np.ones((128, 128), np.float32), 1)
    c[:, 2] = np.eye(128, dtype=np.float32)
    return c.astype(BF)


def _fft256_consts():
    n = np.arange(256, dtype=np.float64)
    ang = 2 * np.pi * np.outer(n, n) / 256.0
    cn = np.stack([np.cos(ang) / 16.0, np.sin(ang) / 16.0], axis=0)
    return _c(cn.reshape(2, 2, 128, 256).transpose(2, 0, 1, 3)).astype(BF)


def kernel(x, c, ctx, c_ctx, w_ada, b_ada, g_mix, g_ffn, w_in, na_q_g, na_k_g, na_rpb, df_q_g, df_k_g, df_lambda,
           df_subln_g, pool_w, pool_scale, fnet_w, w_branch, w_out, w_router, w_gate_e, w_up_e, w_down_e, _dbg=None):
    f32 = np.float32
    x = np.asarray(x, f32); ctx = np.asarray(ctx, f32)
    B, T, Dm = x.shape
    TC = ctx.shape[1]
    L = w_in.shape[0]
    dbg = _dbg or (lambda *a, **k: None)
    ident_f = np.eye(128, dtype=f32)
    ones_bf = np.ones((128, 128), BF)
    cm_ = cmats(); dftm_ = dft_ch(); fcm, ftw = fft_consts(); ecst = _expert_csts(); cn256 = _fft256_consts()

    c3 = np.stack([c[0], c[1], c_ctx], axis=1).astype(f32)
    cv = _c(c3.reshape(8, 128, 3).transpose(1, 0, 2))
    wall = np.concatenate([w_ada[l] for l in range(L)], axis=1)
    ball = np.concatenate([b_ada[l] for l in range(L)])
    ncol = wall.shape[1] // 8
    nc = _prog(("ada", ncol), lambda: build_ada(ncol))
    res = _run(nc, [dict(wa=_c(wall[:, k * ncol:(k + 1) * ncol]), ba=_c(ball[k * ncol:(k + 1) * ncol].reshape(ncol // 128, 128).T), cv=cv)
                    for k in range(8)])
    modT = np.concatenate([r["modT"].transpose(1, 0, 2).reshape(ncol, 3) for r in res], axis=0)
    mods = [modT[l * 6144:(l + 1) * 6144].T.copy() for l in range(L)]
    dbg("mod", mods)

    def seg(m, j):
        return m[j * 1024:(j + 1) * 1024]

    for l in range(L):
        last = l == L - 1
        lam_init = 0.8 - 0.6 * math.exp(-0.3 * l)
        gq = _c(np.stack([np.tile(na_q_g[l], 2), np.tile(df_q_g[l], 4), np.tile(na_k_g[l], 2), np.tile(df_k_g[l], 4)], axis=1).astype(f32))
        wc = _c(np.concatenate([w_in[l][:, 0:1024], w_in[l][:, 5120:6144]], axis=1))
        lamv = np.tile(df_lambda[l].reshape(1, 128), (128, 1)).astype(f32)
        gsub = np.tile(df_subln_g[l][None], (128, 1)).astype(f32)
        cst2 = np.tile(np.array([[lam_init, 1 - lam_init]], f32), (128, 1))

        def proj(tok_arrays, modrows, tposs, rope_on, ntok, G):
            nc = _prog(("proj", ntok, G), lambda: build_proj(ntok, G))
            maps = []
            for k in range(8):
                m = modrows[k]
                vec = np.concatenate([fm8(seg(m, 0)), fm8(seg(m, 1)), fm8(g_mix[l])], axis=1)
                maps.append(dict(xT=_c(tok_arrays[k].T), w_in=wc, vecs=vec, gq=gq, cmats=cm_, cossin=cossin_table(tposs[k], rope_on), dftm=dftm_))
            return _run(nc, maps)
        NQ = T // 4
        rl_ = proj([x[k // 4, (k % 4) * NQ:(k % 4 + 1) * NQ] for k in range(8)], [mods[l][k // 4] for k in range(8)],
                   [np.arange((k % 4) * NQ, (k % 4 + 1) * NQ) for k in range(8)], True, NQ, 512)
        CQ = TC // 4
        rc_ = proj([ctx[k // 4, (k % 4) * CQ:(k % 4 + 1) * CQ] for k in range(8)], [mods[l][2]] * 8,
                   [np.arange(CQ)] * 8, False, CQ, CQ)
        qkT = [np.concatenate([rl_[b * 4 + q]["qkT"] for q in range(4)], axis=1) for b in range(B)]
        tm = [np.concatenate([rl_[b * 4 + q]["tm"] for q in range(4)], axis=0) for b in range(B)]
        cqkT = [np.concatenate([rc_[b * 4 + q]["qkT"] for q in range(4)], axis=1) for b in range(B)]
        ctm = [np.concatenate([rc_[b * 4 + q]["tm"] for q in range(4)], axis=0) for b in range(B)]
        dbg("proj", l, qkT, tm, cqkT, ctm)

        nc = _prog("na", build_na)
        maps = []
        for k in range(8):
            b, q = k // 4, k % 4
            d = na_core_inputs(na_rpb[l], q * 64, qkT[b][512:768], tm[b][:, 768:1024])
            d.update(qT=_c(qkT[b][0:256, q * NQ:(q + 1) * NQ]), kcT=_c(cqkT[b][512:768]),
                     vc=_c(ctm[b][:, 768:1024].reshape(2, 128, 256).transpose(1, 0, 2)), ident=ident_f)
            maps.append(d)
        r = _run(nc, maps)
        y_na = [np.concatenate([r[b * 4 + q]["y"] for q in range(4)], axis=0) for b in range(B)]

        TK = T + TC
        nc = _prog(("fattn", T, TK, 2, 32, True), lambda: build_fattn(T, TK, 2, 32, True))
        maps = []
        for k in range(8):
            b, h = k // 4, k % 4
            hs = slice(h * 64, (h + 1) * 64)
            kT = np.concatenate([qkT[b][768:1024][hs], cqkT[b][768:1024][hs]], axis=1)
            v = np.concatenate([tm[b][:, 1024:1280][:, hs], ctm[b][:, 1024:1280][:, hs]], axis=0)
            maps.append(dict(qT=_c(qkT[b][256:512][hs]), kT=_c(kT), v=_c(v.reshape(TK // 128, 128, 64).transpose(1, 0, 2)),
                             lamv=lamv, gsub=gsub, cst=cst2, ident=ident_f))
        r = _run(nc, maps)
        y_df = [np.concatenate([r[b * 4 + h]["y"] for h in range(4)], axis=1) for b in range(B)]

        nc = _prog("fft", build_fft)
        maps = []
        for k in range(8):
            b, cg = k // 4, k % 4
            Z = tm[b][:, 256:768]
            maps.append(dict(z=fft_z_layout(Z[:, cg * 64:(cg + 1) * 64], Z[:, 256 + cg * 64:256 + (cg + 1) * 64]), cm=fcm, tw=ftw))
        r = _run(nc, maps)
        fT = [np.concatenate([r[b * 4 + cg]["fT"] for cg in range(4)], axis=0) for b in range(B)]
        dbg("mix", l, y_na, y_df, fT)

        if not last:
            nc = _prog(("fattn", TC, TC, 1, 64, False), lambda: build_fattn(TC, TC, 1, 64, False))
            maps = []
            for k in range(8):
                b, h = k // 4, k % 4
                hs = slice(h * 64, (h + 1) * 64)
                maps.append(dict(qT=_c(cqkT[b][0:256][hs]), kT=_c(cqkT[b][512:768][hs]),
                                 v=_c(ctm[b][:, 768:1024][:, hs].reshape(TC // 128, 128, 64).transpose(1, 0, 2)),
                                 lamv=lamv, gsub=gsub, cst=cst2, ident=ident_f))
            r = _run(nc, maps)
            yc_na = [np.concatenate([r[b * 4 + h]["y"] for h in range(4)], axis=1) for b in range(B)]
            nc = _prog(("fattn", TC, TC, 2, 32, True), lambda: build_fattn(TC, TC, 2, 32, True))
            maps = []
            for k in range(8):
                b, h = k // 4, k % 4
                hs = slice(h * 64, (h + 1) * 64)
                maps.append(dict(qT=_c(cqkT[b][256:512][hs]), kT=_c(cqkT[b][768:1024][hs]),
                                 v=_c(ctm[b][:, 1024:1280][:, hs].reshape(TC // 128, 128, 64).transpose(1, 0, 2)),
                                 lamv=lamv, gsub=gsub, cst=cst2, ident=ident_f))
            r = _run(nc, maps)
            yc_df = [np.concatenate([r[b * 4 + h]["y"] for h in range(4)], axis=1) for b in range(B)]
            nc = _prog("fft256", build_fft256)
            maps = []
            for k in range(8):
                b, cg = k // 4, k % 4
                Z = ctm[b][:, 256:768]
                lay = lambda a: a.reshape(2, 128, 64).transpose(1, 0, 2)
                z = _c(np.stack([lay(Z[:, cg * 64:(cg + 1) * 64]), lay(Z[:, 256 + cg * 64:256 + (cg + 1) * 64])], axis=1))
                maps.append(dict(z=z, cn=cn256))
            r = _run(nc, maps)
            fcT = [np.concatenate([r[b * 4 + cg]["fT"] for cg in range(4)], axis=0) for b in range(B)]
            dbg("cmix", l, yc_na, yc_df, fcT)

        def merge(xs_tok, modrows, yTs, us, bands, ntok, G):
            nc = _prog(("merge", ntok, G), lambda: build_merge(ntok, G))
            maps = []
            for k in range(8):
                m = modrows[k]
                vec = np.concatenate([fm8(seg(m, 0)), fm8(seg(m, 1)), fm8(g_mix[l]), fm8(seg(m, 2)),
                                      fm8(seg(m, 3)), fm8(seg(m, 4)), fm8(g_ffn[l])], axis=1)
                maps.append(dict(xT=_c(xs_tok[k].T), vecs=vec, wg=_c(w_in[l][:, 1024:5120]), wbr=w_branch[l], wo=w_out[l], wr=w_router[l],
                                 fw=fnet_w[l], pw=pool_w[l], psc=_c(pool_scale[l].reshape(4, 64).T), band=bands[k], ones=ones_bf,
                                 yT=yTs[k], u=us[k]))
            return _run(nc, maps)
        sl = lambda q: slice(q * NQ, (q + 1) * NQ)
        r = merge([x[k // 4, sl(k % 4)] for k in range(8)], [mods[l][k // 4] for k in range(8)],
                  [_c(np.stack([y_na[k // 4][sl(k % 4)].T, y_df[k // 4][sl(k % 4)].T, fT[k // 4][:, sl(k % 4)]])) for k in range(8)],
                  [pool_u_layout(tm[k // 4][:, 0:256], (k % 4) * NQ, NQ) for k in range(8)],
                  [pool_bands_for_core((k % 4) * (NQ // 128), NQ // 128, T) for k in range(8)], NQ, 512)
        x_mid = [np.concatenate([r[b * 4 + q]["xmT"].T for q in range(4)], axis=0) for b in range(B)]
        h2 = [np.concatenate([r[b * 4 + q]["h2T"].T for q in range(4)], axis=0) for b in range(B)]
        aff = [np.concatenate([r[b * 4 + q]["aff"] for q in range(4)], axis=0) for b in range(B)]
        dbg("merge", l, x_mid, aff)
        if not last:
            cidx = [(k % 4) // 2 for k in range(8)], [(k % 4) % 2 for k in range(8)]
            csl = lambda t: slice(t * 128, (t + 1) * 128)
            r = merge([ctx[cidx[0][k], csl(cidx[1][k])] for k in range(8)], [mods[l][2]] * 8,
                      [_c(np.stack([yc_na[cidx[0][k]][csl(cidx[1][k])].T, yc_df[cidx[0][k]][csl(cidx[1][k])].T, fcT[cidx[0][k]][:, csl(cidx[1][k])]])) for k in range(8)],
                      [pool_u_layout(ctm[cidx[0][k]][:, 0:256], cidx[1][k] * 128, 128) for k in range(8)],
                      [pool_bands_for_core(cidx[1][k], 1, TC) for k in range(8)], 128, 128)
            c_mid = [np.concatenate([r[b * 2 + t]["xmT"].T for t in range(2)], axis=0) for b in range(B)]
            c_h2 = [np.concatenate([r[b * 2 + t]["h2T"].T for t in range(2)], axis=0) for b in range(B)]
            c_aff = [np.concatenate([r[b * 2 + t]["aff"] for t in range(2)], axis=0) for b in range(B)]
            dbg("cmerge", l, c_mid, c_aff)

        def experts(affs, h2s, NI, CAP):
            nc = _prog(("exp", NI, CAP), lambda: build_experts_full(NI, CAP))
            maps = []
            for k in range(8):
                b, eg = k // 4, k % 4
                perm = np.roll(np.arange(16), -4 * eg)
                es = slice(4 * eg, 4 * eg + 4)
                maps.append(dict(aff=_lay_pie(affs[b][:, perm], NI), h2=_c(h2s[b]), wgate=_c(w_gate_e[l][es]), wup=_c(w_up_e[l][es]),
                                 wdown=_c(w_down_e[l][es]), cst=ecst))
            r = _run(nc, maps)
            pos = [r[b * 4]["pos"] for b in range(B)]
            ye = [np.concatenate([r[b * 4 + eg]["ye"].reshape(4 * CAP, 1024) for eg in range(4)], axis=0) for b in range(B)]
            return pos, ye
        CAP = max(1, 2 * T // 16)
        pos, ye = experts(aff, h2, T // 128, CAP)
        dbg("experts", l, pos, ye)

        def combine(xms, poss, affs, yes, gt2s, ntok, CAPx):
            nc = _prog(("comb", ntok, CAPx), lambda: build_combine(ntok, 16 * CAPx))
            eoff = np.tile((np.arange(16) * CAPx).astype(f32)[None], (128, 1))
            maps = [dict(xm=_c(xms[k]), pos=_c(poss[k]), aff=affs[k], eoff=eoff, gt2=np.tile(gt2s[k][None].astype(f32), (128, 1)), ye=yes[k])
                    for k in range(8)]
            return _run(nc, maps)
        nt = NQ // 128
        r = combine([x_mid[k // 4][sl(k % 4)] for k in range(8)], [pos[k // 4][:, (k % 4) * nt:(k % 4 + 1) * nt, :] for k in range(8)],
                    [_lay_pie(aff[k // 4][sl(k % 4)], nt) for k in range(8)], [ye[k // 4] for k in range(8)],
                    [seg(mods[l][k // 4], 5) for k in range(8)], NQ, CAP)
        x = np.stack([np.concatenate([r[b * 4 + q]["xo"] for q in range(4)], axis=0) for b in range(B)])
        dbg("xout", l, x)
        if not last:
            CAPc = max(1, 2 * TC // 16)
            cpos, cye = experts(c_aff, c_h2, TC // 128, CAPc)
            r = combine([c_mid[cidx[0][k]][csl(cidx[1][k])] for k in range(8)], [cpos[cidx[0][k]][:, cidx[1][k]:cidx[1][k] + 1, :] for k in range(8)],
                        [_lay_pie(c_aff[cidx[0][k]][csl(cidx[1][k])], 1) for k in range(8)], [cye[cidx[0][k]] for k in range(8)],
                        [seg(mods[l][2], 5)] * 8, 128, CAPc)
            ctx = np.stack([np.concatenate([r[b * 2 + t]["xo"] for t in range(2)], axis=0) for b in range(B)])
            dbg("ctxout", l, ctx)
    return x.astype(np.float32)
```
